# Optimizing a Trainium2 kernel written in Bass

```python
import math
import jax, jax.numpy as jnp
from jax import lax
import numpy as np

D_MODEL = 1024
BATCH = 8
SEQ = 4096
DEPTH = 2

N_MEM = 256
GM_GROUPS = 4
GM_DIM = 128
GM_CHUNK = 128
GM_WIDTH = GM_GROUPS * GM_DIM
DN_HEADS = 4
DN_DK = 128
DN_DV = 128
DN_CHUNK = 64
CONV_W = 4
DN_QK = DN_HEADS * DN_DK
DN_V = DN_HEADS * DN_DV
MIX_WIDTH = GM_WIDTH + DN_V
D_FF = 4 * D_MODEL
XA_HEADS = 4
XA_DIM = D_MODEL // XA_HEADS
EPS = 1e-6
IN_SIZES = (GM_WIDTH, GM_WIDTH, DN_QK, DN_QK, DN_V, DN_V, DN_HEADS, DN_HEADS)
D_IN = 2 * GM_WIDTH + 2 * DN_QK + 2 * DN_V + 2 * DN_HEADS

kernel_name = "hybrid_sgu_gdn_memxattn_block"


def rmsnorm(x, g):
    xf = x.astype(jnp.float32)
    y = xf * lax.rsqrt(jnp.mean(xf * xf, axis=-1, keepdims=True) + EPS)
    return (y * g.astype(jnp.float32)).astype(x.dtype)


def layernorm(x, g, b):
    xf = x.astype(jnp.float32)
    mu = jnp.mean(xf, axis=-1, keepdims=True)
    var = jnp.mean(jnp.square(xf - mu), axis=-1, keepdims=True)
    y = (xf - mu) * lax.rsqrt(var + EPS)
    return (y * g.astype(jnp.float32) + b.astype(jnp.float32)).astype(x.dtype)


def l2norm(x):
    return x * lax.rsqrt(jnp.sum(x * x, axis=-1, keepdims=True) + EPS)


def split_cols(t, sizes):
    idx = [int(i) for i in np.cumsum(np.array(sizes))[:-1]]
    return jnp.split(t, idx, axis=-1)


def chunk_spatial_gating(u, v, ln_g, ln_b, w_s, b_s):
    bsz, s, _ = u.shape
    nc = s // GM_CHUNK
    u = jax.nn.gelu(u).reshape(bsz, nc, GM_CHUNK, GM_GROUPS, GM_DIM)
    v = jax.nn.gelu(v).reshape(bsz, nc, GM_CHUNK, GM_GROUPS, GM_DIM)
    v = layernorm(v, ln_g, ln_b)
    causal = jnp.tril(jnp.ones((GM_CHUNK, GM_CHUNK), dtype=bool))
    w = jnp.where(causal, w_s, jnp.zeros_like(w_s))
    mixed = jnp.einsum('gts,bcsgd->bctgd', w, v) + jnp.transpose(b_s)[None, None, :, :, None]
    return (u * mixed).reshape(bsz, s, GM_WIDTH)


def causal_dwconv(x, w):
    c = x.shape[-1]
    return lax.conv_general_dilated(
        x, w[:, None, :].astype(x.dtype), window_strides=(1,), padding=[(CONV_W - 1, 0)],
        dimension_numbers=('NWC', 'WIO', 'NWC'), feature_group_count=c)


def gated_delta_rule(q, k, v, g, beta):
    f32 = jnp.float32
    bsz, s, h, dk = q.shape
    dv = v.shape[-1]
    c = DN_CHUNK
    n = s // c

    def chunks(t):
        return t.astype(f32).reshape(bsz, n, c, h, -1).transpose(0, 3, 1, 2, 4)

    q = chunks(q) * (dk ** -0.5)
    k = chunks(k)
    v = chunks(v)
    beta = beta.astype(f32).reshape(bsz, n, c, h).transpose(0, 3, 1, 2)[..., None]
    g = g.astype(f32).reshape(bsz, n, c, h).transpose(0, 3, 1, 2)
    decay = jnp.cumsum(g, axis=-1)
    tri = jnp.tril(jnp.ones((c, c), dtype=bool))
    strict = jnp.tril(jnp.ones((c, c), dtype=bool), -1)
    lmask = jnp.exp(jnp.where(tri, decay[..., :, None] - decay[..., None, :], -jnp.inf))

    k_beta = k * beta
    v_beta = v * beta
    m = jnp.where(strict, jnp.einsum('bhnid,bhnjd->bhnij', k_beta, k) * lmask, 0.0)
    a = m + jnp.eye(c, dtype=f32)
    rhs = jnp.concatenate([v_beta, k_beta * jnp.exp(decay)[..., None]], axis=-1)
    sol = lax.linalg.triangular_solve(a, rhs, left_side=True, lower=True, unit_diagonal=True)
    u_c = sol[..., :dv]
    w_c = sol[..., dv:]

    qk = jnp.where(tri, jnp.einsum('bhnid,bhnjd->bhnij', q, k) * lmask, 0.0)
    q_dec = q * jnp.exp(decay)[..., None]
    k_dec = k * jnp.exp(decay[..., -1:] - decay)[..., None]
    chunk_decay = jnp.exp(decay[..., -1])

    xs = tuple(jnp.moveaxis(t, 2, 0) for t in (qk, q_dec, k_dec, u_c, w_c, chunk_decay))

    def step(state, inp):
        qk_i, q_dec_i, k_dec_i, u_i, w_i, cd_i = inp
        v_new = u_i - jnp.einsum('bhck,bhkv->bhcv', w_i, state)
        o = jnp.einsum('bhck,bhkv->bhcv', q_dec_i, state) + jnp.einsum('bhij,bhjv->bhiv', qk_i, v_new)
        state = state * cd_i[..., None, None] + jnp.einsum('bhck,bhcv->bhkv', k_dec_i, v_new)
        return state, o

    s0 = jnp.zeros((bsz, h, dk, dv), f32)
    _, o = lax.scan(step, s0, xs)
    return o.transpose(1, 0, 3, 2, 4).reshape(bsz, s, h, dv)


def hybrid_mixer(h, w_in, gm_ln_g, gm_ln_b, gm_ws, gm_bs, dn_conv, dn_a_log, dn_dt_bias, dn_onorm, w_out):
    bsz, s, _ = h.shape
    proj = h @ w_in
    gu, gv, q, k, v, gate, b_logit, a_logit = split_cols(proj, IN_SIZES)

    y_a = chunk_spatial_gating(gu, gv, gm_ln_g, gm_ln_b, gm_ws, gm_bs)

    qkv = jax.nn.silu(causal_dwconv(jnp.concatenate([q, k, v], axis=-1), dn_conv))
    q, k, v = split_cols(qkv, (DN_QK, DN_QK, DN_V))
    q = l2norm(q.astype(jnp.float32).reshape(bsz, s, DN_HEADS, DN_DK))
    k = l2norm(k.astype(jnp.float32).reshape(bsz, s, DN_HEADS, DN_DK))
    v = v.reshape(bsz, s, DN_HEADS, DN_DV)
    beta = jax.nn.sigmoid(b_logit.astype(jnp.float32))
    g = -jnp.exp(dn_a_log.astype(jnp.float32)) * jax.nn.softplus(
        a_logit.astype(jnp.float32) + dn_dt_bias.astype(jnp.float32))
    o = gated_delta_rule(q, k, v, g, beta)
    o = rmsnorm(o, dn_onorm) * jax.nn.silu(gate.astype(jnp.float32).reshape(bsz, s, DN_HEADS, DN_DV))
    y_b = o.reshape(bsz, s, DN_V).astype(h.dtype)

    return jnp.concatenate([y_a, y_b], axis=-1) @ w_out


def memory_cross_attention(h, m, w_q, w_k, w_v, w_o):
    bsz, s, _ = h.shape
    nm = m.shape[1]
    q = (h @ w_q).reshape(bsz, s, XA_HEADS, XA_DIM)
    k = (m @ w_k).reshape(bsz, nm, XA_HEADS, XA_DIM)
    v = (m @ w_v).reshape(bsz, nm, XA_HEADS, XA_DIM)
    scores = jnp.einsum('bshd,bmhd->bhsm', q, k).astype(jnp.float32) * (XA_DIM ** -0.5)
    p = jax.nn.softmax(scores, axis=-1).astype(v.dtype)
    o = jnp.einsum('bhsm,bmhd->bshd', p, v).reshape(bsz, s, D_MODEL)
    return o @ w_o


def squared_relu_mlp(h, w1, w2):
    return jnp.square(jax.nn.relu(h @ w1)) @ w2


def setup_inputs(seed: int = 0) -> dict:
    key = jax.random.key(seed)
    ks = jax.random.split(key, 26)
    f32 = jnp.float32
    L = DEPTH

    def nrm(k, shape, scale):
        return jax.random.normal(k, shape, f32) * scale

    def gain(k, shape):
        return 1.0 + 0.02 * jax.random.normal(k, shape, f32)

    dt = jnp.exp(jax.random.uniform(ks[11], (L, DN_HEADS), f32, math.log(1e-3), math.log(1e-1)))
    return {
        "x": nrm(ks[0], (BATCH, SEQ, D_MODEL), 1.0),
        "mem": nrm(ks[1], (BATCH, N_MEM, D_MODEL), 1.0),
        "norm_mix": gain(ks[2], (L, D_MODEL)),
        "w_in": nrm(ks[3], (L, D_MODEL, D_IN), D_MODEL ** -0.5),
        "gm_ln_g": gain(ks[4], (L, GM_GROUPS, GM_DIM)),
        "gm_ln_b": nrm(ks[5], (L, GM_GROUPS, GM_DIM), 0.02),
        "gm_ws": nrm(ks[6], (L, GM_GROUPS, GM_CHUNK, GM_CHUNK), GM_CHUNK ** -0.5),
        "gm_bs": gain(ks[7], (L, GM_GROUPS, GM_CHUNK)),
        "dn_conv": nrm(ks[8], (L, CONV_W, 2 * DN_QK + DN_V), CONV_W ** -0.5),
        "dn_a_log": jnp.log(jax.random.uniform(ks[9], (L, DN_HEADS), f32, 1.0, 16.0)),
        "dn_dt_bias": dt + jnp.log(-jnp.expm1(-dt)),
        "dn_onorm": gain(ks[10], (L, DN_DV)),
        "w_out": nrm(ks[12], (L, MIX_WIDTH, D_MODEL), MIX_WIDTH ** -0.5),
        "norm_xa": gain(ks[13], (L, D_MODEL)),
        "norm_mem": gain(ks[14], (L, D_MODEL)),
        "xa_wq": nrm(ks[15], (L, D_MODEL, D_MODEL), D_MODEL ** -0.5),
        "xa_wk": nrm(ks[16], (L, D_MODEL, D_MODEL), D_MODEL ** -0.5),
        "xa_wv": nrm(ks[17], (L, D_MODEL, D_MODEL), D_MODEL ** -0.5),
        "xa_wo": nrm(ks[18], (L, D_MODEL, D_MODEL), D_MODEL ** -0.5),
        "norm_ffn": gain(ks[19], (L, D_MODEL)),
        "ffn_w1": nrm(ks[20], (L, D_MODEL, D_FF), D_MODEL ** -0.5),
        "ffn_w2": nrm(ks[21], (L, D_FF, D_MODEL), D_FF ** -0.5),
        "norm_final": gain(ks[22], (D_MODEL,)),
    }


def reference(x, mem, norm_mix, w_in, gm_ln_g, gm_ln_b, gm_ws, gm_bs, dn_conv, dn_a_log, dn_dt_bias,
              dn_onorm, w_out, norm_xa, norm_mem, xa_wq, xa_wk, xa_wv, xa_wo, norm_ffn, ffn_w1, ffn_w2,
              norm_final):
    for l in range(DEPTH):
        h = rmsnorm(x, norm_mix[l])
        x = x + hybrid_mixer(h, w_in[l], gm_ln_g[l], gm_ln_b[l], gm_ws[l], gm_bs[l], dn_conv[l],
                             dn_a_log[l], dn_dt_bias[l], dn_onorm[l], w_out[l])
        m = rmsnorm(mem, norm_mem[l])
        x = x + memory_cross_attention(rmsnorm(x, norm_xa[l]), m, xa_wq[l], xa_wk[l], xa_wv[l], xa_wo[l])
        x = x + squared_relu_mlp(rmsnorm(x, norm_ffn[l]), ffn_w1[l], ffn_w2[l])
    return rmsnorm(x, norm_final)
```

```python
from concourse.bass_utils import run_bass_kernel_spmd

import numpy as np
import concourse.bass as bass
import concourse.mybir as mybir

F32 = mybir.dt.float32
BF16 = mybir.dt.bfloat16
F32R = mybir.dt.float32r
AF = mybir.ActivationFunctionType
ALU = mybir.AluOpType
AX = mybir.AxisListType


class Buf:
    __slots__ = ("name", "w", "rs", "xr")

    def __init__(self, name):
        self.name = name
        self.w = None
        self.rs = []
        self.xr = False


class View:
    __slots__ = ("ap", "bufs")

    def __init__(self, ap, bufs):
        self.ap = ap
        self.bufs = bufs

    def __getitem__(self, idx):
        return View(self.ap[idx], self.bufs)

    def bitcast(self, dt):
        return View(self.ap.bitcast(dt), self.bufs)

    def bc(self, shape):
        return View(self.ap.to_broadcast(shape), self.bufs)

    def rr(self, s, **kw):
        return View(self.ap.rearrange(s, **kw), self.bufs)


class T:
    def __init__(self, h, name, bufs=None):
        self.h = h
        self.bufs = bufs if bufs is not None else (Buf(name),)
        self.name = name

    def __getitem__(self, idx):
        return View(self.h[idx], self.bufs)

    def sub(self, name):
        return T(self.h, name)


class Op:
    __slots__ = ("eng", "fn", "deps", "idx", "signal", "ms", "dma_key", "dma_cnt", "name")


ENGS = ("pe", "act", "dve", "pool", "sp")
WINDOW = 16000


class Prog:
    def __init__(self, nc):
        self.nc = nc
        self.ops = {e: [] for e in ENGS}
        self.dma_keys = {}
        self.nsb = 0

    def sb(self, name, shape, dt=F32):
        h = self.nc.alloc_sbuf_tensor(name, list(shape), dt)
        return T(h, name)

    def ps(self, name, shape, dt=F32):
        h = self.nc.alloc_psum_tensor(name, list(shape), dt)
        t = T(h, name)
        t.bufs[0].xr = True
        return t

    def dram(self, name, shape, dt, kind):
        h = self.nc.dram_tensor(name, list(shape), dt, kind=kind)
        return T(h.ap(), name)

    def op(self, eng, fn, reads=(), writes=(), dma_key=None, name=None):
        o = Op()
        o.eng = eng
        o.fn = fn
        o.signal = False
        o.ms = None
        o.dma_key = dma_key
        o.dma_cnt = None
        o.name = name
        deps = {}
        rb = []
        for r in reads:
            for b in r.bufs:
                if b not in rb:
                    rb.append(b)
        wb = []
        for w in writes:
            for b in w.bufs:
                if b not in wb:
                    wb.append(b)
        for b in rb:
            if b.w is not None:
                deps[id(b.w)] = b.w
            if b.xr:
                for r in b.rs:
                    if r.eng != eng:
                        deps[id(r)] = r
        for b in wb:
            if b.w is not None:
                deps[id(b.w)] = b.w
            for r in b.rs:
                deps[id(r)] = r
        dl = []
        for d in deps.values():
            if d is o:
                continue
            if d.dma_key is None and d.eng == "pe" and eng == "pe" and dma_key is None:
                continue
            dl.append(d)
        o.deps = dl
        for d in dl:
            d.signal = True
        if dma_key is not None:
            c = self.dma_keys.get(dma_key, 0) + 1
            self.dma_keys[dma_key] = c
            o.dma_cnt = c
        o.idx = len(self.ops[eng])
        self.ops[eng].append(o)
        for b in rb:
            if b not in wb:
                b.rs.append(o)
        for b in wb:
            b.w = o
            b.rs = []
        return o

    def mm(self, out, lhsT, rhs, start=True, stop=True, **kw):
        rd = [lhsT, rhs] + ([] if start else [out])
        return self.op("pe", lambda e: e.matmul(out.ap, lhsT.ap, rhs.ap, start=start, stop=stop, **kw),
                       reads=rd, writes=[out])

    def tr(self, out, in_, ident):
        return self.op("pe", lambda e: e.transpose(out.ap, in_.ap, ident.ap), reads=[in_, ident], writes=[out])

    def act(self, out, in_, func, bias=None, scale=None, accum_out=None, eng="act"):
        rd = [in_]
        kw = {}
        if bias is not None:
            if isinstance(bias, View):
                rd.append(bias)
                kw["bias"] = bias.ap
            else:
                kw["bias"] = bias
        if scale is not None:
            if isinstance(scale, View):
                rd.append(scale)
                kw["scale"] = scale.ap
            else:
                kw["scale"] = scale
        wr = [out]
        if accum_out is not None:
            wr.append(accum_out)
            kw["accum_out"] = accum_out.ap
        return self.op(eng, lambda e: e.activation(out.ap, in_.ap, func, **kw), reads=rd, writes=wr)

    def tt(self, out, a, b, op, eng="dve"):
        return self.op(eng, lambda e: e.tensor_tensor(out.ap, a.ap, b.ap, op), reads=[a, b], writes=[out])

    def ts(self, out, a, s1, s2, op0, op1=None, eng="dve", accum_out=None):
        rd = [a]
        s1v = s1.ap if isinstance(s1, View) else s1
        s2v = s2.ap if isinstance(s2, View) else s2
        if isinstance(s1, View):
            rd.append(s1)
        if isinstance(s2, View):
            rd.append(s2)
        wr = [out]
        kw = {}
        if accum_out is not None:
            wr.append(accum_out)
            kw["accum_out"] = accum_out.ap
        if op1 is None:
            return self.op(eng, lambda e: e.tensor_scalar(out.ap, a.ap, s1v, None, op0, **kw), reads=rd, writes=wr)
        return self.op(eng, lambda e: e.tensor_scalar(out.ap, a.ap, s1v, s2v, op0, op1, **kw), reads=rd, writes=wr)

    def stt(self, out, a, s, b, op0, op1, eng="dve"):
        rd = [a, b]
        sv = s.ap if isinstance(s, View) else s
        if isinstance(s, View):
            rd.append(s)
        return self.op(eng, lambda e: e.scalar_tensor_tensor(out.ap, a.ap, sv, b.ap, op0, op1), reads=rd, writes=[out])

    def copy(self, out, in_, eng="dve"):
        if eng == "act":
            return self.op(eng, lambda e: e.activation(out.ap, in_.ap, AF.Identity), reads=[in_], writes=[out])
        return self.op(eng, lambda e: e.tensor_copy(out.ap, in_.ap), reads=[in_], writes=[out])

    def memset(self, out, val, eng="dve"):
        return self.op(eng, lambda e: e.memset(out.ap, val), reads=[], writes=[out])

    def reduce(self, out, in_, op, axis=None, eng="dve"):
        ax = axis if axis is not None else AX.X
        return self.op(eng, lambda e: e.tensor_reduce(out.ap, in_.ap, ax, op), reads=[in_], writes=[out])

    def recip(self, out, in_):
        return self.op("dve", lambda e: e.reciprocal(out.ap, in_.ap), reads=[in_], writes=[out])

    def dma(self, out, in_, key, eng="sp", **kw):
        return self.op(eng, lambda e: e.dma_start(out=out.ap, in_=in_.ap, **kw), reads=[in_], writes=[out], dma_key=key)

    def emit(self, final_wait_keys=()):
        nc = self.nc
        from contextlib import ExitStack
        es = ExitStack()
        eng_sems = {}
        for e in ENGS:
            n = 0
            for o in self.ops[e]:
                if o.dma_key is None and o.signal:
                    n += 1
                    o.ms = n
            nwin = (n + WINDOW - 1) // WINDOW
            eng_sems[e] = [es.enter_context(nc.semaphore(f"s_{e}_{i}")) for i in range(max(nwin, 1))]
        key_sems = {k: es.enter_context(nc.semaphore(f"k_{k}")) for k in self.dma_keys}
        self.nsem = sum(len(v) for v in eng_sems.values()) + len(key_sems)

        def dep_sem(d):
            if d.dma_key is not None:
                return key_sems[d.dma_key], 16 * d.dma_cnt
            w = (d.ms - 1) // WINDOW
            return eng_sems[d.eng][w], d.ms - w * WINDOW

        def run(e, engobj):
            waited = {}
            for o in self.ops[e]:
                need = {}
                for d in o.deps:
                    s, v = dep_sem(d)
                    k = id(s)
                    if need.get(k, (None, 0))[1] < v:
                        need[k] = (s, v)
                for k, (s, v) in need.items():
                    if waited.get(k, 0) < v:
                        engobj.wait_ge(s, v)
                        waited[k] = v
                ins = o.fn(engobj)
                if o.dma_key is not None:
                    ins.then_inc(key_sems[o.dma_key], 16)
                elif o.signal:
                    w = (o.ms - 1) // WINDOW
                    ins.then_inc(eng_sems[e][w], 1)
            if e == "sp":
                for k in final_wait_keys:
                    engobj.wait_ge(key_sems[k], 16 * self.dma_keys[k])

        with nc.Block() as block:
            @block.tensor
            def _(pe):
                run("pe", pe)

            @block.scalar
            def _(a):
                run("act", a)

            @block.vector
            def _(v):
                run("dve", v)

            @block.gpsimd
            def _(g):
                run("pool", g)

            @block.sync
            def _(s):
                run("sp", s)
        es.close()

D = 1024
SEQ = 4096
NMEM = 256
DIN = 3080
DFF = 4096
TS = 512
NT = TS // 128
EPS = 1e-6
NRING = 3


class Arena:
    PAGE = 128

    def __init__(self, P, name, words):
        self.t = P.sb(name, [128, words], F32)
        self.words = words
        self.pages = [Buf(f"{name}_pg{i}") for i in range((words + self.PAGE - 1) // self.PAGE)]

    def view(self, off, nwords, dt=F32, pat=None, **kw):
        assert off + nwords <= self.words, (off, nwords, self.words)
        ap = self.t.h[:, off:off + nwords]
        if dt != F32:
            ap = ap.bitcast(dt)
        if pat is not None:
            ap = ap.rearrange(pat, **kw)
        p0 = off // self.PAGE
        p1 = (off + nwords - 1) // self.PAGE
        return T(ap, "av", bufs=tuple(self.pages[p0:p1 + 1]))


class Alloc:
    def __init__(self, arena, start=0):
        self.a = arena
        self.o = start

    def f32(self, shape):
        n = int(np.prod(shape))
        n_al = (n + 127) // 128 * 128
        pat, kw = _pat(shape)
        t = self.a.view(self.o, n, F32, pat, **kw)
        self.o += n_al
        return t

    def bf16(self, shape):
        n = int(np.prod(shape))
        w = (n + 1) // 2
        w_al = (w + 127) // 128 * 128
        pat, kw = _pat(shape)
        t = self.a.view(self.o, w, BF16, pat, **kw)
        self.o += w_al
        return t


def _pat(shape):
    if len(shape) == 1:
        return None, {}
    if len(shape) == 2:
        return "p (a b) -> p a b", {"a": shape[0]}
    if len(shape) == 3:
        return "p (a b c) -> p a b c", {"a": shape[0], "b": shape[1]}
    raise ValueError(shape)


def build_program(NSEG=SEQ // TS, NL=2, dbg=None):
    nc = bass.Bass("TRN2", target_bir_lowering=False)
    P = Prog(nc)
    L = 2

    class _Stop(Exception):
        pass

    st = {"dbg": dbg}

    def ck(name, l=None):
        if dbg == name or (l is not None and dbg == f"{name}{l}"):
            if "xT" in st:
                xT_, outd_ = st["xT"], st["outd"]
                for c in range(8):
                    P.dma(View(outd_.h[c * 128:(c + 1) * 128, 0:TS], outd_.bufs), xT_[:, c, :], key="out", eng="sp")
            raise _Stop()
    try:
        _build_body(nc, P, L, NSEG, NL, ck, st)
    except _Stop:
        P.emit(final_wait_keys=[k for k in ("out",) if k in P.dma_keys])
    return nc, P


def _build_body(nc, P, L, NSEG, NL, ck, st):
    din = {}

    def di(name, shape):
        din[name] = P.dram(name, shape, F32, "ExternalInput")
        return din[name]
    xd = di("x", [SEQ, D])
    memd = di("mem", [NMEM, D])
    norm_mix = di("norm_mix", [L, D])
    w_in = di("w_in", [L, D, DIN])
    gm_ln_g = di("gm_ln_g", [L, 4, 128])
    gm_ln_b = di("gm_ln_b", [L, 4, 128])
    gm_ws = di("gm_ws", [L, 4, 128, 128])
    gm_bs = di("gm_bs", [L, 4, 128])
    dn_conv = di("dn_conv", [L, 4, 1536])
    dn_a_log = di("dn_a_log", [L, 4])
    dn_dt_bias = di("dn_dt_bias", [L, 4])
    dn_onorm = di("dn_onorm", [L, 128])
    w_out = di("w_out", [L, D, D])
    norm_xa = di("norm_xa", [L, D])
    norm_mem = di("norm_mem", [L, D])
    xa_wq = di("xa_wq", [L, D, D])
    xa_wk = di("xa_wk", [L, D, D])
    xa_wv = di("xa_wv", [L, D, D])
    xa_wo = di("xa_wo", [L, D, D])
    norm_ffn = di("norm_ffn", [L, D])
    ffn_w1 = di("ffn_w1", [L, D, DFF])
    ffn_w2 = di("ffn_w2", [L, DFF, D])
    norm_final = di("norm_final", [D])
    outd = P.dram("out", [SEQ, D], F32, "ExternalOutput")

    banks = [P.ps(f"bank{i}", [128, 512], F32) for i in range(8)]
    bstate = {"i": 0}

    def nb(exclude=()):
        while True:
            b = banks[bstate["i"] % 8]
            bstate["i"] += 1
            if b not in exclude:
                return b

    def b16(bank):
        return View(bank.h[:, :].bitcast(BF16), bank.bufs)

    ident_f = P.sb("ident_f", [128, 128], F32)
    ident_b = P.sb("ident_b", [128, 128], BF16)
    ones_b = P.sb("ones_b", [128, 128], BF16)
    ones_f = P.sb("ones_f", [128, 128], F32)
    U_f = P.sb("U_f", [128, 128], F32)
    bones_f = P.sb("bones_f", [128, 128], F32)
    csel_f = P.sb("csel_f", [128, 2, 128], F32)
    negm_b = P.sb("negm_b", [128, 4, 128], BF16)
    nsu_f = P.sb("nsu_f", [128, 4, 128], F32)
    colsA = P.sb("colsA", [128, 128], F32)
    colsB = P.sb("colsB", [128, 96], F32)
    onorm_bc = P.sb("onorm_bc", [128, L, 128], F32)
    alog_bc = P.sb("alog_bc", [128, L, 4], F32)
    dtb_bc = P.sb("dtb_bc", [128, L, 4], F32)
    negA_bc = P.sb("negA_bc", [128, L, 4], F32)
    Bg = P.sb("Bg", [128, L, 512], F32)
    WsT = P.sb("WsT", [128, L, 512], BF16)
    wtail = P.sb("wtail", [128, L, 64], BF16)
    hist = [P.sb(f"hist{l}", [128, 12, 4], BF16) for l in range(L)]
    KT = [P.sb(f"KT{l}", [128, 8, 256], BF16) for l in range(L)]
    Vm = [P.sb(f"Vm{l}", [128, 2, 1024], BF16) for l in range(L)]
    diag = P.sb("diag", [128, 48, 128], BF16)
    Sst = [P.sb(f"S{l}", [128, 4, 128], F32) for l in range(L)]
    Rr = P.sb("Rr", [128, 4, 128], F32R)
    residc = [P.sb(f"resid_r{c}", [128, 4, 128], F32R) for c in range(2)]
    xT = P.sb("xT", [128, 8, TS], F32)
    hT = P.sb("hT", [128, 8, TS], BF16)
    st["xT"] = xT
    st["outd"] = outd
    sq = [P.sb(f"sq{i}", [128, TS], BF16) for i in range(2)]
    lnv = P.sb("lnv", [128, TS], F32)
    rstd = P.sb("rstd", [128, TS], F32)
    ring = [P.sb(f"ring{i}", [128, 8, 512], BF16) for i in range(NRING)]
    YR = [P.sb(f"YR{i}", [128, 4, 2, 128], F32R) for i in range(2)]
    XX = [P.sb(f"XX{i}", [128, 4, 128], F32R) for i in range(2)]
    ARW = 21 * 1024
    arena = Arena(P, "arena", ARW)

    class WStream:
        def __init__(self):
            self.blocks = []
            self.issued = 0
            self.taken = 0

        def add(self, src):
            self.blocks.append(src)

        def _issue(self, n):
            while self.issued < min(n, len(self.blocks)):
                i = self.issued
                slot = ring[i % NRING]
                src = self.blocks[i]
                for q in range(4):
                    P.dma(slot[:, 2 * q:2 * q + 2, :], src[:, 2 * q:2 * q + 2, :], key=f"ring{i % NRING}", eng="pool")
                self.issued += 1

        def take(self):
            i = self.taken
            self._issue(i + NRING)
            self.taken += 1
            return ring[i % NRING]

    WS = WStream()

    def wblk(Wd, l, r0, c0):
        return View(Wd.h[l, r0:r0 + 1024, c0:c0 + 512].rearrange("(kc p) n -> p kc n", p=128), Wd.bufs)

    for l in range(NL):
        for c in range(2):
            WS.add(wblk(xa_wk, l, 0, 512 * c))
        for c in range(2):
            WS.add(wblk(xa_wv, l, 0, 512 * c))
    for seg in range(NSEG):
        for l in range(NL):
            for c in range(6):
                WS.add(wblk(w_in, l, 0, 512 * c))
            for c in range(2):
                WS.add(wblk(w_out, l, 0, 512 * c))
            for c in range(2):
                WS.add(wblk(xa_wq, l, 0, 512 * c))
            for c in range(2):
                WS.add(wblk(xa_wo, l, 0, 512 * c))
            for c in range(8):
                WS.add(wblk(ffn_w1, l, 0, 512 * c))
            for cg in range(2):
                for ks in range(4):
                    WS.add(wblk(ffn_w2, l, 1024 * ks, 512 * cg))

    def iota_mask(t, fill_keep_cmp, fill, pattern_w=128, nrep=1):
        pass

    P.memset(ident_f[:, :], 0.0, eng="pool")
    P.op("pool", lambda e: e.affine_select(ident_f.h[:, :], ident_f.h[:, :], pattern=[[-1, 128]],
                                           compare_op=ALU.not_equal, fill=1.0, base=0, channel_multiplier=1),
         reads=[ident_f], writes=[ident_f])
    P.copy(ident_b[:, :], ident_f[:, :], eng="pool")
    P.memset(ones_b[:, :], 1.0, eng="pool")
    P.memset(ones_f[:, :], 1.0, eng="pool")
    P.memset(U_f[:, :], 1.0, eng="pool")
    P.op("pool", lambda e: e.affine_select(U_f.h[:, :], U_f.h[:, :], pattern=[[1, 128]],
                                           compare_op=ALU.is_ge, fill=0.0, base=0, channel_multiplier=-1),
         reads=[U_f], writes=[U_f])
    P.memset(negm_b[:, :, :], 0.0, eng="pool")
    P.op("pool", lambda e: e.affine_select(negm_b.h[:, :, :], negm_b.h[:, :, :], pattern=[[0, 4], [1, 128]],
                                           compare_op=ALU.is_ge, fill=-30000.0, base=0, channel_multiplier=-1),
         reads=[negm_b], writes=[negm_b])
    P.memset(nsu_f[:, :, :], -1.0, eng="pool")
    P.op("pool", lambda e: e.affine_select(nsu_f.h[:, :, :], nsu_f.h[:, :, :], pattern=[[0, 4], [1, 128]],
                                           compare_op=ALU.is_gt, fill=0.0, base=0, channel_multiplier=-1),
         reads=[nsu_f], writes=[nsu_f])

    ck("s1")
    P.memset(U_f[0:64, 64:128], 0.0, eng="pool")
    P.memset(negm_b[0:64, :, 64:128], -30000.0, eng="pool")
    P.memset(nsu_f[0:64, :, 64:128], 0.0, eng="pool")
    P.memset(bones_f[:, :], 1.0, eng="pool")
    P.memset(bones_f[0:64, 64:128], 0.0, eng="pool")
    P.memset(bones_f[64:128, 0:64], 0.0, eng="pool")
    P.memset(csel_f[:, :, :], 1.0, eng="pool")
    P.memset(csel_f[64:128, 0, :], 0.0, eng="pool")
    P.memset(csel_f[0:64, 1, :], 0.0, eng="pool")
    for c in range(2):
        P.ts(residc[c][:, :, :], nsu_f[:, :, :], 0.0, None, ALU.mult)
    pa = Alloc(arena)
    stageA = pa.f32([128])
    stageB = pa.f32([128])
    ws_nat = pa.f32([L * 4, 128])
    gmem_bc = pa.f32([L, D])
    memst = pa.f32([2, D])
    mn_b = pa.bf16([2, D])
    mnT = pa.bf16([8, 256])
    junk = pa.f32([D])
    sm = pa.f32([64])
    bs_bc = pa.f32([L, 512])
    lngb_dummy = None

    P.memset(stageA[:, :], 0.0, eng="dve")
    P.memset(stageB[:, :], 0.0, eng="dve")

    ikey = {"n": 0}

    def ik():
        ikey["n"] += 1
        return f"init{ikey['n']}"

    def rows(dst, r0, n, src_ap):
        P.dma(dst[r0:r0 + n, :], View(src_ap, (Buf("dram"),)), key=ik(), eng="sp")
    for l in range(L):
        rows(stageA, 8 * l, 8, norm_mix.h[l].rearrange("(c p) -> c p", p=128))
        rows(stageA, 16 + 8 * l, 8, norm_xa.h[l].rearrange("(c p) -> c p", p=128))
        rows(stageA, 32 + 8 * l, 8, norm_ffn.h[l].rearrange("(c p) -> c p", p=128))
        rows(stageA, 56 + 4 * l, 4, gm_ln_g.h[l])
        rows(stageA, 64 + 4 * l, 4, gm_ln_b.h[l])
        for j in range(4):
            rows(stageB, 48 * l + 12 * j, 12, dn_conv.h[l, j].rearrange("(c p) -> c p", p=128))
    rows(stageA, 48, 8, norm_final.h.rearrange("(c p) -> c p", p=128))

    ck("s2")

    def bcast_load(dst_view, src_ap):
        P.dma(dst_view, View(src_ap.partition_broadcast(128), (Buf("dram"),)), key=ik(), eng="sp")
    for l in range(L):
        bcast_load(onorm_bc[:, l, :], dn_onorm.h[l])
        bcast_load(bs_bc[:, l, :], gm_bs.h[l].rearrange("g t -> (g t)"))
        bcast_load(alog_bc[:, l, :], dn_a_log.h[l])
        bcast_load(dtb_bc[:, l, :], dn_dt_bias.h[l])
        bcast_load(gmem_bc[:, l, :], norm_mem.h[l])
        P.dma(ws_nat[:, 4 * l:4 * l + 4, :], View(gm_ws.h[l].rearrange("g t s -> t g s"), (Buf("dram"),)), key=ik(), eng="sp")
        P.dma(wtail[:, l, :].rr("p (kc n) -> p kc n", kc=8),
              View(w_in.h[l, :, 3072:3080].rearrange("(kc p) n -> p kc n", p=128), (Buf("dram"),)), key=ik(), eng="pool")
    P.dma(memst[:, :, :], View(memd.h.rearrange("(mt p) d -> p mt d", p=128), (Buf("dram"),)), key=ik(), eng="sp")

    ck("s3")
    bk = nb()
    P.tr(bk[:, 0:128], stageA[:, :], ident_f[:, :])
    P.tr(bk[:, 128:224], stageB[0:96, :], ident_f[0:96, 0:96])
    P.copy(colsA[:, :], bk[:, 0:128], eng="dve")
    P.copy(colsB[:, :], bk[:, 128:224], eng="dve")

    def gcol(kind, l):
        base = {"mix": 0, "xa": 16, "ffn": 32}[kind] + 8 * l
        return colsA[:, base:base + 8]

    ck("s4")
    P.act(negA_bc[:, :, :], alog_bc[:, :, :], AF.Exp)
    P.ts(negA_bc[:, :, :], negA_bc[:, :, :], -1.0, None, ALU.mult)

    ck("s5")
    for l in range(L):
        for g in range(4):
            i = 4 * l + g
            P.op("pool", lambda e, i=i: e.affine_select(ws_nat.h[:, i, :], ws_nat.h[:, i, :], pattern=[[-1, 128]],
                                                        compare_op=ALU.is_ge, fill=0.0, base=0, channel_multiplier=1),
                 reads=[ws_nat], writes=[ws_nat])
        ck("s6")
        bk = nb()
        for g in range(4):
            P.tr(bk[:, g * 128:(g + 1) * 128], ws_nat[:, 4 * l + g, :], ident_f[:, :])
        ck("s6a")
        wsT_f = junk
        P.copy(wsT_f[:, 0:512], bk[:, :], eng="dve")
        ck("s6b")
        P.copy(WsT[:, l, :], wsT_f[:, 0:512], eng="act")
        ck("s7")
        bk2 = nb()
        P.mm(bk2[:, :], ones_f[:, :], wsT_f[:, 0:512], start=True, stop=True)
        for g in range(4):
            P.stt(Bg[:, l, g * 128:(g + 1) * 128], bk2[:, g * 128:(g + 1) * 128], colsA[:, 64 + 4 * l + g:65 + 4 * l + g],
                  bs_bc[:, l, g * 128:(g + 1) * 128], ALU.mult, ALU.add)

    ck("setup")
    for l in range(NL):
        for mt in range(2):
            P.act(junk[:, :], memst[:, mt, :], AF.Square, accum_out=sm[:, mt:mt + 1])
        P.act(sm[:, 2:4], sm[:, 0:2], AF.Ln, bias=EPS, scale=1.0 / D)
        P.act(sm[:, 4:6], sm[:, 2:4], AF.Exp, scale=-0.5)
        for mt in range(2):
            P.stt(mn_b[:, mt, :], memst[:, mt, :], sm[:, 4 + mt:5 + mt], gmem_bc[:, l, :], ALU.mult, ALU.mult)
        for mt in range(2):
            bk = nb()
            for kc in range(8):
                P.tr(b16(bk)[:, kc * 128:(kc + 1) * 128], mn_b[:, mt, kc * 128:(kc + 1) * 128], ident_b[:, :])
            P.copy(mnT[:, :, mt * 128:(mt + 1) * 128], b16(bk)[:, :].rr("p (a b) -> p a b", a=8), eng="dve")
        for c in range(2):
            W = WS.take()
            for cc in range(4):
                bk = nb()
                for kc in range(8):
                    P.mm(bk[:, 0:256], W[:, kc, cc * 128:(cc + 1) * 128], mnT[:, kc, :], start=(kc == 0), stop=(kc == 7))
                P.copy(KT[l][:, 4 * c + cc, :], bk[:, 0:256], eng="act")
        for c in range(2):
            W = WS.take()
            for mt in range(2):
                bk = nb()
                for kc in range(8):
                    P.mm(bk[:, :], mnT[:, kc, mt * 128:(mt + 1) * 128], W[:, kc, :], start=(kc == 0), stop=(kc == 7))
                P.copy(Vm[l][:, mt, 512 * c:512 * (c + 1)], bk[:, :], eng="dve")

    ck("kv")
    for l in range(L):
        P.memset(Sst[l][:, :, :], 0.0, eng="dve")
        P.memset(hist[l][:, :, :], 0.0, eng="dve")

    ma = Alloc(arena)
    uT = ma.bf16([4, TS])
    gv = [ma.f32([4, 128]) for _ in range(NT)]
    pcT = ma.bf16([12, TS + 4])
    sg = [ma.bf16([512]) for _ in range(NT)]
    bl = ma.f32([NT, 8])
    yT = ma.bf16([8, TS])
    tok = ma.f32([16, 16])
    vn_b = ma.bf16([512])
    tmpA = ma.f32([512])
    tmpB = ma.f32([512])
    stat = ma.f32([64])
    qk_f = ma.f32([2, 512])
    v_f = ma.f32([4, 128])
    qn_b = ma.bf16([4, 128])
    qd_f = ma.f32([4, 128])
    kn_b = ma.bf16([4, 128])
    kd_f = ma.f32([4, 128])
    kdT_f = ma.f32([4, 128])
    kdec_f = ma.f32([4, 128])
    kT_b = ma.bf16([4, 128])
    qnT_b = ma.bf16([4, 128])
    qdT_f = ma.f32([4, 128])
    ET = ma.f32([4, 128])
    GE = ma.f32([4, 128])
    QKT_f = ma.f32([4, 128])
    vnewc = [ma.f32([4, 128]) for _ in range(2)]
    o_f = ma.f32([4, 128])
    on_f = ma.f32([4, 128])
    gsc = ma.f32([4, 128])
    yb_b = ma.bf16([4, 128])
    assert ma.o <= ARW, ma.o
    xa = Alloc(arena)
    qxT = xa.bf16([8, TS])
    oxT = xa.bf16([8, TS])
    expT = [xa.bf16([2, TS]) for _ in range(2)]
    rinv = [xa.f32([TS]) for _ in range(2)]
    fa = Alloc(arena)
    hidT = fa.bf16([32, TS])
    rl = [fa.bf16([TS]) for _ in range(2)]
    xst = fa.f32([NT, D])
    ost = fa.f32([NT, D])
    outT = ost
    assert fa.o <= ARW, fa.o

    K_BETA, K_NBETA, K_G, K_D, K_NEGD, K_ED, K_KDS, K_CD0, K_TMP, K_TMP2, K_DL, K_CD1 = range(12)

    def rmsnorm_to_hT(gains, final_out=None):
        bk = nb()
        for c in range(8):
            s = sq[c % 2]
            P.act(s[:, :], xT[:, c, :], AF.Square)
            P.mm(bk[:, :], ones_b[:, :], s[:, :], start=(c == 0), stop=(c == 7))
        P.act(lnv[:, :], bk[:, :], AF.Ln, bias=EPS, scale=1.0 / D)
        P.act(rstd[:, :], lnv[:, :], AF.Exp, scale=-0.5)
        for c in range(8):
            dst = hT[:, c, :] if final_out is None else final_out[:, c, :]
            P.stt(dst, xT[:, c, :], gains[:, c:c + 1], rstd[:, :], ALU.mult, ALU.mult)

    def proj_fm(W, rhsT, nk, evac):
        for cc in range(4):
            bk = nb()
            for kc in range(nk):
                P.mm(bk[:, :], W[:, kc, cc * 128:(cc + 1) * 128], rhsT[:, kc, :], start=(kc == 0), stop=(kc == nk - 1))
            evac(cc, bk)

    def proj_tm(W, lhsT_all, evac):
        for tt in range(NT):
            bk = nb()
            for kc in range(8):
                P.mm(bk[:, :], lhsT_all[:, kc, tt * 128:(tt + 1) * 128], W[:, kc, :], start=(kc == 0), stop=(kc == 7))
            evac(tt, bk)

    def resid_add(c, bk):
        P.tt(xT[:, c, :], xT[:, c, :], bk[:, :], ALU.add)

    first_x = {"done": False}

    def load_x(seg):
        P.dma(xst[:, :, :], View(xd.h[seg * TS:(seg + 1) * TS, :].rearrange("(tt p) d -> p tt d", p=128), (Buf("dram"),)),
              key="xin", eng="sp")

    for seg in range(NSEG):
        if seg == 0:
            load_x(0)
        for c in range(8):
            bk = nb()
            for tt in range(NT):
                P.tr(bk[:, tt * 128:(tt + 1) * 128], xst[:, tt, c * 128:(c + 1) * 128], ident_f[:, :])
            P.copy(xT[:, c, :], bk[:, :], eng=("act" if c % 2 else "dve"))

        ck("xT")
        for l in range(NL):
            for j in range(4):
                for c in range(12):
                    r = 48 * l + 12 * j + c
                    P.ts(diag[:, 12 * j + c, :], ident_f[:, :], colsB[:, r:r + 1], None, ALU.mult,
                         eng=("dve" if (c % 2) else "pool"))
            P.memset(vnewc[0][64:128, :, :], 0.0, eng="pool")
            P.memset(vnewc[1][0:64, :, :], 0.0, eng="pool")
            P.copy(pcT[:, :, 0:4], hist[l][:, :, :], eng="dve")
            rmsnorm_to_hT(gcol("mix", l))
            ck("norm1")
            W = WS.take()
            proj_fm(W, hT, 8, lambda cc, bk: P.act(uT[:, cc, :], bk[:, :], AF.Gelu_apprx_tanh))
            W = WS.take()
            proj_tm(W, hT, lambda tt, bk: P.act(gv[tt][:, :, :], bk[:, :].rr("p (a b) -> p a b", a=4), AF.Gelu_apprx_tanh))
            for cg in range(3):
                W = WS.take()
                proj_fm(W, hT, 8, lambda cc, bk, cg=cg: P.copy(pcT[:, 4 * cg + cc, 4:4 + TS], bk[:, :],
                                                              eng=("act" if cc % 2 else "dve")))
            P.copy(hist[l][:, :, :], pcT[:, :, TS:TS + 4], eng="dve")
            W = WS.take()
            proj_tm(W, hT, lambda tt, bk: P.act(sg[tt][:, :], bk[:, :], AF.Silu))
            bk = nb()
            for tt in range(NT):
                for kc in range(8):
                    P.mm(bk[:, tt * 8:(tt + 1) * 8], hT[:, kc, tt * 128:(tt + 1) * 128], wtail[:, l, kc * 8:(kc + 1) * 8],
                         start=(kc == 0), stop=(kc == 7))
            P.copy(bl[:, :, :], bk[:, 0:NT * 8].rr("p (a b) -> p a b", a=NT), eng="dve")

            ck("inproj")
            def tk(kind):
                return tok[:, kind, :]

            def tk3(kind):
                return tok[:, kind, :].rr("p (a b) -> p a b", a=NT)
            P.act(tk3(K_TMP), bl[:, :, 0:4], AF.Tanh, scale=0.5)
            P.ts(tk(K_BETA), tk(K_TMP), 0.5, 0.5, ALU.mult, ALU.add)
            P.ts(tk(K_NBETA), tk(K_BETA), -1.0, None, ALU.mult)
            P.tt(tk3(K_TMP), bl[:, :, 4:8], dtb_bc[:, l:l + 1, :].bc([128, NT, 4]), ALU.add)
            P.act(tk(K_TMP2), tk(K_TMP), AF.Exp)
            P.act(tk(K_TMP), tk(K_TMP2), AF.Ln, bias=1.0)
            P.tt(tk3(K_G), tk3(K_TMP), negA_bc[:, l:l + 1, :].bc([128, NT, 4]), ALU.mult)
            bk = nb()
            P.mm(bk[:, 0:16], U_f[:, :], tk(K_G), start=True, stop=True)
            P.mm(bk[:, 16:32], bones_f[:, :], tk(K_G), start=True, stop=True)
            P.mm(bk[:, 32:48], csel_f[:, 0, :], tk(K_G), start=True, stop=True)
            P.mm(bk[:, 48:64], csel_f[:, 1, :], tk(K_G), start=True, stop=True)
            P.copy(tk(K_D), bk[:, 0:16], eng="dve")
            P.copy(tk(K_DL), bk[:, 16:32], eng="dve")
            P.act(tk(K_CD0), bk[:, 32:48], AF.Exp)
            P.act(tk(K_CD1), bk[:, 48:64], AF.Exp)
            P.ts(tk(K_NEGD), tk(K_D), -1.0, None, ALU.mult)
            P.act(tk(K_ED), tk(K_D), AF.Exp)
            P.tt(tk(K_TMP), tk(K_DL), tk(K_D), ALU.subtract)
            P.act(tk(K_KDS), tk(K_TMP), AF.Exp)

            ck("tok")
            S = Sst[l]
            for tt in range(NT):
                tsl = slice(tt * 128, (tt + 1) * 128)
                hs = slice(tt * 4, tt * 4 + 4)
                g3 = gv[tt]
                P.reduce(stat[:, 0:4], g3[:, :, :], ALU.add)
                P.act(tmpA[:, :], g3[:, :, :].rr("p a b -> p (a b)"), AF.Square)
                P.reduce(stat[:, 4:8], tmpA[:, :].rr("p (a b) -> p a b", a=4), ALU.add)
                P.ts(stat[:, 8:12], stat[:, 0:4], 1.0 / 128, None, ALU.mult)
                P.tt(stat[:, 12:16], stat[:, 8:12], stat[:, 8:12], ALU.mult)
                P.stt(stat[:, 16:20], stat[:, 4:8], 1.0 / 128, stat[:, 12:16], ALU.mult, ALU.subtract)
                P.act(stat[:, 20:24], stat[:, 16:20], AF.Ln, bias=EPS)
                P.act(stat[:, 24:28], stat[:, 20:24], AF.Exp, scale=-0.5)
                P.tt(tmpB[:, :].rr("p (a b) -> p a b", a=4), g3[:, :, :],
                     stat[:, 8:12].rr("p (a b) -> p a b", b=1).bc([128, 4, 128]), ALU.subtract)
                P.tt(vn_b[:, :].rr("p (a b) -> p a b", a=4), tmpB[:, :].rr("p (a b) -> p a b", a=4),
                     stat[:, 24:28].rr("p (a b) -> p a b", b=1).bc([128, 4, 128]), ALU.mult)
                bkA = nb()
                for g in range(4):
                    P.mm(bkA[:, g * 128:(g + 1) * 128], vn_b[:, g * 128:(g + 1) * 128], WsT[:, l, g * 128:(g + 1) * 128],
                         start=True, stop=True)
                P.tt(tmpA[:, :].rr("p (a b) -> p a b", a=4), bkA[:, :].rr("p (a b) -> p a b", a=4),
                     colsA[:, 56 + 4 * l:60 + 4 * l].rr("p (a b) -> p a b", b=1).bc([128, 4, 128]), ALU.mult)
                P.tt(tmpA[:, :], tmpA[:, :], Bg[:, l, :], ALU.add, eng="pool")
                P.tt(yT[:, 0:4, tsl], tmpA[:, :].rr("p (a b) -> p a b", a=4), uT[:, :, tsl], ALU.mult, eng="pool")

                ck("mixA")
                bq, bkk, bv = nb(), nb(), nb()
                for cg, bkc in enumerate((bq, bkk, bv)):
                    for cc in range(4):
                        c = 4 * cg + cc
                        for j in range(4):
                            c0 = tt * 128 + j + 1
                            P.mm(bkc[:, cc * 128:(cc + 1) * 128], pcT[:, c, c0:c0 + 128], diag[:, 12 * j + c, :],
                                 start=(j == 0), stop=(j == 3))
                P.act(qk_f[:, 0, :], bq[:, :], AF.Silu)
                P.act(qk_f[:, 1, :], bkk[:, :], AF.Silu)
                P.act(v_f[:, :, :].rr("p a b -> p (a b)"), bv[:, :], AF.Silu)
                P.act(tmpA[:, :], qk_f[:, 0, :], AF.Square)
                P.act(tmpB[:, :], qk_f[:, 1, :], AF.Square)
                P.reduce(stat[:, 32:36], tmpA[:, :].rr("p (a b) -> p a b", a=4), ALU.add)
                P.reduce(stat[:, 36:40], tmpB[:, :].rr("p (a b) -> p a b", a=4), ALU.add)
                P.act(stat[:, 40:48], stat[:, 32:40], AF.Ln, bias=EPS)
                P.act(stat[:, 48:56], stat[:, 40:48], AF.Exp, scale=-0.5)
                P.ts(stat[:, 48:52], stat[:, 48:52], 128.0 ** -0.5, None, ALU.mult)
                P.tt(stat[:, 56:60], stat[:, 48:52], tok[:, K_ED, hs], ALU.mult)
                P.tt(stat[:, 60:64], stat[:, 52:56], tok[:, K_ED, hs], ALU.mult)
                P.tt(stat[:, 28:32], stat[:, 52:56], tok[:, K_KDS, hs], ALU.mult)

                def bc4(v):
                    return v.rr("p (a b) -> p a b", b=1).bc([128, 4, 128])
                q3 = qk_f[:, 0, :].rr("p (a b) -> p a b", a=4)
                k3 = qk_f[:, 1, :].rr("p (a b) -> p a b", a=4)
                P.tt(qn_b[:, :, :], q3, bc4(stat[:, 48:52]), ALU.mult)
                P.tt(qd_f[:, :, :], q3, bc4(stat[:, 56:60]), ALU.mult)
                P.tt(kn_b[:, :, :], k3, bc4(stat[:, 52:56]), ALU.mult)
                P.tt(kd_f[:, :, :], k3, bc4(stat[:, 60:64]), ALU.mult, eng="pool")
                P.tt(kdec_f[:, :, :], k3, bc4(stat[:, 28:32]), ALU.mult, eng="pool")
                bkT = nb()
                for h in range(4):
                    P.tr(b16(bkT)[:, h * 128:(h + 1) * 128], kn_b[:, h, :], ident_b[:, :])
                    P.tr(b16(bkT)[:, 512 + h * 128:512 + (h + 1) * 128], qn_b[:, h, :], ident_b[:, :])
                P.copy(kT_b[:, :, :].rr("p a b -> p (a b)"), b16(bkT)[:, 0:512], eng="dve")
                P.copy(qnT_b[:, :, :].rr("p a b -> p (a b)"), b16(bkT)[:, 512:1024], eng="act")
                bkT2 = nb()
                for h in range(4):
                    P.tr(bkT2[:, h * 128:(h + 1) * 128], qd_f[:, h, :], ident_f[:, :])
                P.copy(qdT_f[:, :, :].rr("p a b -> p (a b)"), bkT2[:, :], eng="act")
                bkD = nb()
                P.mm(bkD[:, :], ident_b[:, :], negm_b[:, :, :].rr("p a b -> p (a b)"), start=True, stop=False)
                for h in range(4):
                    gcolv = tok[:, K_G, tt * 4 + h:tt * 4 + h + 1]
                    P.mm(bkD[:, h * 128:(h + 1) * 128], gcolv.bc([128, 128]), U_f[:, :], start=False, stop=True)
                for h in range(4):
                    P.act(ET[:, h, :], bkD[:, h * 128:(h + 1) * 128], AF.Exp, bias=tok[:, K_NEGD, tt * 4 + h:tt * 4 + h + 1])
                bkG = nb()
                bkKQ = nb()
                for h in range(4):
                    P.mm(bkG[:, h * 128:(h + 1) * 128], kT_b[:, h, :], kT_b[:, h, :], start=True, stop=True)
                for h in range(4):
                    P.mm(bkKQ[:, h * 128:(h + 1) * 128], kT_b[:, h, :], qnT_b[:, h, :], start=True, stop=True)
                P.tt(GE[:, :, :].rr("p a b -> p (a b)"), bkG[:, :], ET[:, :, :].rr("p a b -> p (a b)"), ALU.mult)
                P.tt(QKT_f[:, :, :].rr("p a b -> p (a b)"), bkKQ[:, :], ET[:, :, :].rr("p a b -> p (a b)"), ALU.mult)
                P.tt(GE[:, :, :], GE[:, :, :], nsu_f[:, :, :], ALU.mult, eng="pool")
                cur = 0
                P.tt(YR[cur][:, :, 0, :], GE[:, :, :], bc4(tok[:, K_BETA, hs]), ALU.mult)
                for h in range(4):
                    P.copy(YR[cur][:, h, 1, :], ident_f[:, :], eng="dve")
                bkX = nb()
                for h in range(4):
                    P.tr(bkX[:, h * 128:(h + 1) * 128], YR[cur][:, h, 0, :].bitcast(F32), ident_f[:, :])
                P.copy(XX[cur][:, :, :].rr("p a b -> p (a b)"), bkX[:, :], eng="act")
                NLEV = 6
                for k in range(NLEV):
                    nxt = 1 - cur
                    last = (k == NLEV - 1)
                    bA0, bA1 = nb(), nb()
                    bAs = (bA0, bA0, bA1, bA1)
                    if not last:
                        for h in range(4):
                            P.mm(bAs[h][:, (h % 2) * 256:(h % 2) * 256 + 256], XX[cur][:, h, :],
                                 YR[cur][:, h, :, :].rr("p a b -> p (a b)"), start=True, stop=True)
                        bB = nb()
                        for h in range(4):
                            P.mm(bB[:, h * 128:(h + 1) * 128], YR[cur][:, h, 0, :],
                                 XX[cur][:, h, :], start=True, stop=True)
                        for hp in range(2):
                            src = (bA0, bA1)[hp][:, :].rr("p (a b c) -> p a b c", a=2, b=2)
                            P.copy(YR[nxt][:, 2 * hp:2 * hp + 2, 0, :], src[:, :, 0, :], eng="act")
                            P.tt(YR[nxt][:, 2 * hp:2 * hp + 2, 1, :], src[:, :, 1, :],
                                 YR[cur][:, 2 * hp:2 * hp + 2, 1, :].bitcast(F32), ALU.add)
                        P.copy(XX[nxt][:, :, :].rr("p a b -> p (a b)"), bB[:, :], eng="act")
                        cur = nxt
                    else:
                        for h in range(4):
                            P.mm(bA0[:, h * 128:(h + 1) * 128], XX[cur][:, h, :],
                                 YR[cur][:, h, 1, :], start=True, stop=True)
                        P.tt(Rr[:, :, :], bA0[:, :].rr("p (a b) -> p a b", a=4), YR[cur][:, :, 1, :].bitcast(F32), ALU.add)
                bkW = nb()
                for h in range(4):
                    P.tr(bkW[:, h * 128:(h + 1) * 128], kd_f[:, h, :], ident_f[:, :])
                P.copy(kdT_f[:, :, :].rr("p a b -> p (a b)"), bkW[:, :], eng="act")
                for c in range(2):
                    rs = slice(64 * c, 64 * c + 64)
                    res_c, vn_c = residc[c], vnewc[c]
                    bkV1 = nb()
                    for h in range(4):
                        P.mm(bkV1[:, h * 128:(h + 1) * 128], kdT_f[:, h, :], S[:, h, :], start=True, stop=True)
                    P.tt(res_c[rs, :, :].rr("p a b -> p (a b)"), v_f[rs, :, :].rr("p a b -> p (a b)"), bkV1[rs, :], ALU.subtract)
                    bkV = nb()
                    for h in range(4):
                        P.mm(bkV[:, h * 128:(h + 1) * 128], Rr[:, h, :], res_c[:, h, :], start=True, stop=True)
                    beta_bc = View(tok.h[rs, K_BETA, hs].rearrange("p (a b) -> p a b", b=1).to_broadcast([64, 4, 128]), tok.bufs)
                    P.tt(vn_c[rs, :, :], bkV[rs, :].rr("p (a b) -> p a b", a=4), beta_bc, ALU.mult)
                    bkO = nb()
                    for h in range(4):
                        P.mm(bkO[:, h * 128:(h + 1) * 128], qdT_f[:, h, :], S[:, h, :], start=True, stop=False)
                        P.mm(bkO[:, h * 128:(h + 1) * 128], QKT_f[:, h, :], vn_c[:, h, :], start=False, stop=True)
                    P.copy(o_f[rs, :, :].rr("p a b -> p (a b)"), bkO[rs, :], eng="act")
                    bkS = nb()
                    for h in range(4):
                        P.mm(bkS[:, h * 128:(h + 1) * 128], kdec_f[:, h, :], vn_c[:, h, :], start=True, stop=True)
                    kcd = K_CD0 if c == 0 else K_CD1
                    for h in range(4):
                        P.stt(S[:, h, :], S[:, h, :], tok[:, kcd, tt * 4 + h:tt * 4 + h + 1], bkS[:, h * 128:(h + 1) * 128],
                              ALU.mult, ALU.add)
                P.act(tmpB[:, :], o_f[:, :, :].rr("p a b -> p (a b)"), AF.Square)
                P.reduce(stat[:, 0:4], tmpB[:, :].rr("p (a b) -> p a b", a=4), ALU.add)
                P.act(stat[:, 4:8], stat[:, 0:4], AF.Ln, bias=EPS, scale=1.0 / 128)
                P.act(stat[:, 8:12], stat[:, 4:8], AF.Exp, scale=-0.5)
                P.tt(gsc[:, :, :], sg[tt][:, :].rr("p (a b) -> p a b", a=4), onorm_bc[:, l:l + 1, :].bc([128, 4, 128]),
                     ALU.mult, eng="pool")
                P.tt(on_f[:, :, :], o_f[:, :, :], bc4(stat[:, 8:12]), ALU.mult)
                P.tt(yb_b[:, :, :], on_f[:, :, :], gsc[:, :, :], ALU.mult, eng="pool")
                bkY = nb()
                for h in range(4):
                    P.tr(b16(bkY)[:, h * 128:(h + 1) * 128], yb_b[:, h, :], ident_b[:, :])
                P.copy(yT[:, 4:8, tsl], b16(bkY)[:, 0:512].rr("p (a b) -> p a b", a=4), eng="act")

            st["yT"] = yT
            if st["dbg"] == f"ygdn{l}":
                for c in range(8):
                    P.dma(View(outd.h[c * 128:(c + 1) * 128, 0:TS], outd.bufs), yT[:, c, :], key="out", eng="pool")
                del st["xT"]
                ck(f"ygdn{l}")
            ck("gdn", l)
            for c in range(2):
                W = WS.take()
                proj_fm(W, yT, 8, lambda cc, bk, c=c: resid_add(4 * c + cc, bk))

            ck("wout", l)
            rmsnorm_to_hT(gcol("xa", l))
            for c in range(2):
                W = WS.take()
                proj_fm(W, hT, 8, lambda cc, bk, c=c: P.copy(qxT[:, 4 * c + cc, :], bk[:, :], eng=("act" if cc % 2 else "dve")))
            for h in range(4):
                e = expT[h % 2]
                for mc in range(2):
                    bk = nb()
                    for dc in range(2):
                        P.mm(bk[:, :], KT[l][:, 2 * h + dc, mc * 128:(mc + 1) * 128], qxT[:, 2 * h + dc, :],
                             start=(dc == 0), stop=(dc == 1))
                    P.act(e[:, mc, :], bk[:, :], AF.Exp, scale=1.0 / 16)
                bk = nb()
                for mc in range(2):
                    P.mm(bk[:, :], ones_b[:, :], e[:, mc, :], start=(mc == 0), stop=(mc == 1))
                ri = rinv[h % 2]
                P.recip(ri[:, :], bk[:, :])
                for dc in range(2):
                    bk = nb()
                    for mc in range(2):
                        P.mm(bk[:, :], Vm[l][:, mc, (2 * h + dc) * 128:(2 * h + dc + 1) * 128], e[:, mc, :],
                             start=(mc == 0), stop=(mc == 1))
                    P.tt(oxT[:, 2 * h + dc, :], bk[:, :], ri[:, :], ALU.mult)
            for c in range(2):
                W = WS.take()
                proj_fm(W, oxT, 8, lambda cc, bk, c=c: resid_add(4 * c + cc, bk))

            ck("xattn", l)
            rmsnorm_to_hT(gcol("ffn", l))
            for c in range(8):
                W = WS.take()

                def ev(cc, bk, c=c):
                    r = rl[cc % 2]
                    P.act(r[:, :], bk[:, :], AF.Relu)
                    P.tt(hidT[:, 4 * c + cc, :], r[:, :], r[:, :], ALU.mult, eng=("dve" if cc % 2 else "pool"))
                proj_fm(W, hT, 8, ev)
            if l == NL - 1 and seg + 1 < NSEG:
                load_x(seg + 1)
            for cg in range(2):
                accs = [nb() for _ in range(4)]
                for ks in range(4):
                    W = WS.take()
                    for cc in range(4):
                        for kc in range(8):
                            P.mm(accs[cc][:, :], W[:, kc, cc * 128:(cc + 1) * 128], hidT[:, ks * 8 + kc, :],
                                 start=(ks == 0 and kc == 0), stop=(ks == 3 and kc == 7))
                for cc in range(4):
                    resid_add(4 * cg + cc, accs[cc])
            ck("ffn", l)

        fin = hidT_f32 = None
        fo = Alloc(arena)
        finT = fo.f32([8, TS])
        rmsnorm_to_hT(colsA[:, 48:56], final_out=finT)
        for tt in range(NT):
            for half in range(2):
                bk = nb()
                for j in range(4):
                    c = 4 * half + j
                    P.tr(bk[:, j * 128:(j + 1) * 128], finT[:, c, tt * 128:(tt + 1) * 128], ident_f[:, :])
                P.copy(ost[:, tt, half * 512:(half + 1) * 512], bk[:, :], eng=("act" if half else "dve"))
        P.dma(View(outd.h[seg * TS:(seg + 1) * TS, :].rearrange("(tt p) d -> p tt d", p=128), outd.bufs), ost[:, :, :],
              key="out", eng="sp")

    P.emit(final_wait_keys=["out"])


_CACHE = {}


def kernel(**inputs):
    names = ["x", "mem", "norm_mix", "w_in", "gm_ln_g", "gm_ln_b", "gm_ws", "gm_bs", "dn_conv", "dn_a_log",
             "dn_dt_bias", "dn_onorm", "w_out", "norm_xa", "norm_mem", "xa_wq", "xa_wk", "xa_wv", "xa_wo",
             "norm_ffn", "ffn_w1", "ffn_w2", "norm_final"]
    arrs = {k: np.ascontiguousarray(np.asarray(inputs[k], dtype=np.float32)) for k in names}
    nc, _ = build_program()
    in_maps = []
    for b in range(8):
        m = {k: arrs[k] for k in names if k not in ("x", "mem")}
        m["x"] = np.ascontiguousarray(arrs["x"][b])
        m["mem"] = np.ascontiguousarray(arrs["mem"][b])
        in_maps.append(m)
    res = run_bass_kernel_spmd(nc, in_maps, core_ids=list(range(8)))
    out = np.stack([np.asarray(r["out"], dtype=np.float32) for r in res.results], axis=0)
    return out
```

```python
from concourse.bass_utils import run_bass_kernel_spmd

import numpy as np
import concourse.bass as bass
import concourse.mybir as mybir

F32 = mybir.dt.float32
BF16 = mybir.dt.bfloat16
F32R = mybir.dt.float32r
AF = mybir.ActivationFunctionType
ALU = mybir.AluOpType
AX = mybir.AxisListType


class Buf:
    __slots__ = ("name", "w", "rs", "xr")

    def __init__(self, name):
        self.name = name
        self.w = None
        self.rs = []
        self.xr = False


class View:
    __slots__ = ("ap", "bufs")

    def __init__(self, ap, bufs):
        self.ap = ap
        self.bufs = bufs

    def __getitem__(self, idx):
        return View(self.ap[idx], self.bufs)

    def bitcast(self, dt):
        return View(self.ap.bitcast(dt), self.bufs)

    def bc(self, shape):
        return View(self.ap.to_broadcast(shape), self.bufs)

    def rr(self, s, **kw):
        return View(self.ap.rearrange(s, **kw), self.bufs)


class T:
    def __init__(self, h, name, bufs=None):
        self.h = h
        self.bufs = bufs if bufs is not None else (Buf(name),)
        self.name = name

    def __getitem__(self, idx):
        return View(self.h[idx], self.bufs)

    def sub(self, name):
        return T(self.h, name)


class Op:
    __slots__ = ("eng", "fn", "deps", "idx", "signal", "ms", "dma_key", "dma_cnt", "name")


ENGS = ("pe", "act", "dve", "pool", "sp")
WINDOW = 16000


class Prog:
    def __init__(self, nc):
        self.nc = nc
        self.ops = {e: [] for e in ENGS}
        self.dma_keys = {}
        self.nsb = 0

    def sb(self, name, shape, dt=F32):
        h = self.nc.alloc_sbuf_tensor(name, list(shape), dt)
        return T(h, name)

    def ps(self, name, shape, dt=F32):
        h = self.nc.alloc_psum_tensor(name, list(shape), dt)
        t = T(h, name)
        t.bufs[0].xr = True
        return t

    def dram(self, name, shape, dt, kind):
        h = self.nc.dram_tensor(name, list(shape), dt, kind=kind)
        return T(h.ap(), name)

    def op(self, eng, fn, reads=(), writes=(), dma_key=None, name=None):
        o = Op()
        o.eng = eng
        o.fn = fn
        o.signal = False
        o.ms = None
        o.dma_key = dma_key
        o.dma_cnt = None
        o.name = name
        deps = {}
        rb = []
        for r in reads:
            for b in r.bufs:
                if b not in rb:
                    rb.append(b)
        wb = []
        for w in writes:
            for b in w.bufs:
                if b not in wb:
                    wb.append(b)
        for b in rb:
            if b.w is not None:
                deps[id(b.w)] = b.w
            if b.xr:
                for r in b.rs:
                    if r.eng != eng:
                        deps[id(r)] = r
        for b in wb:
            if b.w is not None:
                deps[id(b.w)] = b.w
            for r in b.rs:
                deps[id(r)] = r
        dl = []
        for d in deps.values():
            if d is o:
                continue
            if d.dma_key is None and d.eng == "pe" and eng == "pe" and dma_key is None:
                continue
            dl.append(d)
        o.deps = dl
        for d in dl:
            d.signal = True
        if dma_key is not None:
            c = self.dma_keys.get(dma_key, 0) + 1
            self.dma_keys[dma_key] = c
            o.dma_cnt = c
        o.idx = len(self.ops[eng])
        self.ops[eng].append(o)
        for b in rb:
            if b not in wb:
                b.rs.append(o)
        for b in wb:
            b.w = o
            b.rs = []
        return o

    def mm(self, out, lhsT, rhs, start=True, stop=True, **kw):
        rd = [lhsT, rhs] + ([] if start else [out])
        return self.op("pe", lambda e: e.matmul(out.ap, lhsT.ap, rhs.ap, start=start, stop=stop, **kw),
                       reads=rd, writes=[out])

    def tr(self, out, in_, ident):
        return self.op("pe", lambda e: e.transpose(out.ap, in_.ap, ident.ap), reads=[in_, ident], writes=[out])

    def act(self, out, in_, func, bias=None, scale=None, accum_out=None, eng="act"):
        rd = [in_]
        kw = {}
        if bias is not None:
            if isinstance(bias, View):
                rd.append(bias)
                kw["bias"] = bias.ap
            else:
                kw["bias"] = bias
        if scale is not None:
            if isinstance(scale, View):
                rd.append(scale)
                kw["scale"] = scale.ap
            else:
                kw["scale"] = scale
        wr = [out]
        if accum_out is not None:
            wr.append(accum_out)
            kw["accum_out"] = accum_out.ap
        return self.op(eng, lambda e: e.activation(out.ap, in_.ap, func, **kw), reads=rd, writes=wr)

    def tt(self, out, a, b, op, eng="dve"):
        return self.op(eng, lambda e: e.tensor_tensor(out.ap, a.ap, b.ap, op), reads=[a, b], writes=[out])

    def ts(self, out, a, s1, s2, op0, op1=None, eng="dve", accum_out=None):
        rd = [a]
        s1v = s1.ap if isinstance(s1, View) else s1
        s2v = s2.ap if isinstance(s2, View) else s2
        if isinstance(s1, View):
            rd.append(s1)
        if isinstance(s2, View):
            rd.append(s2)
        wr = [out]
        kw = {}
        if accum_out is not None:
            wr.append(accum_out)
            kw["accum_out"] = accum_out.ap
        if op1 is None:
            return self.op(eng, lambda e: e.tensor_scalar(out.ap, a.ap, s1v, None, op0, **kw), reads=rd, writes=wr)
        return self.op(eng, lambda e: e.tensor_scalar(out.ap, a.ap, s1v, s2v, op0, op1, **kw), reads=rd, writes=wr)

    def stt(self, out, a, s, b, op0, op1, eng="dve"):
        rd = [a, b]
        sv = s.ap if isinstance(s, View) else s
        if isinstance(s, View):
            rd.append(s)
        return self.op(eng, lambda e: e.scalar_tensor_tensor(out.ap, a.ap, sv, b.ap, op0, op1), reads=rd, writes=[out])

    def copy(self, out, in_, eng="dve"):
        if eng == "act":
            return self.op(eng, lambda e: e.activation(out.ap, in_.ap, AF.Identity), reads=[in_], writes=[out])
        return self.op(eng, lambda e: e.tensor_copy(out.ap, in_.ap), reads=[in_], writes=[out])

    def memset(self, out, val, eng="dve"):
        return self.op(eng, lambda e: e.memset(out.ap, val), reads=[], writes=[out])

    def reduce(self, out, in_, op, axis=None, eng="dve"):
        ax = axis if axis is not None else AX.X
        return self.op(eng, lambda e: e.tensor_reduce(out.ap, in_.ap, ax, op), reads=[in_], writes=[out])

    def recip(self, out, in_):
        return self.op("dve", lambda e: e.reciprocal(out.ap, in_.ap), reads=[in_], writes=[out])

    def dma(self, out, in_, key, eng="sp", **kw):
        return self.op(eng, lambda e: e.dma_start(out=out.ap, in_=in_.ap, **kw), reads=[in_], writes=[out], dma_key=key)

    def emit(self, final_wait_keys=()):
        nc = self.nc
        from contextlib import ExitStack
        es = ExitStack()
        eng_sems = {}
        for e in ENGS:
            n = 0
            for o in self.ops[e]:
                if o.dma_key is None and o.signal:
                    n += 1
                    o.ms = n
            nwin = (n + WINDOW - 1) // WINDOW
            eng_sems[e] = [es.enter_context(nc.semaphore(f"s_{e}_{i}")) for i in range(max(nwin, 1))]
        key_sems = {k: es.enter_context(nc.semaphore(f"k_{k}")) for k in self.dma_keys}
        self.nsem = sum(len(v) for v in eng_sems.values()) + len(key_sems)

        def dep_sem(d):
            if d.dma_key is not None:
                return key_sems[d.dma_key], 16 * d.dma_cnt
            w = (d.ms - 1) // WINDOW
            return eng_sems[d.eng][w], d.ms - w * WINDOW

        def run(e, engobj):
            waited = {}
            for o in self.ops[e]:
                need = {}
                for d in o.deps:
                    s, v = dep_sem(d)
                    k = id(s)
                    if need.get(k, (None, 0))[1] < v:
                        need[k] = (s, v)
                for k, (s, v) in need.items():
                    if waited.get(k, 0) < v:
                        engobj.wait_ge(s, v)
                        waited[k] = v
                ins = o.fn(engobj)
                if o.dma_key is not None:
                    ins.then_inc(key_sems[o.dma_key], 16)
                elif o.signal:
                    w = (o.ms - 1) // WINDOW
                    ins.then_inc(eng_sems[e][w], 1)
            if e == "sp":
                for k in final_wait_keys:
                    engobj.wait_ge(key_sems[k], 16 * self.dma_keys[k])

        with nc.Block() as block:
            @block.tensor
            def _(pe):
                run("pe", pe)

            @block.scalar
            def _(a):
                run("act", a)

            @block.vector
            def _(v):
                run("dve", v)

            @block.gpsimd
            def _(g):
                run("pool", g)

            @block.sync
            def _(s):
                run("sp", s)
        es.close()

D = 1024
SEQ = 4096
NMEM = 256
DIN = 3080
DFF = 4096
TS = 512
NT = TS // 128
EPS = 1e-6
NRING = 3


class Arena:
    PAGE = 128

    def __init__(self, P, name, words):
        self.t = P.sb(name, [128, words], F32)
        self.words = words
        self.pages = [Buf(f"{name}_pg{i}") for i in range((words + self.PAGE - 1) // self.PAGE)]

    def view(self, off, nwords, dt=F32, pat=None, **kw):
        assert off + nwords <= self.words, (off, nwords, self.words)
        ap = self.t.h[:, off:off + nwords]
        if dt != F32:
            ap = ap.bitcast(dt)
        if pat is not None:
            ap = ap.rearrange(pat, **kw)
        p0 = off // self.PAGE
        p1 = (off + nwords - 1) // self.PAGE
        return T(ap, "av", bufs=tuple(self.pages[p0:p1 + 1]))


class Alloc:
    def __init__(self, arena, start=0):
        self.a = arena
        self.o = start

    def f32(self, shape):
        n = int(np.prod(shape))
        n_al = (n + 127) // 128 * 128
        pat, kw = _pat(shape)
        t = self.a.view(self.o, n, F32, pat, **kw)
        self.o += n_al
        return t

    def bf16(self, shape):
        n = int(np.prod(shape))
        w = (n + 1) // 2
        w_al = (w + 127) // 128 * 128
        pat, kw = _pat(shape)
        t = self.a.view(self.o, w, BF16, pat, **kw)
        self.o += w_al
        return t


def _pat(shape):
    if len(shape) == 1:
        return None, {}
    if len(shape) == 2:
        return "p (a b) -> p a b", {"a": shape[0]}
    if len(shape) == 3:
        return "p (a b c) -> p a b c", {"a": shape[0], "b": shape[1]}
    raise ValueError(shape)


def build_program(NSEG=SEQ // TS, NL=2, dbg=None):
    nc = bass.Bass("TRN2", target_bir_lowering=False)
    P = Prog(nc)
    L = 2

    class _Stop(Exception):
        pass

    st = {"dbg": dbg}

    def ck(name, l=None):
        if dbg == name or (l is not None and dbg == f"{name}{l}"):
            if "xT" in st:
                xT_, outd_ = st["xT"], st["outd"]
                for c in range(8):
                    P.dma(View(outd_.h[c * 128:(c + 1) * 128, 0:TS], outd_.bufs), xT_[:, c, :], key="out", eng="sp")
            raise _Stop()
    try:
        _build_body(nc, P, L, NSEG, NL, ck, st)
    except _Stop:
        P.emit(final_wait_keys=[k for k in ("out",) if k in P.dma_keys])
    return nc, P


def _build_body(nc, P, L, NSEG, NL, ck, st):
    din = {}

    def di(name, shape):
        din[name] = P.dram(name, shape, F32, "ExternalInput")
        return din[name]
    xd = di("x", [SEQ, D])
    memd = di("mem", [NMEM, D])
    norm_mix = di("norm_mix", [L, D])
    w_in = di("w_in", [L, D, DIN])
    gm_ln_g = di("gm_ln_g", [L, 4, 128])
    gm_ln_b = di("gm_ln_b", [L, 4, 128])
    gm_ws = di("gm_ws", [L, 4, 128, 128])
    gm_bs = di("gm_bs", [L, 4, 128])
    dn_conv = di("dn_conv", [L, 4, 1536])
    dn_a_log = di("dn_a_log", [L, 4])
    dn_dt_bias = di("dn_dt_bias", [L, 4])
    dn_onorm = di("dn_onorm", [L, 128])
    w_out = di("w_out", [L, D, D])
    norm_xa = di("norm_xa", [L, D])
    norm_mem = di("norm_mem", [L, D])
    xa_wq = di("xa_wq", [L, D, D])
    xa_wk = di("xa_wk", [L, D, D])
    xa_wv = di("xa_wv", [L, D, D])
    xa_wo = di("xa_wo", [L, D, D])
    norm_ffn = di("norm_ffn", [L, D])
    ffn_w1 = di("ffn_w1", [L, D, DFF])
    ffn_w2 = di("ffn_w2", [L, DFF, D])
    norm_final = di("norm_final", [D])
    outd = P.dram("out", [SEQ, D], F32, "ExternalOutput")

    banks = [P.ps(f"bank{i}", [128, 512], F32) for i in range(8)]
    bstate = {"i": 0}

    def nb(exclude=()):
        while True:
            b = banks[bstate["i"] % 8]
            bstate["i"] += 1
            if b not in exclude:
                return b

    def b16(bank):
        return View(bank.h[:, :].bitcast(BF16), bank.bufs)

    ident_f = P.sb("ident_f", [128, 128], F32)
    ident_b = P.sb("ident_b", [128, 128], BF16)
    ones_b = P.sb("ones_b", [128, 128], BF16)
    ones_f = P.sb("ones_f", [128, 128], F32)
    U_f = P.sb("U_f", [128, 128], F32)
    bones_f = P.sb("bones_f", [128, 128], F32)
    csel_f = P.sb("csel_f", [128, 2, 128], F32)
    negm_b = P.sb("negm_b", [128, 4, 128], BF16)
    nsu_f = P.sb("nsu_f", [128, 4, 128], F32)
    colsA = P.sb("colsA", [128, 128], F32)
    colsB = P.sb("colsB", [128, 96], F32)
    onorm_bc = P.sb("onorm_bc", [128, L, 128], F32)
    alog_bc = P.sb("alog_bc", [128, L, 4], F32)
    dtb_bc = P.sb("dtb_bc", [128, L, 4], F32)
    negA_bc = P.sb("negA_bc", [128, L, 4], F32)
    Bg = P.sb("Bg", [128, L, 512], F32)
    WsT = P.sb("WsT", [128, L, 512], BF16)
    wtail = P.sb("wtail", [128, L, 64], BF16)
    hist = [P.sb(f"hist{l}", [128, 12, 4], BF16) for l in range(L)]
    KT = [P.sb(f"KT{l}", [128, 8, 256], BF16) for l in range(L)]
    Vm = [P.sb(f"Vm{l}", [128, 2, 1024], BF16) for l in range(L)]
    diag = P.sb("diag", [128, 48, 128], BF16)
    Sst = [P.sb(f"S{l}", [128, 4, 128], F32) for l in range(L)]
    Rr = P.sb("Rr", [128, 4, 128], F32R)
    residc = [P.sb(f"resid_r{c}", [128, 4, 128], F32R) for c in range(2)]
    xT = P.sb("xT", [128, 8, TS], F32)
    hT = P.sb("hT", [128, 8, TS], BF16)
    st["xT"] = xT
    st["outd"] = outd
    sq = [P.sb(f"sq{i}", [128, TS], BF16) for i in range(2)]
    lnv = P.sb("lnv", [128, TS], F32)
    rstd = P.sb("rstd", [128, TS], F32)
    ring = [P.sb(f"ring{i}", [128, 8, 512], BF16) for i in range(NRING)]
    YR = [P.sb(f"YR{i}", [128, 4, 2, 128], F32R) for i in range(2)]
    XX = [P.sb(f"XX{i}", [128, 4, 128], F32R) for i in range(2)]
    ARW = 21 * 1024
    arena = Arena(P, "arena", ARW)

    NSCR = 28 * NL
    wsc = nc.dram_tensor("wscratch", [NSCR, 128, 4096], BF16, kind="Internal").ap()
    wsc_bufs = [(Buf(f"wsc{i}"),) for i in range(NSCR)]

    class WStream:
        def __init__(self):
            self.blocks = []
            self.issued = 0
            self.taken = 0
            self.seen = set()

        def add(self, src, bid=None):
            first = bid not in self.seen
            if bid is not None:
                self.seen.add(bid)
            self.blocks.append((src, bid, first))

        def _issue(self, n):
            while self.issued < min(n, len(self.blocks)):
                i = self.issued
                s = i % NRING
                slot = ring[s]
                src, bid, first = self.blocks[i]
                flat = slot[:, :, :].rr("p a b -> p (a b)")
                if bid is None or first:
                    for q in range(4):
                        P.dma(slot[:, 2 * q:2 * q + 2, :], src[:, 2 * q:2 * q + 2, :], key=f"ring{s}", eng="pool")
                    if bid is not None and NSEG > 1:
                        P.dma(View(wsc[bid], wsc_bufs[bid]), flat, key=f"wb{s}", eng="sp")
                else:
                    P.dma(flat, View(wsc[bid], wsc_bufs[bid]), key=f"ring{s}", eng="sp")
                self.issued += 1

        def take(self):
            i = self.taken
            self._issue(i + NRING)
            self.taken += 1
            return ring[i % NRING]

    WS = WStream()

    def wblk(Wd, l, r0, c0):
        return View(Wd.h[l, r0:r0 + 1024, c0:c0 + 512].rearrange("(kc p) n -> p kc n", p=128), Wd.bufs)

    for l in range(NL):
        for c in range(2):
            WS.add(wblk(xa_wk, l, 0, 512 * c))
        for c in range(2):
            WS.add(wblk(xa_wv, l, 0, 512 * c))
    for seg in range(NSEG):
        for l in range(NL):
            bid = [28 * l]

            def nxt():
                bid[0] += 1
                return bid[0] - 1
            for c in range(6):
                WS.add(wblk(w_in, l, 0, 512 * c), nxt())
            for c in range(2):
                WS.add(wblk(w_out, l, 0, 512 * c), nxt())
            for c in range(2):
                WS.add(wblk(xa_wq, l, 0, 512 * c), nxt())
            for c in range(2):
                WS.add(wblk(xa_wo, l, 0, 512 * c), nxt())
            for c in range(8):
                WS.add(wblk(ffn_w1, l, 0, 512 * c), nxt())
            for cg in range(2):
                for ks in range(4):
                    WS.add(wblk(ffn_w2, l, 1024 * ks, 512 * cg), nxt())

    def iota_mask(t, fill_keep_cmp, fill, pattern_w=128, nrep=1):
        pass

    P.memset(ident_f[:, :], 0.0, eng="pool")
    P.op("pool", lambda e: e.affine_select(ident_f.h[:, :], ident_f.h[:, :], pattern=[[-1, 128]],
                                           compare_op=ALU.not_equal, fill=1.0, base=0, channel_multiplier=1),
         reads=[ident_f], writes=[ident_f])
    P.copy(ident_b[:, :], ident_f[:, :], eng="pool")
    P.memset(ones_b[:, :], 1.0, eng="pool")
    P.memset(ones_f[:, :], 1.0, eng="pool")
    P.memset(U_f[:, :], 1.0, eng="pool")
    P.op("pool", lambda e: e.affine_select(U_f.h[:, :], U_f.h[:, :], pattern=[[1, 128]],
                                           compare_op=ALU.is_ge, fill=0.0, base=0, channel_multiplier=-1),
         reads=[U_f], writes=[U_f])
    P.memset(negm_b[:, :, :], 0.0, eng="pool")
    P.op("pool", lambda e: e.affine_select(negm_b.h[:, :, :], negm_b.h[:, :, :], pattern=[[0, 4], [1, 128]],
                                           compare_op=ALU.is_ge, fill=-30000.0, base=0, channel_multiplier=-1),
         reads=[negm_b], writes=[negm_b])
    P.memset(nsu_f[:, :, :], -1.0, eng="pool")
    P.op("pool", lambda e: e.affine_select(nsu_f.h[:, :, :], nsu_f.h[:, :, :], pattern=[[0, 4], [1, 128]],
                                           compare_op=ALU.is_gt, fill=0.0, base=0, channel_multiplier=-1),
         reads=[nsu_f], writes=[nsu_f])

    ck("s1")
    P.memset(U_f[0:64, 64:128], 0.0, eng="pool")
    P.memset(negm_b[0:64, :, 64:128], -30000.0, eng="pool")
    P.memset(nsu_f[0:64, :, 64:128], 0.0, eng="pool")
    P.memset(bones_f[:, :], 1.0, eng="pool")
    P.memset(bones_f[0:64, 64:128], 0.0, eng="pool")
    P.memset(bones_f[64:128, 0:64], 0.0, eng="pool")
    P.memset(csel_f[:, :, :], 1.0, eng="pool")
    P.memset(csel_f[64:128, 0, :], 0.0, eng="pool")
    P.memset(csel_f[0:64, 1, :], 0.0, eng="pool")
    for c in range(2):
        P.ts(residc[c][:, :, :], nsu_f[:, :, :], 0.0, None, ALU.mult)
    pa = Alloc(arena)
    stageA = pa.f32([128])
    stageB = pa.f32([128])
    ws_nat = pa.f32([L * 4, 128])
    gmem_bc = pa.f32([L, D])
    memst = pa.f32([2, D])
    mn_b = pa.bf16([2, D])
    mnT = pa.bf16([8, 256])
    junk = pa.f32([D])
    sm = pa.f32([64])
    bs_bc = pa.f32([L, 512])
    lngb_dummy = None

    P.memset(stageA[:, :], 0.0, eng="dve")
    P.memset(stageB[:, :], 0.0, eng="dve")

    ikey = {"n": 0}

    def ik():
        ikey["n"] += 1
        return f"init{ikey['n']}"

    def rows(dst, r0, n, src_ap):
        P.dma(dst[r0:r0 + n, :], View(src_ap, (Buf("dram"),)), key=ik(), eng="sp")
    for l in range(L):
        rows(stageA, 8 * l, 8, norm_mix.h[l].rearrange("(c p) -> c p", p=128))
        rows(stageA, 16 + 8 * l, 8, norm_xa.h[l].rearrange("(c p) -> c p", p=128))
        rows(stageA, 32 + 8 * l, 8, norm_ffn.h[l].rearrange("(c p) -> c p", p=128))
        rows(stageA, 56 + 4 * l, 4, gm_ln_g.h[l])
        rows(stageA, 64 + 4 * l, 4, gm_ln_b.h[l])
        for j in range(4):
            rows(stageB, 48 * l + 12 * j, 12, dn_conv.h[l, j].rearrange("(c p) -> c p", p=128))
    rows(stageA, 48, 8, norm_final.h.rearrange("(c p) -> c p", p=128))

    ck("s2")

    def bcast_load(dst_view, src_ap):
        P.dma(dst_view, View(src_ap.partition_broadcast(128), (Buf("dram"),)), key=ik(), eng="sp")
    for l in range(L):
        bcast_load(onorm_bc[:, l, :], dn_onorm.h[l])
        bcast_load(bs_bc[:, l, :], gm_bs.h[l].rearrange("g t -> (g t)"))
        bcast_load(alog_bc[:, l, :], dn_a_log.h[l])
        bcast_load(dtb_bc[:, l, :], dn_dt_bias.h[l])
        bcast_load(gmem_bc[:, l, :], norm_mem.h[l])
        P.dma(ws_nat[:, 4 * l:4 * l + 4, :], View(gm_ws.h[l].rearrange("g t s -> t g s"), (Buf("dram"),)), key=ik(), eng="sp")
        P.dma(wtail[:, l, :].rr("p (kc n) -> p kc n", kc=8),
              View(w_in.h[l, :, 3072:3080].rearrange("(kc p) n -> p kc n", p=128), (Buf("dram"),)), key=ik(), eng="pool")
    P.dma(memst[:, :, :], View(memd.h.rearrange("(mt p) d -> p mt d", p=128), (Buf("dram"),)), key=ik(), eng="sp")

    ck("s3")
    bk = nb()
    P.tr(bk[:, 0:128], stageA[:, :], ident_f[:, :])
    P.tr(bk[:, 128:224], stageB[0:96, :], ident_f[0:96, 0:96])
    P.copy(colsA[:, :], bk[:, 0:128], eng="dve")
    P.copy(colsB[:, :], bk[:, 128:224], eng="dve")

    def gcol(kind, l):
        base = {"mix": 0, "xa": 16, "ffn": 32}[kind] + 8 * l
        return colsA[:, base:base + 8]

    ck("s4")
    P.act(negA_bc[:, :, :], alog_bc[:, :, :], AF.Exp)
    P.ts(negA_bc[:, :, :], negA_bc[:, :, :], -1.0, None, ALU.mult)

    ck("s5")
    for l in range(L):
        for g in range(4):
            i = 4 * l + g
            P.op("pool", lambda e, i=i: e.affine_select(ws_nat.h[:, i, :], ws_nat.h[:, i, :], pattern=[[-1, 128]],
                                                        compare_op=ALU.is_ge, fill=0.0, base=0, channel_multiplier=1),
                 reads=[ws_nat], writes=[ws_nat])
        ck("s6")
        bk = nb()
        for g in range(4):
            P.tr(bk[:, g * 128:(g + 1) * 128], ws_nat[:, 4 * l + g, :], ident_f[:, :])
        ck("s6a")
        wsT_f = junk
        P.copy(wsT_f[:, 0:512], bk[:, :], eng="dve")
        ck("s6b")
        P.copy(WsT[:, l, :], wsT_f[:, 0:512], eng="act")
        ck("s7")
        bk2 = nb()
        P.mm(bk2[:, :], ones_f[:, :], wsT_f[:, 0:512], start=True, stop=True)
        for g in range(4):
            P.stt(Bg[:, l, g * 128:(g + 1) * 128], bk2[:, g * 128:(g + 1) * 128], colsA[:, 64 + 4 * l + g:65 + 4 * l + g],
                  bs_bc[:, l, g * 128:(g + 1) * 128], ALU.mult, ALU.add)

    ck("setup")
    for l in range(NL):
        for mt in range(2):
            P.act(junk[:, :], memst[:, mt, :], AF.Square, accum_out=sm[:, mt:mt + 1])
        P.act(sm[:, 2:4], sm[:, 0:2], AF.Ln, bias=EPS, scale=1.0 / D)
        P.act(sm[:, 4:6], sm[:, 2:4], AF.Exp, scale=-0.5)
        for mt in range(2):
            P.stt(mn_b[:, mt, :], memst[:, mt, :], sm[:, 4 + mt:5 + mt], gmem_bc[:, l, :], ALU.mult, ALU.mult)
        for mt in range(2):
            bk = nb()
            for kc in range(8):
                P.tr(b16(bk)[:, kc * 128:(kc + 1) * 128], mn_b[:, mt, kc * 128:(kc + 1) * 128], ident_b[:, :])
            P.copy(mnT[:, :, mt * 128:(mt + 1) * 128], b16(bk)[:, :].rr("p (a b) -> p a b", a=8), eng="dve")
        for c in range(2):
            W = WS.take()
            for cc in range(4):
                bk = nb()
                for kc in range(8):
                    P.mm(bk[:, 0:256], W[:, kc, cc * 128:(cc + 1) * 128], mnT[:, kc, :], start=(kc == 0), stop=(kc == 7))
                P.copy(KT[l][:, 4 * c + cc, :], bk[:, 0:256], eng="act")
        for c in range(2):
            W = WS.take()
            for mt in range(2):
                bk = nb()
                for kc in range(8):
                    P.mm(bk[:, :], mnT[:, kc, mt * 128:(mt + 1) * 128], W[:, kc, :], start=(kc == 0), stop=(kc == 7))
                P.copy(Vm[l][:, mt, 512 * c:512 * (c + 1)], bk[:, :], eng="dve")

    ck("kv")
    for l in range(L):
        P.memset(Sst[l][:, :, :], 0.0, eng="dve")
        P.memset(hist[l][:, :, :], 0.0, eng="dve")

    ma = Alloc(arena)
    uT = ma.bf16([4, TS])
    gv = [ma.f32([4, 128]) for _ in range(NT)]
    pcT = ma.bf16([12, TS + 4])
    sg = [ma.bf16([512]) for _ in range(NT)]
    bl = ma.f32([NT, 8])
    yT = ma.bf16([8, TS])
    tok = ma.f32([16, 16])
    vn_b = ma.bf16([512])
    tmpA = ma.f32([512])
    tmpB = ma.f32([512])
    stat = ma.f32([64])
    qk_f = ma.f32([2, 512])
    v_f = ma.f32([4, 128])
    qn_b = ma.bf16([4, 128])
    qd_f = ma.f32([4, 128])
    kn_b = ma.bf16([4, 128])
    kd_f = ma.f32([4, 128])
    kdT_f = ma.f32([4, 128])
    kdec_f = ma.f32([4, 128])
    kT_b = ma.bf16([4, 128])
    qnT_b = ma.bf16([4, 128])
    qdT_f = ma.f32([4, 128])
    ET = ma.f32([4, 128])
    GE = ma.f32([4, 128])
    QKT_f = ma.f32([4, 128])
    vnewc = [ma.f32([4, 128]) for _ in range(2)]
    o_f = ma.f32([4, 128])
    on_f = ma.f32([4, 128])
    gsc = ma.f32([4, 128])
    yb_b = ma.bf16([4, 128])
    assert ma.o <= ARW, ma.o
    xa = Alloc(arena)
    qxT = xa.bf16([8, TS])
    oxT = xa.bf16([8, TS])
    expT = [xa.bf16([2, TS]) for _ in range(2)]
    rinv = [xa.f32([TS]) for _ in range(2)]
    fa = Alloc(arena)
    hidT = fa.bf16([32, TS])
    rl = [fa.bf16([TS]) for _ in range(2)]
    xst = fa.f32([NT, D])
    ost = fa.f32([NT, D])
    outT = ost
    assert fa.o <= ARW, fa.o

    K_BETA, K_NBETA, K_G, K_D, K_NEGD, K_ED, K_KDS, K_CD0, K_TMP, K_TMP2, K_DL, K_CD1 = range(12)

    def rmsnorm_to_hT(gains, final_out=None):
        bk = nb()
        for c in range(8):
            s = sq[c % 2]
            P.act(s[:, :], xT[:, c, :], AF.Square)
            P.mm(bk[:, :], ones_b[:, :], s[:, :], start=(c == 0), stop=(c == 7))
        P.act(lnv[:, :], bk[:, :], AF.Ln, bias=EPS, scale=1.0 / D)
        P.act(rstd[:, :], lnv[:, :], AF.Exp, scale=-0.5)
        for c in range(8):
            dst = hT[:, c, :] if final_out is None else final_out[:, c, :]
            P.stt(dst, xT[:, c, :], gains[:, c:c + 1], rstd[:, :], ALU.mult, ALU.mult)

    def proj_fm(W, rhsT, nk, evac):
        for cc in range(4):
            bk = nb()
            for kc in range(nk):
                P.mm(bk[:, :], W[:, kc, cc * 128:(cc + 1) * 128], rhsT[:, kc, :], start=(kc == 0), stop=(kc == nk - 1))
            evac(cc, bk)

    def proj_tm(W, lhsT_all, evac):
        for tt in range(NT):
            bk = nb()
            for kc in range(8):
                P.mm(bk[:, :], lhsT_all[:, kc, tt * 128:(tt + 1) * 128], W[:, kc, :], start=(kc == 0), stop=(kc == 7))
            evac(tt, bk)

    def resid_add(c, bk):
        P.tt(xT[:, c, :], xT[:, c, :], bk[:, :], ALU.add)

    first_x = {"done": False}

    def load_x(seg):
        P.dma(xst[:, :, :], View(xd.h[seg * TS:(seg + 1) * TS, :].rearrange("(tt p) d -> p tt d", p=128), (Buf("dram"),)),
              key="xin", eng="sp")

    for seg in range(NSEG):
        if seg == 0:
            load_x(0)
        for c in range(8):
            bk = nb()
            for tt in range(NT):
                P.tr(bk[:, tt * 128:(tt + 1) * 128], xst[:, tt, c * 128:(c + 1) * 128], ident_f[:, :])
            P.copy(xT[:, c, :], bk[:, :], eng=("act" if c % 2 else "dve"))

        ck("xT")
        for l in range(NL):
            for j in range(4):
                for c in range(12):
                    r = 48 * l + 12 * j + c
                    P.ts(diag[:, 12 * j + c, :], ident_f[:, :], colsB[:, r:r + 1], None, ALU.mult,
                         eng=("dve" if (c % 2) else "pool"))
            P.memset(vnewc[0][64:128, :, :], 0.0, eng="pool")
            P.memset(vnewc[1][0:64, :, :], 0.0, eng="pool")
            P.copy(pcT[:, :, 0:4], hist[l][:, :, :], eng="dve")
            rmsnorm_to_hT(gcol("mix", l))
            ck("norm1")
            W = WS.take()
            proj_fm(W, hT, 8, lambda cc, bk: P.act(uT[:, cc, :], bk[:, :], AF.Gelu_apprx_tanh))
            W = WS.take()
            proj_tm(W, hT, lambda tt, bk: P.act(gv[tt][:, :, :], bk[:, :].rr("p (a b) -> p a b", a=4), AF.Gelu_apprx_tanh))
            for cg in range(3):
                W = WS.take()
                proj_fm(W, hT, 8, lambda cc, bk, cg=cg: P.copy(pcT[:, 4 * cg + cc, 4:4 + TS], bk[:, :],
                                                              eng=("act" if cc % 2 else "dve")))
            P.copy(hist[l][:, :, :], pcT[:, :, TS:TS + 4], eng="dve")
            W = WS.take()
            proj_tm(W, hT, lambda tt, bk: P.act(sg[tt][:, :], bk[:, :], AF.Silu))
            bk = nb()
            for tt in range(NT):
                for kc in range(8):
                    P.mm(bk[:, tt * 8:(tt + 1) * 8], hT[:, kc, tt * 128:(tt + 1) * 128], wtail[:, l, kc * 8:(kc + 1) * 8],
                         start=(kc == 0), stop=(kc == 7))
            P.copy(bl[:, :, :], bk[:, 0:NT * 8].rr("p (a b) -> p a b", a=NT), eng="dve")

            ck("inproj")
            def tk(kind):
                return tok[:, kind, :]

            def tk3(kind):
                return tok[:, kind, :].rr("p (a b) -> p a b", a=NT)
            P.act(tk3(K_TMP), bl[:, :, 0:4], AF.Tanh, scale=0.5)
            P.ts(tk(K_BETA), tk(K_TMP), 0.5, 0.5, ALU.mult, ALU.add)
            P.ts(tk(K_NBETA), tk(K_BETA), -1.0, None, ALU.mult)
            P.tt(tk3(K_TMP), bl[:, :, 4:8], dtb_bc[:, l:l + 1, :].bc([128, NT, 4]), ALU.add)
            P.act(tk(K_TMP2), tk(K_TMP), AF.Exp)
            P.act(tk(K_TMP), tk(K_TMP2), AF.Ln, bias=1.0)
            P.tt(tk3(K_G), tk3(K_TMP), negA_bc[:, l:l + 1, :].bc([128, NT, 4]), ALU.mult)
            bk = nb()
            P.mm(bk[:, 0:16], U_f[:, :], tk(K_G), start=True, stop=True)
            P.mm(bk[:, 16:32], bones_f[:, :], tk(K_G), start=True, stop=True)
            P.mm(bk[:, 32:48], csel_f[:, 0, :], tk(K_G), start=True, stop=True)
            P.mm(bk[:, 48:64], csel_f[:, 1, :], tk(K_G), start=True, stop=True)
            P.copy(tk(K_D), bk[:, 0:16], eng="dve")
            P.copy(tk(K_DL), bk[:, 16:32], eng="dve")
            P.act(tk(K_CD0), bk[:, 32:48], AF.Exp)
            P.act(tk(K_CD1), bk[:, 48:64], AF.Exp)
            P.ts(tk(K_NEGD), tk(K_D), -1.0, None, ALU.mult)
            P.act(tk(K_ED), tk(K_D), AF.Exp)
            P.tt(tk(K_TMP), tk(K_DL), tk(K_D), ALU.subtract)
            P.act(tk(K_KDS), tk(K_TMP), AF.Exp)

            ck("tok")
            S = Sst[l]
            for tt in range(NT):
                tsl = slice(tt * 128, (tt + 1) * 128)
                hs = slice(tt * 4, tt * 4 + 4)
                g3 = gv[tt]
                P.reduce(stat[:, 0:4], g3[:, :, :], ALU.add)
                P.act(tmpA[:, :], g3[:, :, :].rr("p a b -> p (a b)"), AF.Square)
                P.reduce(stat[:, 4:8], tmpA[:, :].rr("p (a b) -> p a b", a=4), ALU.add)
                P.ts(stat[:, 8:12], stat[:, 0:4], 1.0 / 128, None, ALU.mult)
                P.tt(stat[:, 12:16], stat[:, 8:12], stat[:, 8:12], ALU.mult)
                P.stt(stat[:, 16:20], stat[:, 4:8], 1.0 / 128, stat[:, 12:16], ALU.mult, ALU.subtract)
                P.act(stat[:, 20:24], stat[:, 16:20], AF.Ln, bias=EPS)
                P.act(stat[:, 24:28], stat[:, 20:24], AF.Exp, scale=-0.5)
                P.tt(tmpB[:, :].rr("p (a b) -> p a b", a=4), g3[:, :, :],
                     stat[:, 8:12].rr("p (a b) -> p a b", b=1).bc([128, 4, 128]), ALU.subtract)
                P.tt(vn_b[:, :].rr("p (a b) -> p a b", a=4), tmpB[:, :].rr("p (a b) -> p a b", a=4),
                     stat[:, 24:28].rr("p (a b) -> p a b", b=1).bc([128, 4, 128]), ALU.mult)
                bkA = nb()
                for g in range(4):
                    P.mm(bkA[:, g * 128:(g + 1) * 128], vn_b[:, g * 128:(g + 1) * 128], WsT[:, l, g * 128:(g + 1) * 128],
                         start=True, stop=True)
                P.tt(tmpA[:, :].rr("p (a b) -> p a b", a=4), bkA[:, :].rr("p (a b) -> p a b", a=4),
                     colsA[:, 56 + 4 * l:60 + 4 * l].rr("p (a b) -> p a b", b=1).bc([128, 4, 128]), ALU.mult)
                P.tt(tmpA[:, :], tmpA[:, :], Bg[:, l, :], ALU.add, eng="pool")
                P.tt(yT[:, 0:4, tsl], tmpA[:, :].rr("p (a b) -> p a b", a=4), uT[:, :, tsl], ALU.mult, eng="pool")

                ck("mixA")
                bq, bkk, bv = nb(), nb(), nb()
                for cg, bkc in enumerate((bq, bkk, bv)):
                    for cc in range(4):
                        c = 4 * cg + cc
                        for j in range(4):
                            c0 = tt * 128 + j + 1
                            P.mm(bkc[:, cc * 128:(cc + 1) * 128], pcT[:, c, c0:c0 + 128], diag[:, 12 * j + c, :],
                                 start=(j == 0), stop=(j == 3))
                P.act(qk_f[:, 0, :], bq[:, :], AF.Silu)
                P.act(qk_f[:, 1, :], bkk[:, :], AF.Silu)
                P.act(v_f[:, :, :].rr("p a b -> p (a b)"), bv[:, :], AF.Silu)
                P.act(tmpA[:, :], qk_f[:, 0, :], AF.Square)
                P.act(tmpB[:, :], qk_f[:, 1, :], AF.Square)
                P.reduce(stat[:, 32:36], tmpA[:, :].rr("p (a b) -> p a b", a=4), ALU.add)
                P.reduce(stat[:, 36:40], tmpB[:, :].rr("p (a b) -> p a b", a=4), ALU.add)
                P.act(stat[:, 40:48], stat[:, 32:40], AF.Ln, bias=EPS)
                P.act(stat[:, 48:56], stat[:, 40:48], AF.Exp, scale=-0.5)
                P.ts(stat[:, 48:52], stat[:, 48:52], 128.0 ** -0.5, None, ALU.mult)
                P.tt(stat[:, 56:60], stat[:, 48:52], tok[:, K_ED, hs], ALU.mult)
                P.tt(stat[:, 60:64], stat[:, 52:56], tok[:, K_ED, hs], ALU.mult)
                P.tt(stat[:, 28:32], stat[:, 52:56], tok[:, K_KDS, hs], ALU.mult)

                def bc4(v):
                    return v.rr("p (a b) -> p a b", b=1).bc([128, 4, 128])
                q3 = qk_f[:, 0, :].rr("p (a b) -> p a b", a=4)
                k3 = qk_f[:, 1, :].rr("p (a b) -> p a b", a=4)
                P.tt(qn_b[:, :, :], q3, bc4(stat[:, 48:52]), ALU.mult)
                P.tt(qd_f[:, :, :], q3, bc4(stat[:, 56:60]), ALU.mult)
                P.tt(kn_b[:, :, :], k3, bc4(stat[:, 52:56]), ALU.mult)
                P.tt(kd_f[:, :, :], k3, bc4(stat[:, 60:64]), ALU.mult, eng="pool")
                P.tt(kdec_f[:, :, :], k3, bc4(stat[:, 28:32]), ALU.mult, eng="pool")
                bkT = nb()
                for h in range(4):
                    P.tr(b16(bkT)[:, h * 128:(h + 1) * 128], kn_b[:, h, :], ident_b[:, :])
                    P.tr(b16(bkT)[:, 512 + h * 128:512 + (h + 1) * 128], qn_b[:, h, :], ident_b[:, :])
                P.copy(kT_b[:, :, :].rr("p a b -> p (a b)"), b16(bkT)[:, 0:512], eng="dve")
                P.copy(qnT_b[:, :, :].rr("p a b -> p (a b)"), b16(bkT)[:, 512:1024], eng="act")
                bkT2 = nb()
                for h in range(4):
                    P.tr(bkT2[:, h * 128:(h + 1) * 128], qd_f[:, h, :], ident_f[:, :])
                P.copy(qdT_f[:, :, :].rr("p a b -> p (a b)"), bkT2[:, :], eng="act")
                bkD = nb()
                P.mm(bkD[:, :], ident_b[:, :], negm_b[:, :, :].rr("p a b -> p (a b)"), start=True, stop=False)
                for h in range(4):
                    gcolv = tok[:, K_G, tt * 4 + h:tt * 4 + h + 1]
                    P.mm(bkD[:, h * 128:(h + 1) * 128], gcolv.bc([128, 128]), U_f[:, :], start=False, stop=True)
                for h in range(4):
                    P.act(ET[:, h, :], bkD[:, h * 128:(h + 1) * 128], AF.Exp, bias=tok[:, K_NEGD, tt * 4 + h:tt * 4 + h + 1])
                bkG = nb()
                bkKQ = nb()
                for h in range(4):
                    P.mm(bkG[:, h * 128:(h + 1) * 128], kT_b[:, h, :], kT_b[:, h, :], start=True, stop=True)
                for h in range(4):
                    P.mm(bkKQ[:, h * 128:(h + 1) * 128], kT_b[:, h, :], qnT_b[:, h, :], start=True, stop=True)
                P.tt(GE[:, :, :].rr("p a b -> p (a b)"), bkG[:, :], ET[:, :, :].rr("p a b -> p (a b)"), ALU.mult)
                P.tt(QKT_f[:, :, :].rr("p a b -> p (a b)"), bkKQ[:, :], ET[:, :, :].rr("p a b -> p (a b)"), ALU.mult)
                P.tt(GE[:, :, :], GE[:, :, :], nsu_f[:, :, :], ALU.mult, eng="pool")
                cur = 0
                P.tt(YR[cur][:, :, 0, :], GE[:, :, :], bc4(tok[:, K_BETA, hs]), ALU.mult)
                for h in range(4):
                    P.copy(YR[cur][:, h, 1, :], ident_f[:, :], eng="dve")
                bkX = nb()
                for h in range(4):
                    P.tr(bkX[:, h * 128:(h + 1) * 128], YR[cur][:, h, 0, :].bitcast(F32), ident_f[:, :])
                P.copy(XX[cur][:, :, :].rr("p a b -> p (a b)"), bkX[:, :], eng="act")
                NLEV = 6
                for k in range(NLEV):
                    nxt = 1 - cur
                    last = (k == NLEV - 1)
                    bA0, bA1 = nb(), nb()
                    bAs = (bA0, bA0, bA1, bA1)
                    if not last:
                        for h in range(4):
                            P.mm(bAs[h][:, (h % 2) * 256:(h % 2) * 256 + 256], XX[cur][:, h, :],
                                 YR[cur][:, h, :, :].rr("p a b -> p (a b)"), start=True, stop=True)
                        bB = nb()
                        for h in range(4):
                            P.mm(bB[:, h * 128:(h + 1) * 128], YR[cur][:, h, 0, :],
                                 XX[cur][:, h, :], start=True, stop=True)
                        for hp in range(2):
                            src = (bA0, bA1)[hp][:, :].rr("p (a b c) -> p a b c", a=2, b=2)
                            P.copy(YR[nxt][:, 2 * hp:2 * hp + 2, 0, :], src[:, :, 0, :], eng="act")
                            P.tt(YR[nxt][:, 2 * hp:2 * hp + 2, 1, :], src[:, :, 1, :],
                                 YR[cur][:, 2 * hp:2 * hp + 2, 1, :].bitcast(F32), ALU.add)
                        P.copy(XX[nxt][:, :, :].rr("p a b -> p (a b)"), bB[:, :], eng="act")
                        cur = nxt
                    else:
                        for h in range(4):
                            P.mm(bA0[:, h * 128:(h + 1) * 128], XX[cur][:, h, :],
                                 YR[cur][:, h, 1, :], start=True, stop=True)
                        P.tt(Rr[:, :, :], bA0[:, :].rr("p (a b) -> p a b", a=4), YR[cur][:, :, 1, :].bitcast(F32), ALU.add)
                bkW = nb()
                for h in range(4):
                    P.tr(bkW[:, h * 128:(h + 1) * 128], kd_f[:, h, :], ident_f[:, :])
                P.copy(kdT_f[:, :, :].rr("p a b -> p (a b)"), bkW[:, :], eng="act")
                for c in range(2):
                    rs = slice(64 * c, 64 * c + 64)
                    res_c, vn_c = residc[c], vnewc[c]
                    bkV1 = nb()
                    for h in range(4):
                        P.mm(bkV1[:, h * 128:(h + 1) * 128], kdT_f[:, h, :], S[:, h, :], start=True, stop=True)
                    P.tt(res_c[rs, :, :].rr("p a b -> p (a b)"), v_f[rs, :, :].rr("p a b -> p (a b)"), bkV1[rs, :], ALU.subtract)
                    bkV = nb()
                    for h in range(4):
                        P.mm(bkV[:, h * 128:(h + 1) * 128], Rr[:, h, :], res_c[:, h, :], start=True, stop=True)
                    beta_bc = View(tok.h[rs, K_BETA, hs].rearrange("p (a b) -> p a b", b=1).to_broadcast([64, 4, 128]), tok.bufs)
                    P.tt(vn_c[rs, :, :], bkV[rs, :].rr("p (a b) -> p a b", a=4), beta_bc, ALU.mult)
                    bkO = nb()
                    for h in range(4):
                        P.mm(bkO[:, h * 128:(h + 1) * 128], qdT_f[:, h, :], S[:, h, :], start=True, stop=False)
                        P.mm(bkO[:, h * 128:(h + 1) * 128], QKT_f[:, h, :], vn_c[:, h, :], start=False, stop=True)
                    P.copy(o_f[rs, :, :].rr("p a b -> p (a b)"), bkO[rs, :], eng="act")
                    bkS = nb()
                    for h in range(4):
                        P.mm(bkS[:, h * 128:(h + 1) * 128], kdec_f[:, h, :], vn_c[:, h, :], start=True, stop=True)
                    kcd = K_CD0 if c == 0 else K_CD1
                    for h in range(4):
                        P.stt(S[:, h, :], S[:, h, :], tok[:, kcd, tt * 4 + h:tt * 4 + h + 1], bkS[:, h * 128:(h + 1) * 128],
                              ALU.mult, ALU.add)
                P.act(tmpB[:, :], o_f[:, :, :].rr("p a b -> p (a b)"), AF.Square)
                P.reduce(stat[:, 0:4], tmpB[:, :].rr("p (a b) -> p a b", a=4), ALU.add)
                P.act(stat[:, 4:8], stat[:, 0:4], AF.Ln, bias=EPS, scale=1.0 / 128)
                P.act(stat[:, 8:12], stat[:, 4:8], AF.Exp, scale=-0.5)
                P.tt(gsc[:, :, :], sg[tt][:, :].rr("p (a b) -> p a b", a=4), onorm_bc[:, l:l + 1, :].bc([128, 4, 128]),
                     ALU.mult, eng="pool")
                P.tt(on_f[:, :, :], o_f[:, :, :], bc4(stat[:, 8:12]), ALU.mult)
                P.tt(yb_b[:, :, :], on_f[:, :, :], gsc[:, :, :], ALU.mult, eng="pool")
                bkY = nb()
                for h in range(4):
                    P.tr(b16(bkY)[:, h * 128:(h + 1) * 128], yb_b[:, h, :], ident_b[:, :])
                P.copy(yT[:, 4:8, tsl], b16(bkY)[:, 0:512].rr("p (a b) -> p a b", a=4), eng="act")

            st["yT"] = yT
            if st["dbg"] == f"ygdn{l}":
                for c in range(8):
                    P.dma(View(outd.h[c * 128:(c + 1) * 128, 0:TS], outd.bufs), yT[:, c, :], key="out", eng="pool")
                del st["xT"]
                ck(f"ygdn{l}")
            ck("gdn", l)
            for c in range(2):
                W = WS.take()
                proj_fm(W, yT, 8, lambda cc, bk, c=c: resid_add(4 * c + cc, bk))

            ck("wout", l)
            rmsnorm_to_hT(gcol("xa", l))
            for c in range(2):
                W = WS.take()
                proj_fm(W, hT, 8, lambda cc, bk, c=c: P.copy(qxT[:, 4 * c + cc, :], bk[:, :], eng=("act" if cc % 2 else "dve")))
            for h in range(4):
                e = expT[h % 2]
                for mc in range(2):
                    bk = nb()
                    for dc in range(2):
                        P.mm(bk[:, :], KT[l][:, 2 * h + dc, mc * 128:(mc + 1) * 128], qxT[:, 2 * h + dc, :],
                             start=(dc == 0), stop=(dc == 1))
                    P.act(e[:, mc, :], bk[:, :], AF.Exp, scale=1.0 / 16)
                bk = nb()
                for mc in range(2):
                    P.mm(bk[:, :], ones_b[:, :], e[:, mc, :], start=(mc == 0), stop=(mc == 1))
                ri = rinv[h % 2]
                P.recip(ri[:, :], bk[:, :])
                for dc in range(2):
                    bk = nb()
                    for mc in range(2):
                        P.mm(bk[:, :], Vm[l][:, mc, (2 * h + dc) * 128:(2 * h + dc + 1) * 128], e[:, mc, :],
                             start=(mc == 0), stop=(mc == 1))
                    P.tt(oxT[:, 2 * h + dc, :], bk[:, :], ri[:, :], ALU.mult)
            for c in range(2):
                W = WS.take()
                proj_fm(W, oxT, 8, lambda cc, bk, c=c: resid_add(4 * c + cc, bk))

            ck("xattn", l)
            rmsnorm_to_hT(gcol("ffn", l))
            for c in range(8):
                W = WS.take()

                def ev(cc, bk, c=c):
                    r = rl[cc % 2]
                    P.act(r[:, :], bk[:, :], AF.Relu)
                    P.tt(hidT[:, 4 * c + cc, :], r[:, :], r[:, :], ALU.mult, eng=("dve" if cc % 2 else "pool"))
                proj_fm(W, hT, 8, ev)
            if l == NL - 1 and seg + 1 < NSEG:
                load_x(seg + 1)
            for cg in range(2):
                accs = [nb() for _ in range(4)]
                for ks in range(4):
                    W = WS.take()
                    for cc in range(4):
                        for kc in range(8):
                            P.mm(accs[cc][:, :], W[:, kc, cc * 128:(cc + 1) * 128], hidT[:, ks * 8 + kc, :],
                                 start=(ks == 0 and kc == 0), stop=(ks == 3 and kc == 7))
                for cc in range(4):
                    resid_add(4 * cg + cc, accs[cc])
            ck("ffn", l)

        fin = hidT_f32 = None
        fo = Alloc(arena)
        finT = fo.f32([8, TS])
        rmsnorm_to_hT(colsA[:, 48:56], final_out=finT)
        for tt in range(NT):
            for half in range(2):
                bk = nb()
                for j in range(4):
                    c = 4 * half + j
                    P.tr(bk[:, j * 128:(j + 1) * 128], finT[:, c, tt * 128:(tt + 1) * 128], ident_f[:, :])
                P.copy(ost[:, tt, half * 512:(half + 1) * 512], bk[:, :], eng=("act" if half else "dve"))
        P.dma(View(outd.h[seg * TS:(seg + 1) * TS, :].rearrange("(tt p) d -> p tt d", p=128), outd.bufs), ost[:, :, :],
              key="out", eng="sp")

    P.emit(final_wait_keys=["out"])


_CACHE = {}


def kernel(**inputs):
    names = ["x", "mem", "norm_mix", "w_in", "gm_ln_g", "gm_ln_b", "gm_ws", "gm_bs", "dn_conv", "dn_a_log",
             "dn_dt_bias", "dn_onorm", "w_out", "norm_xa", "norm_mem", "xa_wq", "xa_wk", "xa_wv", "xa_wo",
             "norm_ffn", "ffn_w1", "ffn_w2", "norm_final"]
    arrs = {k: np.ascontiguousarray(np.asarray(inputs[k], dtype=np.float32)) for k in names}
    nc, _ = build_program()
    in_maps = []
    for b in range(8):
        m = {k: arrs[k] for k in names if k not in ("x", "mem")}
        m["x"] = np.ascontiguousarray(arrs["x"][b])
        m["mem"] = np.ascontiguousarray(arrs["mem"][b])
        in_maps.append(m)
    res = run_bass_kernel_spmd(nc, in_maps, core_ids=list(range(8)))
    out = np.stack([np.asarray(r["out"], dtype=np.float32) for r in res.results], axis=0)
    return out
```

```python
from concourse.bass_utils import run_bass_kernel_spmd

import numpy as np
import concourse.bass as bass
import concourse.mybir as mybir

F32 = mybir.dt.float32
BF16 = mybir.dt.bfloat16
F32R = mybir.dt.float32r
AF = mybir.ActivationFunctionType
ALU = mybir.AluOpType
AX = mybir.AxisListType


class Buf:
    __slots__ = ("name", "w", "rs", "xr")

    def __init__(self, name):
        self.name = name
        self.w = None
        self.rs = []
        self.xr = False


class View:
    __slots__ = ("ap", "bufs")

    def __init__(self, ap, bufs):
        self.ap = ap
        self.bufs = bufs

    def __getitem__(self, idx):
        return View(self.ap[idx], self.bufs)

    def bitcast(self, dt):
        return View(self.ap.bitcast(dt), self.bufs)

    def bc(self, shape):
        return View(self.ap.to_broadcast(shape), self.bufs)

    def rr(self, s, **kw):
        return View(self.ap.rearrange(s, **kw), self.bufs)


class T:
    def __init__(self, h, name, bufs=None):
        self.h = h
        self.bufs = bufs if bufs is not None else (Buf(name),)
        self.name = name

    def __getitem__(self, idx):
        return View(self.h[idx], self.bufs)

    def sub(self, name):
        return T(self.h, name)


class Op:
    __slots__ = ("eng", "fn", "deps", "idx", "signal", "ms", "dma_key", "dma_cnt", "name")


ENGS = ("pe", "act", "dve", "pool", "sp")
WINDOW = 16000


class Prog:
    def __init__(self, nc):
        self.nc = nc
        self.ops = {e: [] for e in ENGS}
        self.dma_keys = {}
        self.nsb = 0

    def sb(self, name, shape, dt=F32):
        h = self.nc.alloc_sbuf_tensor(name, list(shape), dt)
        return T(h, name)

    def ps(self, name, shape, dt=F32):
        h = self.nc.alloc_psum_tensor(name, list(shape), dt)
        t = T(h, name)
        t.bufs[0].xr = True
        return t

    def dram(self, name, shape, dt, kind):
        h = self.nc.dram_tensor(name, list(shape), dt, kind=kind)
        return T(h.ap(), name)

    def op(self, eng, fn, reads=(), writes=(), dma_key=None, name=None):
        o = Op()
        o.eng = eng
        o.fn = fn
        o.signal = False
        o.ms = None
        o.dma_key = dma_key
        o.dma_cnt = None
        o.name = name
        deps = {}
        rb = []
        for r in reads:
            for b in r.bufs:
                if b not in rb:
                    rb.append(b)
        wb = []
        for w in writes:
            for b in w.bufs:
                if b not in wb:
                    wb.append(b)
        for b in rb:
            if b.w is not None:
                deps[id(b.w)] = b.w
            if b.xr:
                for r in b.rs:
                    if r.eng != eng:
                        deps[id(r)] = r
        for b in wb:
            if b.w is not None:
                deps[id(b.w)] = b.w
            for r in b.rs:
                deps[id(r)] = r
        dl = []
        for d in deps.values():
            if d is o:
                continue
            if d.dma_key is None and d.eng == "pe" and eng == "pe" and dma_key is None:
                continue
            dl.append(d)
        o.deps = dl
        for d in dl:
            d.signal = True
        if dma_key is not None:
            c = self.dma_keys.get(dma_key, 0) + 1
            self.dma_keys[dma_key] = c
            o.dma_cnt = c
        o.idx = len(self.ops[eng])
        self.ops[eng].append(o)
        for b in rb:
            if b not in wb:
                b.rs.append(o)
        for b in wb:
            b.w = o
            b.rs = []
        return o

    def mm(self, out, lhsT, rhs, start=True, stop=True, **kw):
        rd = [lhsT, rhs] + ([] if start else [out])
        return self.op("pe", lambda e: e.matmul(out.ap, lhsT.ap, rhs.ap, start=start, stop=stop, **kw),
                       reads=rd, writes=[out])

    def tr(self, out, in_, ident):
        return self.op("pe", lambda e: e.transpose(out.ap, in_.ap, ident.ap), reads=[in_, ident], writes=[out])

    def act(self, out, in_, func, bias=None, scale=None, accum_out=None, eng="act"):
        rd = [in_]
        kw = {}
        if bias is not None:
            if isinstance(bias, View):
                rd.append(bias)
                kw["bias"] = bias.ap
            else:
                kw["bias"] = bias
        if scale is not None:
            if isinstance(scale, View):
                rd.append(scale)
                kw["scale"] = scale.ap
            else:
                kw["scale"] = scale
        wr = [out]
        if accum_out is not None:
            wr.append(accum_out)
            kw["accum_out"] = accum_out.ap
        return self.op(eng, lambda e: e.activation(out.ap, in_.ap, func, **kw), reads=rd, writes=wr)

    def tt(self, out, a, b, op, eng="dve"):
        return self.op(eng, lambda e: e.tensor_tensor(out.ap, a.ap, b.ap, op), reads=[a, b], writes=[out])

    def ts(self, out, a, s1, s2, op0, op1=None, eng="dve", accum_out=None):
        rd = [a]
        s1v = s1.ap if isinstance(s1, View) else s1
        s2v = s2.ap if isinstance(s2, View) else s2
        if isinstance(s1, View):
            rd.append(s1)
        if isinstance(s2, View):
            rd.append(s2)
        wr = [out]
        kw = {}
        if accum_out is not None:
            wr.append(accum_out)
            kw["accum_out"] = accum_out.ap
        if op1 is None:
            return self.op(eng, lambda e: e.tensor_scalar(out.ap, a.ap, s1v, None, op0, **kw), reads=rd, writes=wr)
        return self.op(eng, lambda e: e.tensor_scalar(out.ap, a.ap, s1v, s2v, op0, op1, **kw), reads=rd, writes=wr)

    def stt(self, out, a, s, b, op0, op1, eng="dve"):
        rd = [a, b]
        sv = s.ap if isinstance(s, View) else s
        if isinstance(s, View):
            rd.append(s)
        return self.op(eng, lambda e: e.scalar_tensor_tensor(out.ap, a.ap, sv, b.ap, op0, op1), reads=rd, writes=[out])

    def copy(self, out, in_, eng="dve"):
        if eng == "act":
            return self.op(eng, lambda e: e.activation(out.ap, in_.ap, AF.Identity), reads=[in_], writes=[out])
        return self.op(eng, lambda e: e.tensor_copy(out.ap, in_.ap), reads=[in_], writes=[out])

    def memset(self, out, val, eng="dve"):
        return self.op(eng, lambda e: e.memset(out.ap, val), reads=[], writes=[out])

    def reduce(self, out, in_, op, axis=None, eng="dve"):
        ax = axis if axis is not None else AX.X
        return self.op(eng, lambda e: e.tensor_reduce(out.ap, in_.ap, ax, op), reads=[in_], writes=[out])

    def recip(self, out, in_):
        return self.op("dve", lambda e: e.reciprocal(out.ap, in_.ap), reads=[in_], writes=[out])

    def dma(self, out, in_, key, eng="sp", **kw):
        return self.op(eng, lambda e: e.dma_start(out=out.ap, in_=in_.ap, **kw), reads=[in_], writes=[out], dma_key=key)

    def emit(self, final_wait_keys=()):
        nc = self.nc
        from contextlib import ExitStack
        es = ExitStack()
        eng_sems = {}
        for e in ENGS:
            n = 0
            for o in self.ops[e]:
                if o.dma_key is None and o.signal:
                    n += 1
                    o.ms = n
            nwin = (n + WINDOW - 1) // WINDOW
            eng_sems[e] = [es.enter_context(nc.semaphore(f"s_{e}_{i}")) for i in range(max(nwin, 1))]
        key_sems = {k: es.enter_context(nc.semaphore(f"k_{k}")) for k in self.dma_keys}
        self.nsem = sum(len(v) for v in eng_sems.values()) + len(key_sems)

        def dep_sem(d):
            if d.dma_key is not None:
                return key_sems[d.dma_key], 16 * d.dma_cnt
            w = (d.ms - 1) // WINDOW
            return eng_sems[d.eng][w], d.ms - w * WINDOW

        def run(e, engobj):
            waited = {}
            for o in self.ops[e]:
                need = {}
                for d in o.deps:
                    s, v = dep_sem(d)
                    k = id(s)
                    if need.get(k, (None, 0))[1] < v:
                        need[k] = (s, v)
                for k, (s, v) in need.items():
                    if waited.get(k, 0) < v:
                        engobj.wait_ge(s, v)
                        waited[k] = v
                ins = o.fn(engobj)
                if o.dma_key is not None:
                    ins.then_inc(key_sems[o.dma_key], 16)
                elif o.signal:
                    w = (o.ms - 1) // WINDOW
                    ins.then_inc(eng_sems[e][w], 1)
            if e == "sp":
                for k in final_wait_keys:
                    engobj.wait_ge(key_sems[k], 16 * self.dma_keys[k])

        with nc.Block() as block:
            @block.tensor
            def _(pe):
                run("pe", pe)

            @block.scalar
            def _(a):
                run("act", a)

            @block.vector
            def _(v):
                run("dve", v)

            @block.gpsimd
            def _(g):
                run("pool", g)

            @block.sync
            def _(s):
                run("sp", s)
        es.close()

D = 1024
SEQ = 4096
NMEM = 256
DIN = 3080
DFF = 4096
TS = 512
NT = TS // 128
EPS = 1e-6
NRING = 3


class Arena:
    PAGE = 128

    def __init__(self, P, name, words):
        self.t = P.sb(name, [128, words], F32)
        self.words = words
        self.pages = [Buf(f"{name}_pg{i}") for i in range((words + self.PAGE - 1) // self.PAGE)]

    def view(self, off, nwords, dt=F32, pat=None, **kw):
        assert off + nwords <= self.words, (off, nwords, self.words)
        ap = self.t.h[:, off:off + nwords]
        if dt != F32:
            ap = ap.bitcast(dt)
        if pat is not None:
            ap = ap.rearrange(pat, **kw)
        p0 = off // self.PAGE
        p1 = (off + nwords - 1) // self.PAGE
        return T(ap, "av", bufs=tuple(self.pages[p0:p1 + 1]))


class Alloc:
    def __init__(self, arena, start=0):
        self.a = arena
        self.o = start

    def f32(self, shape):
        n = int(np.prod(shape))
        n_al = (n + 127) // 128 * 128
        pat, kw = _pat(shape)
        t = self.a.view(self.o, n, F32, pat, **kw)
        self.o += n_al
        return t

    def bf16(self, shape):
        n = int(np.prod(shape))
        w = (n + 1) // 2
        w_al = (w + 127) // 128 * 128
        pat, kw = _pat(shape)
        t = self.a.view(self.o, w, BF16, pat, **kw)
        self.o += w_al
        return t


def _pat(shape):
    if len(shape) == 1:
        return None, {}
    if len(shape) == 2:
        return "p (a b) -> p a b", {"a": shape[0]}
    if len(shape) == 3:
        return "p (a b c) -> p a b c", {"a": shape[0], "b": shape[1]}
    raise ValueError(shape)


def build_program(NSEG=SEQ // TS, NL=2, dbg=None):
    nc = bass.Bass("TRN2", target_bir_lowering=False)
    P = Prog(nc)
    L = 2

    class _Stop(Exception):
        pass

    dseg = 0
    if dbg is not None and "@" in dbg:
        dbg, dseg = dbg.split("@")
        dseg = int(dseg)
    st = {"dbg": dbg, "seg": 0}

    def ck(name, l=None):
        if st["seg"] != dseg and name not in ("setup", "kv"):
            return
        if dbg == name or (l is not None and dbg == f"{name}{l}"):
            if "xT" in st:
                xT_, outd_ = st["xT"], st["outd"]
                for c in range(8):
                    P.dma(View(outd_.h[c * 128:(c + 1) * 128, 0:TS], outd_.bufs), xT_[:, c, :], key="out", eng="sp")
            raise _Stop()
    try:
        _build_body(nc, P, L, NSEG, NL, ck, st)
    except _Stop:
        P.emit(final_wait_keys=[k for k in ("out",) if k in P.dma_keys])
    return nc, P


def _build_body(nc, P, L, NSEG, NL, ck, st):
    din = {}

    def di(name, shape):
        din[name] = P.dram(name, shape, F32, "ExternalInput")
        return din[name]
    xd = di("x", [SEQ, D])
    memd = di("mem", [NMEM, D])
    norm_mix = di("norm_mix", [L, D])
    w_in = di("w_in", [L, D, DIN])
    gm_ln_g = di("gm_ln_g", [L, 4, 128])
    gm_ln_b = di("gm_ln_b", [L, 4, 128])
    gm_ws = di("gm_ws", [L, 4, 128, 128])
    gm_bs = di("gm_bs", [L, 4, 128])
    dn_conv = di("dn_conv", [L, 4, 1536])
    dn_a_log = di("dn_a_log", [L, 4])
    dn_dt_bias = di("dn_dt_bias", [L, 4])
    dn_onorm = di("dn_onorm", [L, 128])
    w_out = di("w_out", [L, D, D])
    norm_xa = di("norm_xa", [L, D])
    norm_mem = di("norm_mem", [L, D])
    xa_wq = di("xa_wq", [L, D, D])
    xa_wk = di("xa_wk", [L, D, D])
    xa_wv = di("xa_wv", [L, D, D])
    xa_wo = di("xa_wo", [L, D, D])
    norm_ffn = di("norm_ffn", [L, D])
    ffn_w1 = di("ffn_w1", [L, D, DFF])
    ffn_w2 = di("ffn_w2", [L, DFF, D])
    norm_final = di("norm_final", [D])
    outd = P.dram("out", [SEQ, D], F32, "ExternalOutput")

    banks = [P.ps(f"bank{i}", [128, 512], F32) for i in range(8)]
    bstate = {"i": 0}

    def nb(exclude=()):
        while True:
            b = banks[bstate["i"] % 8]
            bstate["i"] += 1
            if b not in exclude:
                return b

    def b16(bank):
        return View(bank.h[:, :].bitcast(BF16), bank.bufs)

    ident_f = P.sb("ident_f", [128, 128], F32)
    ident_b = P.sb("ident_b", [128, 128], BF16)
    ones_b = P.sb("ones_b", [128, 128], BF16)
    ones_f = P.sb("ones_f", [128, 128], F32)
    U_f = P.sb("U_f", [128, 128], F32)
    bones_f = P.sb("bones_f", [128, 128], F32)
    csel_f = P.sb("csel_f", [128, 2, 128], F32)
    negm_b = P.sb("negm_b", [128, 4, 128], BF16)
    nsu_f = P.sb("nsu_f", [128, 128], F32)
    colsA = P.sb("colsA", [128, 128], F32)
    colsB = P.sb("colsB", [128, 96], F32)
    onorm_bc = P.sb("onorm_bc", [128, L, 128], F32)
    alog_bc = P.sb("alog_bc", [128, L, 4], F32)
    dtb_bc = P.sb("dtb_bc", [128, L, 4], F32)
    negA_bc = P.sb("negA_bc", [128, L, 4], F32)
    Bg = P.sb("Bg", [128, L, 512], F32)
    WsT = P.sb("WsT", [128, L, 512], BF16)
    wtail = P.sb("wtail", [128, L, 64], BF16)
    hist = [P.sb(f"hist{l}", [128, 12, 4], BF16) for l in range(L)]
    KT = [P.sb(f"KT{l}", [128, 8, 256], BF16) for l in range(L)]
    Vm = [P.sb(f"Vm{l}", [128, 2, 1024], BF16) for l in range(L)]
    diag = P.sb("diag", [128, 48, 128], BF16)
    Sst = [P.sb(f"S{l}", [128, 4, 128], F32) for l in range(L)]
    Rrs = [P.sb(f"Rr{i}", [128, 4, 128], F32R) for i in range(2)]
    residc = [P.sb(f"resid_r{c}", [128, 4, 128], F32R) for c in range(2)]
    xT = P.sb("xT", [128, 8, TS], F32)
    arena2 = Arena(P, "arena2", 3840)
    a2 = Alloc(arena2)
    hT = a2.bf16([8, TS])
    sq = [a2.bf16([TS]) for _ in range(2)]
    lnv = a2.f32([TS])
    rstd = a2.f32([TS])
    b2 = Alloc(arena2)
    PS1 = {k: b2.f32([4, 128]) for k in ("kdT_f", "qdT_f", "QKT_f", "kdec_f", "v_f")}
    st["xT"] = xT
    st["outd"] = outd
    ring = [P.sb(f"ring{i}", [128, 8, 512], BF16) for i in range(NRING)]
    YR = [P.sb(f"YR{i}", [128, 4, 2, 128], F32R) for i in range(2)]
    XX = [P.sb(f"XX{i}", [128, 4, 128], F32R) for i in range(2)]
    ARW = 21 * 1024
    arena = Arena(P, "arena", ARW)

    NSCR = 28 * NL
    wsc = nc.dram_tensor("wscratch", [NSCR, 128, 4096], BF16, kind="Internal").ap()
    wsc_bufs = [(Buf(f"wsc{i}"),) for i in range(NSCR)]

    class WStream:
        def __init__(self):
            self.blocks = []
            self.issued = 0
            self.taken = 0
            self.seen = set()

        def add(self, src, bid=None):
            first = bid not in self.seen
            if bid is not None:
                self.seen.add(bid)
            self.blocks.append((src, bid, first))

        def _issue(self, n):
            while self.issued < min(n, len(self.blocks)):
                i = self.issued
                s = i % NRING
                slot = ring[s]
                src, bid, first = self.blocks[i]
                flat = slot[:, :, :].rr("p a b -> p (a b)")
                if bid is None or first:
                    for q in range(4):
                        P.dma(slot[:, 2 * q:2 * q + 2, :], src[:, 2 * q:2 * q + 2, :], key=f"ring{s}", eng="pool")
                    if bid is not None and NSEG > 1:
                        P.dma(View(wsc[bid], wsc_bufs[bid]), flat, key=f"wb{s}", eng="sp")
                else:
                    P.dma(flat, View(wsc[bid], wsc_bufs[bid]), key=f"ring{s}", eng="sp")
                self.issued += 1

        def take(self):
            i = self.taken
            self._issue(i + NRING)
            self.taken += 1
            return ring[i % NRING]

    WS = WStream()

    def wblk(Wd, l, r0, c0):
        return View(Wd.h[l, r0:r0 + 1024, c0:c0 + 512].rearrange("(kc p) n -> p kc n", p=128), Wd.bufs)

    for l in range(NL):
        for c in range(2):
            WS.add(wblk(xa_wk, l, 0, 512 * c))
        for c in range(2):
            WS.add(wblk(xa_wv, l, 0, 512 * c))
    for seg in range(NSEG):
        for l in range(NL):
            bid = [28 * l]

            def nxt():
                bid[0] += 1
                return bid[0] - 1
            for c in range(6):
                WS.add(wblk(w_in, l, 0, 512 * c), nxt())
            for c in range(2):
                WS.add(wblk(w_out, l, 0, 512 * c), nxt())
            for c in range(2):
                WS.add(wblk(xa_wq, l, 0, 512 * c), nxt())
            for c in range(2):
                WS.add(wblk(xa_wo, l, 0, 512 * c), nxt())
            for c in range(8):
                WS.add(wblk(ffn_w1, l, 0, 512 * c), nxt())
            for cg in range(2):
                for ks in range(4):
                    WS.add(wblk(ffn_w2, l, 1024 * ks, 512 * cg), nxt())

    def iota_mask(t, fill_keep_cmp, fill, pattern_w=128, nrep=1):
        pass

    P.memset(ident_f[:, :], 0.0, eng="pool")
    P.op("pool", lambda e: e.affine_select(ident_f.h[:, :], ident_f.h[:, :], pattern=[[-1, 128]],
                                           compare_op=ALU.not_equal, fill=1.0, base=0, channel_multiplier=1),
         reads=[ident_f], writes=[ident_f])
    P.copy(ident_b[:, :], ident_f[:, :], eng="pool")

    P.memset(ones_b[:, :], 1.0, eng="pool")
    P.memset(ones_f[:, :], 1.0, eng="pool")
    P.memset(U_f[:, :], 1.0, eng="pool")
    P.op("pool", lambda e: e.affine_select(U_f.h[:, :], U_f.h[:, :], pattern=[[1, 128]],
                                           compare_op=ALU.is_ge, fill=0.0, base=0, channel_multiplier=-1),
         reads=[U_f], writes=[U_f])
    P.memset(negm_b[:, :, :], 0.0, eng="pool")
    P.op("pool", lambda e: e.affine_select(negm_b.h[:, :, :], negm_b.h[:, :, :], pattern=[[0, 4], [1, 128]],
                                           compare_op=ALU.is_ge, fill=-30000.0, base=0, channel_multiplier=-1),
         reads=[negm_b], writes=[negm_b])
    P.memset(nsu_f[:, :], -1.0, eng="pool")
    P.op("pool", lambda e: e.affine_select(nsu_f.h[:, :], nsu_f.h[:, :], pattern=[[1, 128]],
                                           compare_op=ALU.is_gt, fill=0.0, base=0, channel_multiplier=-1),
         reads=[nsu_f], writes=[nsu_f])

    ck("s1")
    P.memset(U_f[0:64, 64:128], 0.0, eng="pool")
    P.memset(negm_b[0:64, :, 64:128], -30000.0, eng="pool")
    P.memset(nsu_f[0:64, 64:128], 0.0, eng="pool")
    P.memset(bones_f[:, :], 1.0, eng="pool")
    P.memset(bones_f[0:64, 64:128], 0.0, eng="pool")
    P.memset(bones_f[64:128, 0:64], 0.0, eng="pool")
    P.memset(csel_f[:, :, :], 1.0, eng="pool")
    P.memset(csel_f[64:128, 0, :], 0.0, eng="pool")
    P.memset(csel_f[0:64, 1, :], 0.0, eng="pool")
    for c in range(2):
        for h in range(4):
            P.ts(residc[c][:, h, :], nsu_f[:, :], 0.0, None, ALU.mult)
    pa = Alloc(arena)
    stageA = pa.f32([128])
    stageB = pa.f32([128])
    ws_nat = pa.f32([L * 4, 128])
    gmem_bc = pa.f32([L, D])
    memst = pa.f32([2, D])
    mn_b = pa.bf16([2, D])
    mnT = pa.bf16([8, 256])
    junk = pa.f32([D])
    sm = pa.f32([64])
    bs_bc = pa.f32([L, 512])
    lngb_dummy = None

    P.memset(stageA[:, :], 0.0, eng="dve")
    P.memset(stageB[:, :], 0.0, eng="dve")

    ikey = {"n": 0}

    def ik():
        ikey["n"] += 1
        return f"init{ikey['n']}"

    def rows(dst, r0, n, src_ap):
        P.dma(dst[r0:r0 + n, :], View(src_ap, (Buf("dram"),)), key=ik(), eng="sp")
    for l in range(L):
        rows(stageA, 8 * l, 8, norm_mix.h[l].rearrange("(c p) -> c p", p=128))
        rows(stageA, 16 + 8 * l, 8, norm_xa.h[l].rearrange("(c p) -> c p", p=128))
        rows(stageA, 32 + 8 * l, 8, norm_ffn.h[l].rearrange("(c p) -> c p", p=128))
        rows(stageA, 56 + 4 * l, 4, gm_ln_g.h[l])
        rows(stageA, 64 + 4 * l, 4, gm_ln_b.h[l])
        for j in range(4):
            rows(stageB, 48 * l + 12 * j, 12, dn_conv.h[l, j].rearrange("(c p) -> c p", p=128))
    rows(stageA, 48, 8, norm_final.h.rearrange("(c p) -> c p", p=128))

    ck("s2")

    def bcast_load(dst_view, src_ap):
        P.dma(dst_view, View(src_ap.partition_broadcast(128), (Buf("dram"),)), key=ik(), eng="sp")
    for l in range(L):
        bcast_load(onorm_bc[:, l, :], dn_onorm.h[l])
        bcast_load(bs_bc[:, l, :], gm_bs.h[l].rearrange("g t -> (g t)"))
        bcast_load(alog_bc[:, l, :], dn_a_log.h[l])
        bcast_load(dtb_bc[:, l, :], dn_dt_bias.h[l])
        bcast_load(gmem_bc[:, l, :], norm_mem.h[l])
        P.dma(ws_nat[:, 4 * l:4 * l + 4, :], View(gm_ws.h[l].rearrange("g t s -> t g s"), (Buf("dram"),)), key=ik(), eng="sp")
        P.dma(wtail[:, l, :].rr("p (kc n) -> p kc n", kc=8),
              View(w_in.h[l, :, 3072:3080].rearrange("(kc p) n -> p kc n", p=128), (Buf("dram"),)), key=ik(), eng="pool")
    P.dma(memst[:, :, :], View(memd.h.rearrange("(mt p) d -> p mt d", p=128), (Buf("dram"),)), key=ik(), eng="sp")

    ck("s3")
    bk = nb()
    P.tr(bk[:, 0:128], stageA[:, :], ident_f[:, :])
    P.tr(bk[:, 128:224], stageB[0:96, :], ident_f[0:96, 0:96])
    P.copy(colsA[:, :], bk[:, 0:128], eng="dve")
    P.copy(colsB[:, :], bk[:, 128:224], eng="dve")

    def gcol(kind, l):
        base = {"mix": 0, "xa": 16, "ffn": 32}[kind] + 8 * l
        return colsA[:, base:base + 8]

    ck("s4")
    P.act(negA_bc[:, :, :], alog_bc[:, :, :], AF.Exp)
    P.ts(negA_bc[:, :, :], negA_bc[:, :, :], -1.0, None, ALU.mult)

    ck("s5")
    for l in range(L):
        for g in range(4):
            i = 4 * l + g
            P.op("pool", lambda e, i=i: e.affine_select(ws_nat.h[:, i, :], ws_nat.h[:, i, :], pattern=[[-1, 128]],
                                                        compare_op=ALU.is_ge, fill=0.0, base=0, channel_multiplier=1),
                 reads=[ws_nat], writes=[ws_nat])
        ck("s6")
        bk = nb()
        for g in range(4):
            P.tr(bk[:, g * 128:(g + 1) * 128], ws_nat[:, 4 * l + g, :], ident_f[:, :])
        ck("s6a")
        wsT_f = junk
        P.copy(wsT_f[:, 0:512], bk[:, :], eng="dve")
        ck("s6b")
        P.copy(WsT[:, l, :], wsT_f[:, 0:512], eng="act")
        ck("s7")
        bk2 = nb()
        P.mm(bk2[:, :], ones_f[:, :], wsT_f[:, 0:512], start=True, stop=True)
        for g in range(4):
            P.stt(Bg[:, l, g * 128:(g + 1) * 128], bk2[:, g * 128:(g + 1) * 128], colsA[:, 64 + 4 * l + g:65 + 4 * l + g],
                  bs_bc[:, l, g * 128:(g + 1) * 128], ALU.mult, ALU.add)

    ck("setup")
    for l in range(NL):
        for mt in range(2):
            P.act(junk[:, :], memst[:, mt, :], AF.Square, accum_out=sm[:, mt:mt + 1])
        P.act(sm[:, 2:4], sm[:, 0:2], AF.Ln, bias=EPS, scale=1.0 / D)
        P.act(sm[:, 4:6], sm[:, 2:4], AF.Exp, scale=-0.5)
        for mt in range(2):
            P.stt(mn_b[:, mt, :], memst[:, mt, :], sm[:, 4 + mt:5 + mt], gmem_bc[:, l, :], ALU.mult, ALU.mult)
        for mt in range(2):
            bk = nb()
            for kc in range(8):
                P.tr(b16(bk)[:, kc * 128:(kc + 1) * 128], mn_b[:, mt, kc * 128:(kc + 1) * 128], ident_b[:, :])
            P.copy(mnT[:, :, mt * 128:(mt + 1) * 128], b16(bk)[:, :].rr("p (a b) -> p a b", a=8), eng="dve")
        for c in range(2):
            W = WS.take()
            for cc in range(4):
                bk = nb()
                for kc in range(8):
                    P.mm(bk[:, 0:256], W[:, kc, cc * 128:(cc + 1) * 128], mnT[:, kc, :], start=(kc == 0), stop=(kc == 7))
                P.copy(KT[l][:, 4 * c + cc, :], bk[:, 0:256], eng="act")
        for c in range(2):
            W = WS.take()
            for mt in range(2):
                bk = nb()
                for kc in range(8):
                    P.mm(bk[:, :], mnT[:, kc, mt * 128:(mt + 1) * 128], W[:, kc, :], start=(kc == 0), stop=(kc == 7))
                P.copy(Vm[l][:, mt, 512 * c:512 * (c + 1)], bk[:, :], eng="dve")

    ck("kv")
    for l in range(L):
        P.memset(Sst[l][:, :, :], 0.0, eng="dve")
        P.memset(hist[l][:, :, :], 0.0, eng="dve")

    ma = Alloc(arena)
    uT = ma.bf16([4, TS])
    gv = [ma.f32([4, 128]) for _ in range(NT)]
    pcT = ma.bf16([12, TS + 4])
    sg = [ma.bf16([512]) for _ in range(NT)]
    bl = ma.f32([NT, 8])
    yT = ma.bf16([8, TS])
    tok = ma.f32([16, 16])
    vn_b = ma.bf16([512])
    tmpA = ma.f32([512])
    tmpB = ma.f32([512])
    stat = ma.f32([64])
    statA = [ma.f32([32]) for _ in range(2)]
    statC = ma.f32([16])
    tmpC = ma.f32([512])
    tmpD = ma.f32([512])
    qk_f = ma.f32([2, 512])
    v_f = ma.f32([4, 128])
    qn_b = ma.bf16([4, 128])
    qd_f = ma.f32([4, 128])
    kn_b = ma.bf16([4, 128])
    kd_f = ma.f32([4, 128])
    kdT_f = ma.f32([4, 128])
    kdec_f = ma.f32([4, 128])
    kT_b = ma.bf16([4, 128])
    qnT_b = ma.bf16([4, 128])
    qdT_f = ma.f32([4, 128])
    ET = ma.f32([4, 128])
    GE = ma.f32([4, 128])
    QKT_f = ma.f32([4, 128])
    vnewc = [ma.f32([4, 128]) for _ in range(2)]
    o_f = ma.f32([4, 128])
    gsc = ma.f32([4, 128])
    yb_b = ma.bf16([4, 128])
    assert ma.o <= ARW, ma.o
    PS = [{"kdT_f": kdT_f, "qdT_f": qdT_f, "QKT_f": QKT_f, "kdec_f": kdec_f, "v_f": v_f, "Rr": Rrs[0]},
          dict(PS1, Rr=Rrs[1])]
    xa = Alloc(arena)
    qxT = xa.bf16([8, TS])
    oxT = xa.bf16([8, TS])
    expT = [xa.bf16([2, TS]) for _ in range(2)]
    rinv = [xa.f32([TS]) for _ in range(2)]
    fa = Alloc(arena)
    hidT = fa.bf16([32, TS])
    rl = [fa.bf16([TS]) for _ in range(2)]
    xst = fa.f32([NT, D])
    ost = fa.f32([NT, D])
    outT = ost
    assert fa.o <= ARW, fa.o

    K_BETA, K_NBETA, K_G, K_D, K_NEGD, K_ED, K_KDS, K_CD0, K_TMP, K_TMP2, K_DL, K_CD1 = range(12)

    def rmsnorm_to_hT(gains, final_out=None):
        bk = nb()
        for c in range(8):
            s = sq[c % 2]
            P.act(s[:, :], xT[:, c, :], AF.Square)
            P.mm(bk[:, :], ones_b[:, :], s[:, :], start=(c == 0), stop=(c == 7))
        P.act(lnv[:, :], bk[:, :], AF.Ln, bias=EPS, scale=1.0 / D)
        P.act(rstd[:, :], lnv[:, :], AF.Exp, scale=-0.5)
        for c in range(8):
            dst = hT[:, c, :] if final_out is None else final_out[:, c, :]
            P.stt(dst, xT[:, c, :], gains[:, c:c + 1], rstd[:, :], ALU.mult, ALU.mult)

    def proj_fm(W, rhsT, nk, evac):
        for cc in range(4):
            bk = nb()
            for kc in range(nk):
                P.mm(bk[:, :], W[:, kc, cc * 128:(cc + 1) * 128], rhsT[:, kc, :], start=(kc == 0), stop=(kc == nk - 1))
            evac(cc, bk)

    def proj_tm(W, lhsT_all, evac):
        for tt in range(NT):
            bk = nb()
            for kc in range(8):
                P.mm(bk[:, :], lhsT_all[:, kc, tt * 128:(tt + 1) * 128], W[:, kc, :], start=(kc == 0), stop=(kc == 7))
            evac(tt, bk)

    def resid_add(c, bk):
        P.tt(xT[:, c, :], xT[:, c, :], bk[:, :], ALU.add)

    first_x = {"done": False}

    def load_x(seg):
        P.dma(xst[:, :, :], View(xd.h[seg * TS:(seg + 1) * TS, :].rearrange("(tt p) d -> p tt d", p=128), (Buf("dram"),)),
              key="xin", eng="sp")

    for seg in range(NSEG):
        st["seg"] = seg
        if seg == 0:
            load_x(0)
        for c in range(8):
            bk = nb()
            for tt in range(NT):
                P.tr(bk[:, tt * 128:(tt + 1) * 128], xst[:, tt, c * 128:(c + 1) * 128], ident_f[:, :])
            P.copy(xT[:, c, :], bk[:, :], eng=("act" if c % 2 else "dve"))

        ck("xT")
        for l in range(NL):
            for j in range(4):
                for c in range(12):
                    r = 48 * l + 12 * j + c
                    P.ts(diag[:, 12 * j + c, :], ident_f[:, :], colsB[:, r:r + 1], None, ALU.mult,
                         eng=("dve" if (c % 2) else "pool"))
            P.memset(vnewc[0][64:128, :, :], 0.0, eng="pool")
            P.memset(vnewc[1][0:64, :, :], 0.0, eng="pool")
            P.copy(pcT[:, :, 0:4], hist[l][:, :, :], eng="dve")
            rmsnorm_to_hT(gcol("mix", l))
            ck("norm1")
            W = WS.take()
            proj_fm(W, hT, 8, lambda cc, bk: P.act(uT[:, cc, :], bk[:, :], AF.Gelu_apprx_tanh))
            W = WS.take()
            proj_tm(W, hT, lambda tt, bk: P.act(gv[tt][:, :, :], bk[:, :].rr("p (a b) -> p a b", a=4), AF.Gelu_apprx_tanh))
            for cg in range(3):
                W = WS.take()
                proj_fm(W, hT, 8, lambda cc, bk, cg=cg: P.copy(pcT[:, 4 * cg + cc, 4:4 + TS], bk[:, :],
                                                              eng=("act" if cc % 2 else "dve")))
            P.copy(hist[l][:, :, :], pcT[:, :, TS:TS + 4], eng="dve")
            W = WS.take()
            proj_tm(W, hT, lambda tt, bk: P.act(sg[tt][:, :], bk[:, :], AF.Silu))
            bk = nb()
            for tt in range(NT):
                for kc in range(8):
                    P.mm(bk[:, tt * 8:(tt + 1) * 8], hT[:, kc, tt * 128:(tt + 1) * 128], wtail[:, l, kc * 8:(kc + 1) * 8],
                         start=(kc == 0), stop=(kc == 7))
            P.copy(bl[:, :, :], bk[:, 0:NT * 8].rr("p (a b) -> p a b", a=NT), eng="dve")

            ck("inproj")
            def tk(kind):
                return tok[:, kind, :]

            def tk3(kind):
                return tok[:, kind, :].rr("p (a b) -> p a b", a=NT)
            P.act(tk3(K_TMP), bl[:, :, 0:4], AF.Tanh, scale=0.5)
            P.ts(tk(K_BETA), tk(K_TMP), 0.5, 0.5, ALU.mult, ALU.add)
            P.ts(tk(K_NBETA), tk(K_BETA), -1.0, None, ALU.mult)
            P.tt(tk3(K_TMP), bl[:, :, 4:8], dtb_bc[:, l:l + 1, :].bc([128, NT, 4]), ALU.add)
            P.act(tk(K_TMP2), tk(K_TMP), AF.Exp)
            P.act(tk(K_TMP), tk(K_TMP2), AF.Ln, bias=1.0)
            P.tt(tk3(K_G), tk3(K_TMP), negA_bc[:, l:l + 1, :].bc([128, NT, 4]), ALU.mult)
            bk = nb()
            P.mm(bk[:, 0:16], U_f[:, :], tk(K_G), start=True, stop=True)
            P.mm(bk[:, 16:32], bones_f[:, :], tk(K_G), start=True, stop=True)
            P.mm(bk[:, 32:48], csel_f[:, 0, :], tk(K_G), start=True, stop=True)
            P.mm(bk[:, 48:64], csel_f[:, 1, :], tk(K_G), start=True, stop=True)
            P.copy(tk(K_D), bk[:, 0:16], eng="dve")
            P.copy(tk(K_DL), bk[:, 16:32], eng="dve")
            P.act(tk(K_CD0), bk[:, 32:48], AF.Exp)
            P.act(tk(K_CD1), bk[:, 48:64], AF.Exp)
            P.ts(tk(K_NEGD), tk(K_D), -1.0, None, ALU.mult)
            P.act(tk(K_ED), tk(K_D), AF.Exp)
            P.tt(tk(K_TMP), tk(K_DL), tk(K_D), ALU.subtract)
            P.act(tk(K_KDS), tk(K_TMP), AF.Exp)

            ck("tok")
            S = Sst[l]

            def bc4(v):
                return v.rr("p (a b) -> p a b", b=1).bc([128, 4, 128])

            def f4(t):
                return t[:, :, :].rr("p a b -> p (a b)")

            def gen_A(tt):
                tsl = slice(tt * 128, (tt + 1) * 128)
                g3 = gv[tt]
                stA = statA[tt % 2]
                P.reduce(stA[:, 0:4], g3[:, :, :], ALU.add)
                P.act(tmpA[:, :], g3[:, :, :].rr("p a b -> p (a b)"), AF.Square)
                yield
                P.reduce(stA[:, 4:8], tmpA[:, :].rr("p (a b) -> p a b", a=4), ALU.add)
                P.ts(stA[:, 8:12], stA[:, 0:4], 1.0 / 128, None, ALU.mult)
                P.tt(stA[:, 12:16], stA[:, 8:12], stA[:, 8:12], ALU.mult)
                P.stt(stA[:, 16:20], stA[:, 4:8], 1.0 / 128, stA[:, 12:16], ALU.mult, ALU.subtract)
                yield
                P.act(stA[:, 20:24], stA[:, 16:20], AF.Ln, bias=EPS)
                P.act(stA[:, 24:28], stA[:, 20:24], AF.Exp, scale=-0.5)
                yield
                P.tt(tmpA[:, :].rr("p (a b) -> p a b", a=4), g3[:, :, :],
                     stA[:, 8:12].rr("p (a b) -> p a b", b=1).bc([128, 4, 128]), ALU.subtract)
                P.tt(vn_b[:, :].rr("p (a b) -> p a b", a=4), tmpA[:, :].rr("p (a b) -> p a b", a=4),
                     stA[:, 24:28].rr("p (a b) -> p a b", b=1).bc([128, 4, 128]), ALU.mult)
                yield
                bkA = nb()
                for g in range(4):
                    P.mm(bkA[:, g * 128:(g + 1) * 128], vn_b[:, g * 128:(g + 1) * 128], WsT[:, l, g * 128:(g + 1) * 128],
                         start=True, stop=True)
                yield
                P.tt(tmpA[:, :].rr("p (a b) -> p a b", a=4), bkA[:, :].rr("p (a b) -> p a b", a=4),
                     colsA[:, 56 + 4 * l:60 + 4 * l].rr("p (a b) -> p a b", b=1).bc([128, 4, 128]), ALU.mult)
                P.tt(tmpA[:, :], tmpA[:, :], Bg[:, l, :], ALU.add, eng="pool")
                P.tt(yT[:, 0:4, tsl], tmpA[:, :].rr("p (a b) -> p a b", a=4), uT[:, :, tsl], ALU.mult, eng="pool")

            def gen_pre(tt, ps):
                hs = slice(tt * 4, tt * 4 + 4)
                kdT_f, qdT_f, QKT_f, kdec_f, v_f, Rr = ps["kdT_f"], ps["qdT_f"], ps["QKT_f"], ps["kdec_f"], ps["v_f"], ps["Rr"]
                bq, bkk, bv = nb(), nb(), nb()
                for cg, bkc in enumerate((bq, bkk, bv)):
                    for cc in range(4):
                        c = 4 * cg + cc
                        for j in range(4):
                            c0 = tt * 128 + j + 1
                            P.mm(bkc[:, cc * 128:(cc + 1) * 128], pcT[:, c, c0:c0 + 128], diag[:, 12 * j + c, :],
                                 start=(j == 0), stop=(j == 3))
                    yield
                P.act(qk_f[:, 0, :], bq[:, :], AF.Silu)
                P.act(qk_f[:, 1, :], bkk[:, :], AF.Silu)
                P.act(f4(v_f), bv[:, :], AF.Silu)
                yield
                P.act(tmpB[:, :], qk_f[:, 0, :], AF.Square)
                P.act(tmpC[:, :], qk_f[:, 1, :], AF.Square)
                yield
                P.reduce(stat[:, 32:36], tmpB[:, :].rr("p (a b) -> p a b", a=4), ALU.add)
                P.reduce(stat[:, 36:40], tmpC[:, :].rr("p (a b) -> p a b", a=4), ALU.add)
                yield
                P.act(stat[:, 40:48], stat[:, 32:40], AF.Ln, bias=EPS)
                P.act(stat[:, 48:56], stat[:, 40:48], AF.Exp, scale=-0.5)
                yield
                P.ts(stat[:, 48:52], stat[:, 48:52], 128.0 ** -0.5, None, ALU.mult)
                P.tt(stat[:, 56:60], stat[:, 48:52], tok[:, K_ED, hs], ALU.mult)
                P.tt(stat[:, 60:64], stat[:, 52:56], tok[:, K_ED, hs], ALU.mult)
                P.tt(stat[:, 28:32], stat[:, 52:56], tok[:, K_KDS, hs], ALU.mult)
                yield
                q3 = qk_f[:, 0, :].rr("p (a b) -> p a b", a=4)
                k3 = qk_f[:, 1, :].rr("p (a b) -> p a b", a=4)
                P.tt(kn_b[:, :, :], k3, bc4(stat[:, 52:56]), ALU.mult)
                P.tt(qn_b[:, :, :], q3, bc4(stat[:, 48:52]), ALU.mult)
                P.tt(qd_f[:, :, :], q3, bc4(stat[:, 56:60]), ALU.mult, eng="pool")
                P.tt(kd_f[:, :, :], k3, bc4(stat[:, 60:64]), ALU.mult, eng="pool")
                P.tt(kdec_f[:, :, :], k3, bc4(stat[:, 28:32]), ALU.mult, eng="pool")
                yield
                bkT = nb()
                for h in range(4):
                    P.tr(b16(bkT)[:, h * 128:(h + 1) * 128], kn_b[:, h, :], ident_b[:, :])
                    P.tr(b16(bkT)[:, 512 + h * 128:512 + (h + 1) * 128], qn_b[:, h, :], ident_b[:, :])
                bkD = nb()
                P.mm(bkD[:, :], ident_b[:, :], negm_b[:, :, :].rr("p a b -> p (a b)"), start=True, stop=False)
                for h in range(4):
                    gcolv = tok[:, K_G, tt * 4 + h:tt * 4 + h + 1]
                    P.mm(bkD[:, h * 128:(h + 1) * 128], gcolv.bc([128, 128]), U_f[:, :], start=False, stop=True)
                yield
                P.copy(f4(kT_b), b16(bkT)[:, 0:512], eng="dve")
                P.copy(f4(qnT_b), b16(bkT)[:, 512:1024], eng="dve")
                for h in range(4):
                    P.act(ET[:, h, :], bkD[:, h * 128:(h + 1) * 128], AF.Exp, bias=tok[:, K_NEGD, tt * 4 + h:tt * 4 + h + 1])
                yield
                bkT2 = nb()
                for h in range(4):
                    P.tr(bkT2[:, h * 128:(h + 1) * 128], qd_f[:, h, :], ident_f[:, :])
                bkW = nb()
                for h in range(4):
                    P.tr(bkW[:, h * 128:(h + 1) * 128], kd_f[:, h, :], ident_f[:, :])
                bkG = nb()
                bkKQ = nb()
                for h in range(4):
                    P.mm(bkG[:, h * 128:(h + 1) * 128], kT_b[:, h, :], kT_b[:, h, :], start=True, stop=True)
                for h in range(4):
                    P.mm(bkKQ[:, h * 128:(h + 1) * 128], kT_b[:, h, :], qnT_b[:, h, :], start=True, stop=True)
                yield
                P.copy(f4(qdT_f), bkT2[:, :], eng="act")
                P.copy(f4(kdT_f), bkW[:, :], eng="act")
                P.tt(f4(GE), bkG[:, :], f4(ET), ALU.mult)
                P.tt(f4(QKT_f), bkKQ[:, :], f4(ET), ALU.mult)
                yield
                P.tt(GE[:, :, :], GE[:, :, :], nsu_f[:, :].rr("p (a b) -> p a b", a=1).bc([128, 4, 128]), ALU.mult, eng="pool")
                cur = 0
                P.tt(YR[cur][:, :, 0, :], GE[:, :, :], bc4(tok[:, K_BETA, hs]), ALU.mult)
                P.copy(YR[cur][:, :, 1, :], ident_f[:, :].rr("p (a b) -> p a b", a=1).bc([128, 4, 128]), eng="dve")
                yield
                bkX = nb()
                for h in range(4):
                    P.tr(bkX[:, h * 128:(h + 1) * 128], YR[cur][:, h, 0, :].bitcast(F32), ident_f[:, :])
                yield
                P.copy(f4(XX[cur]), bkX[:, :], eng="act")
                yield
                NLEV = 6
                for k in range(NLEV):
                    nxt = 1 - cur
                    last = (k == NLEV - 1)
                    bA0, bA1 = nb(), nb()
                    bAs = (bA0, bA0, bA1, bA1)
                    if not last:
                        for h in range(4):
                            P.mm(bAs[h][:, (h % 2) * 256:(h % 2) * 256 + 256], XX[cur][:, h, :],
                                 YR[cur][:, h, :, :].rr("p a b -> p (a b)"), start=True, stop=True)
                        bB = nb()
                        for h in range(4):
                            P.mm(bB[:, h * 128:(h + 1) * 128], YR[cur][:, h, 0, :],
                                 XX[cur][:, h, :], start=True, stop=True)
                        yield
                        for hp in range(2):
                            src = (bA0, bA1)[hp][:, :].rr("p (a b c) -> p a b c", a=2, b=2)
                            P.copy(YR[nxt][:, 2 * hp:2 * hp + 2, 0, :], src[:, :, 0, :], eng="act")
                            P.tt(YR[nxt][:, 2 * hp:2 * hp + 2, 1, :], src[:, :, 1, :],
                                 YR[cur][:, 2 * hp:2 * hp + 2, 1, :].bitcast(F32), ALU.add)
                        P.copy(f4(XX[nxt]), bB[:, :], eng="act")
                        cur = nxt
                        yield
                    else:
                        for h in range(4):
                            P.mm(bA0[:, h * 128:(h + 1) * 128], XX[cur][:, h, :],
                                 YR[cur][:, h, 1, :], start=True, stop=True)
                        yield
                        P.tt(Rr[:, :, :], bA0[:, :].rr("p (a b) -> p a b", a=4), YR[cur][:, :, 1, :].bitcast(F32), ALU.add)

            def gen_chain(tt, ps):
                tsl = slice(tt * 128, (tt + 1) * 128)
                hs = slice(tt * 4, tt * 4 + 4)
                kdT_f, qdT_f, QKT_f, kdec_f, v_f, Rr = ps["kdT_f"], ps["qdT_f"], ps["QKT_f"], ps["kdec_f"], ps["v_f"], ps["Rr"]
                P.tt(gsc[:, :, :], sg[tt][:, :].rr("p (a b) -> p a b", a=4), onorm_bc[:, l:l + 1, :].bc([128, 4, 128]),
                     ALU.mult, eng="pool")
                for c in range(2):
                    rs = slice(64 * c, 64 * c + 64)
                    res_c, vn_c = residc[c], vnewc[c]
                    bkV1 = nb()
                    for h in range(4):
                        P.mm(bkV1[:, h * 128:(h + 1) * 128], kdT_f[:, h, :], S[:, h, :], start=True, stop=True)
                    yield
                    P.tt(res_c[rs, :, :].rr("p a b -> p (a b)"), v_f[rs, :, :].rr("p a b -> p (a b)"), bkV1[rs, :], ALU.subtract)
                    yield
                    bkV = nb()
                    for h in range(4):
                        P.mm(bkV[:, h * 128:(h + 1) * 128], Rr[:, h, :], res_c[:, h, :], start=True, stop=True)
                    yield
                    beta_bc = View(tok.h[rs, K_BETA, hs].rearrange("p (a b) -> p a b", b=1).to_broadcast([64, 4, 128]), tok.bufs)
                    P.tt(vn_c[rs, :, :], bkV[rs, :].rr("p (a b) -> p a b", a=4), beta_bc, ALU.mult)
                    yield
                    bkS = nb()
                    for h in range(4):
                        P.mm(bkS[:, h * 128:(h + 1) * 128], kdec_f[:, h, :], vn_c[:, h, :], start=True, stop=True)
                    bkO = nb()
                    for h in range(4):
                        P.mm(bkO[:, h * 128:(h + 1) * 128], qdT_f[:, h, :], S[:, h, :], start=True, stop=False)
                        P.mm(bkO[:, h * 128:(h + 1) * 128], QKT_f[:, h, :], vn_c[:, h, :], start=False, stop=True)
                    yield
                    P.copy(o_f[rs, :, :].rr("p a b -> p (a b)"), bkO[rs, :], eng="act")
                    kcd = K_CD0 if c == 0 else K_CD1
                    for h in range(4):
                        P.stt(S[:, h, :], S[:, h, :], tok[:, kcd, tt * 4 + h:tt * 4 + h + 1], bkS[:, h * 128:(h + 1) * 128],
                              ALU.mult, ALU.add)
                    yield
                P.act(tmpD[:, :], f4(o_f), AF.Square)
                yield
                P.reduce(statC[:, 0:4], tmpD[:, :].rr("p (a b) -> p a b", a=4), ALU.add)
                yield
                P.act(statC[:, 4:8], statC[:, 0:4], AF.Ln, bias=EPS, scale=1.0 / 128)
                P.act(statC[:, 8:12], statC[:, 4:8], AF.Exp, scale=-0.5)
                yield
                on3 = tmpD[:, :].rr("p (a b) -> p a b", a=4)
                P.tt(on3, o_f[:, :, :], bc4(statC[:, 8:12]), ALU.mult)
                P.tt(yb_b[:, :, :], on3, gsc[:, :, :], ALU.mult, eng="pool")
                yield
                bkY = nb()
                for h in range(4):
                    P.tr(b16(bkY)[:, h * 128:(h + 1) * 128], yb_b[:, h, :], ident_b[:, :])
                yield
                P.copy(yT[:, 4:8, tsl], b16(bkY)[:, 0:512].rr("p (a b) -> p a b", a=4), eng="act")

            def run_rr(gens):
                gens = list(gens)
                while gens:
                    for g in list(gens):
                        try:
                            next(g)
                        except StopIteration:
                            gens.remove(g)

            def seq(*gs):
                for g in gs:
                    yield from g
            run_rr([gen_pre(0, PS[0]), seq(gen_A(0), gen_A(1))])
            for tt in range(NT):
                gs = [gen_chain(tt, PS[tt % 2])]
                if tt + 1 < NT:
                    gs.append(gen_pre(tt + 1, PS[(tt + 1) % 2]))
                if tt + 2 < NT:
                    gs.append(gen_A(tt + 2))
                run_rr(gs)

            st["yT"] = yT
            if st["dbg"] == f"ygdn{l}":
                for c in range(8):
                    P.dma(View(outd.h[c * 128:(c + 1) * 128, 0:TS], outd.bufs), yT[:, c, :], key="out", eng="pool")
                del st["xT"]
                ck(f"ygdn{l}")
            ck("gdn", l)
            for c in range(2):
                W = WS.take()
                proj_fm(W, yT, 8, lambda cc, bk, c=c: resid_add(4 * c + cc, bk))

            ck("wout", l)
            rmsnorm_to_hT(gcol("xa", l))
            for c in range(2):
                W = WS.take()
                proj_fm(W, hT, 8, lambda cc, bk, c=c: P.copy(qxT[:, 4 * c + cc, :], bk[:, :], eng=("act" if cc % 2 else "dve")))
            for h in range(4):
                e = expT[h % 2]
                for mc in range(2):
                    bk = nb()
                    for dc in range(2):
                        P.mm(bk[:, :], KT[l][:, 2 * h + dc, mc * 128:(mc + 1) * 128], qxT[:, 2 * h + dc, :],
                             start=(dc == 0), stop=(dc == 1))
                    P.act(e[:, mc, :], bk[:, :], AF.Exp, scale=1.0 / 16)
                bk = nb()
                for mc in range(2):
                    P.mm(bk[:, :], ones_b[:, :], e[:, mc, :], start=(mc == 0), stop=(mc == 1))
                ri = rinv[h % 2]
                P.recip(ri[:, :], bk[:, :])
                for dc in range(2):
                    bk = nb()
                    for mc in range(2):
                        P.mm(bk[:, :], Vm[l][:, mc, (2 * h + dc) * 128:(2 * h + dc + 1) * 128], e[:, mc, :],
                             start=(mc == 0), stop=(mc == 1))
                    P.tt(oxT[:, 2 * h + dc, :], bk[:, :], ri[:, :], ALU.mult)
            for c in range(2):
                W = WS.take()
                proj_fm(W, oxT, 8, lambda cc, bk, c=c: resid_add(4 * c + cc, bk))

            ck("xattn", l)
            rmsnorm_to_hT(gcol("ffn", l))
            for c in range(8):
                W = WS.take()

                def ev(cc, bk, c=c):
                    r = rl[cc % 2]
                    P.act(r[:, :], bk[:, :], AF.Relu)
                    P.tt(hidT[:, 4 * c + cc, :], r[:, :], r[:, :], ALU.mult, eng=("dve" if cc % 2 else "pool"))
                proj_fm(W, hT, 8, ev)
            if l == NL - 1 and seg + 1 < NSEG:
                load_x(seg + 1)
            for cg in range(2):
                accs = [nb() for _ in range(4)]
                for ks in range(4):
                    W = WS.take()
                    for cc in range(4):
                        for kc in range(8):
                            P.mm(accs[cc][:, :], W[:, kc, cc * 128:(cc + 1) * 128], hidT[:, ks * 8 + kc, :],
                                 start=(ks == 0 and kc == 0), stop=(ks == 3 and kc == 7))
                for cc in range(4):
                    resid_add(4 * cg + cc, accs[cc])
            ck("ffn", l)

        fin = hidT_f32 = None
        fo = Alloc(arena)
        finT = fo.f32([8, TS])
        rmsnorm_to_hT(colsA[:, 48:56], final_out=finT)
        for tt in range(NT):
            for half in range(2):
                bk = nb()
                for j in range(4):
                    c = 4 * half + j
                    P.tr(bk[:, j * 128:(j + 1) * 128], finT[:, c, tt * 128:(tt + 1) * 128], ident_f[:, :])
                P.copy(ost[:, tt, half * 512:(half + 1) * 512], bk[:, :], eng=("act" if half else "dve"))
        P.dma(View(outd.h[seg * TS:(seg + 1) * TS, :].rearrange("(tt p) d -> p tt d", p=128), outd.bufs), ost[:, :, :],
              key="out", eng="sp")

    P.emit(final_wait_keys=["out"])


_CACHE = {}


def kernel(**inputs):
    names = ["x", "mem", "norm_mix", "w_in", "gm_ln_g", "gm_ln_b", "gm_ws", "gm_bs", "dn_conv", "dn_a_log",
             "dn_dt_bias", "dn_onorm", "w_out", "norm_xa", "norm_mem", "xa_wq", "xa_wk", "xa_wv", "xa_wo",
             "norm_ffn", "ffn_w1", "ffn_w2", "norm_final"]
    arrs = {k: np.ascontiguousarray(np.asarray(inputs[k], dtype=np.float32)) for k in names}
    nc, _ = build_program()
    in_maps = []
    for b in range(8):
        m = {k: arrs[k] for k in names if k not in ("x", "mem")}
        m["x"] = np.ascontiguousarray(arrs["x"][b])
        m["mem"] = np.ascontiguousarray(arrs["mem"][b])
        in_maps.append(m)
    res = run_bass_kernel_spmd(nc, in_maps, core_ids=list(range(8)))
    out = np.stack([np.asarray(r["out"], dtype=np.float32) for r in res.results], axis=0)
    return out
```

```python
from concourse.bass_utils import run_bass_kernel_spmd

import numpy as np
import concourse.bass as bass
import concourse.mybir as mybir

F32 = mybir.dt.float32
BF16 = mybir.dt.bfloat16
F32R = mybir.dt.float32r
AF = mybir.ActivationFunctionType
ALU = mybir.AluOpType
AX = mybir.AxisListType


class Buf:
    __slots__ = ("name", "w", "rs", "xr")

    def __init__(self, name):
        self.name = name
        self.w = None
        self.rs = []
        self.xr = False


class View:
    __slots__ = ("ap", "bufs")

    def __init__(self, ap, bufs):
        self.ap = ap
        self.bufs = bufs

    def __getitem__(self, idx):
        return View(self.ap[idx], self.bufs)

    def bitcast(self, dt):
        return View(self.ap.bitcast(dt), self.bufs)

    def bc(self, shape):
        return View(self.ap.to_broadcast(shape), self.bufs)

    def rr(self, s, **kw):
        return View(self.ap.rearrange(s, **kw), self.bufs)


class T:
    def __init__(self, h, name, bufs=None):
        self.h = h
        self.bufs = bufs if bufs is not None else (Buf(name),)
        self.name = name

    def __getitem__(self, idx):
        return View(self.h[idx], self.bufs)

    def sub(self, name):
        return T(self.h, name)


class Op:
    __slots__ = ("eng", "fn", "deps", "idx", "signal", "ms", "dma_key", "dma_cnt", "name")


ENGS = ("pe", "act", "dve", "pool", "sp")
WINDOW = 16000


class Prog:
    def __init__(self, nc):
        self.nc = nc
        self.ops = {e: [] for e in ENGS}
        self.dma_keys = {}
        self.nsb = 0

    def sb(self, name, shape, dt=F32):
        h = self.nc.alloc_sbuf_tensor(name, list(shape), dt)
        return T(h, name)

    def ps(self, name, shape, dt=F32):
        h = self.nc.alloc_psum_tensor(name, list(shape), dt)
        t = T(h, name)
        t.bufs[0].xr = True
        return t

    def dram(self, name, shape, dt, kind):
        h = self.nc.dram_tensor(name, list(shape), dt, kind=kind)
        return T(h.ap(), name)

    def op(self, eng, fn, reads=(), writes=(), dma_key=None, name=None):
        o = Op()
        o.eng = eng
        o.fn = fn
        o.signal = False
        o.ms = None
        o.dma_key = dma_key
        o.dma_cnt = None
        o.name = name
        deps = {}
        rb = []
        for r in reads:
            for b in r.bufs:
                if b not in rb:
                    rb.append(b)
        wb = []
        for w in writes:
            for b in w.bufs:
                if b not in wb:
                    wb.append(b)
        for b in rb:
            if b.w is not None:
                deps[id(b.w)] = b.w
            if b.xr:
                for r in b.rs:
                    if r.eng != eng:
                        deps[id(r)] = r
        for b in wb:
            if b.w is not None:
                deps[id(b.w)] = b.w
            for r in b.rs:
                deps[id(r)] = r
        dl = []
        for d in deps.values():
            if d is o:
                continue
            if d.dma_key is None and d.eng == "pe" and eng == "pe" and dma_key is None:
                continue
            dl.append(d)
        o.deps = dl
        for d in dl:
            d.signal = True
        if dma_key is not None:
            c = self.dma_keys.get(dma_key, 0) + 1
            self.dma_keys[dma_key] = c
            o.dma_cnt = c
        o.idx = len(self.ops[eng])
        self.ops[eng].append(o)
        for b in rb:
            if b not in wb:
                b.rs.append(o)
        for b in wb:
            b.w = o
            b.rs = []
        return o

    def mm(self, out, lhsT, rhs, start=True, stop=True, **kw):
        rd = [lhsT, rhs] + ([] if start else [out])
        return self.op("pe", lambda e: e.matmul(out.ap, lhsT.ap, rhs.ap, start=start, stop=stop, **kw),
                       reads=rd, writes=[out])

    def tr(self, out, in_, ident):
        return self.op("pe", lambda e: e.transpose(out.ap, in_.ap, ident.ap), reads=[in_, ident], writes=[out])

    def act(self, out, in_, func, bias=None, scale=None, accum_out=None, eng="act"):
        rd = [in_]
        kw = {}
        if bias is not None:
            if isinstance(bias, View):
                rd.append(bias)
                kw["bias"] = bias.ap
            else:
                kw["bias"] = bias
        if scale is not None:
            if isinstance(scale, View):
                rd.append(scale)
                kw["scale"] = scale.ap
            else:
                kw["scale"] = scale
        wr = [out]
        if accum_out is not None:
            wr.append(accum_out)
            kw["accum_out"] = accum_out.ap
        return self.op(eng, lambda e: e.activation(out.ap, in_.ap, func, **kw), reads=rd, writes=wr)

    def tt(self, out, a, b, op, eng="dve"):
        return self.op(eng, lambda e: e.tensor_tensor(out.ap, a.ap, b.ap, op), reads=[a, b], writes=[out])

    def ts(self, out, a, s1, s2, op0, op1=None, eng="dve", accum_out=None):
        rd = [a]
        s1v = s1.ap if isinstance(s1, View) else s1
        s2v = s2.ap if isinstance(s2, View) else s2
        if isinstance(s1, View):
            rd.append(s1)
        if isinstance(s2, View):
            rd.append(s2)
        wr = [out]
        kw = {}
        if accum_out is not None:
            wr.append(accum_out)
            kw["accum_out"] = accum_out.ap
        if op1 is None:
            return self.op(eng, lambda e: e.tensor_scalar(out.ap, a.ap, s1v, None, op0, **kw), reads=rd, writes=wr)
        return self.op(eng, lambda e: e.tensor_scalar(out.ap, a.ap, s1v, s2v, op0, op1, **kw), reads=rd, writes=wr)

    def stt(self, out, a, s, b, op0, op1, eng="dve"):
        rd = [a, b]
        sv = s.ap if isinstance(s, View) else s
        if isinstance(s, View):
            rd.append(s)
        return self.op(eng, lambda e: e.scalar_tensor_tensor(out.ap, a.ap, sv, b.ap, op0, op1), reads=rd, writes=[out])

    def copy(self, out, in_, eng="dve"):
        if eng == "act":
            return self.op(eng, lambda e: e.activation(out.ap, in_.ap, AF.Identity), reads=[in_], writes=[out])
        return self.op(eng, lambda e: e.tensor_copy(out.ap, in_.ap), reads=[in_], writes=[out])

    def memset(self, out, val, eng="dve"):
        return self.op(eng, lambda e: e.memset(out.ap, val), reads=[], writes=[out])

    def reduce(self, out, in_, op, axis=None, eng="dve"):
        ax = axis if axis is not None else AX.X
        return self.op(eng, lambda e: e.tensor_reduce(out.ap, in_.ap, ax, op), reads=[in_], writes=[out])

    def recip(self, out, in_):
        return self.op("dve", lambda e: e.reciprocal(out.ap, in_.ap), reads=[in_], writes=[out])

    def dma(self, out, in_, key, eng="sp", **kw):
        return self.op(eng, lambda e: e.dma_start(out=out.ap, in_=in_.ap, **kw), reads=[in_], writes=[out], dma_key=key)

    def emit(self, final_wait_keys=()):
        nc = self.nc
        from contextlib import ExitStack
        es = ExitStack()
        eng_sems = {}
        for e in ENGS:
            n = 0
            for o in self.ops[e]:
                if o.dma_key is None and o.signal:
                    n += 1
                    o.ms = n
            nwin = (n + WINDOW - 1) // WINDOW
            eng_sems[e] = [es.enter_context(nc.semaphore(f"s_{e}_{i}")) for i in range(max(nwin, 1))]
        key_sems = {k: es.enter_context(nc.semaphore(f"k_{k}")) for k in self.dma_keys}
        self.nsem = sum(len(v) for v in eng_sems.values()) + len(key_sems)

        def dep_sem(d):
            if d.dma_key is not None:
                return key_sems[d.dma_key], 16 * d.dma_cnt
            w = (d.ms - 1) // WINDOW
            return eng_sems[d.eng][w], d.ms - w * WINDOW

        def run(e, engobj):
            waited = {}
            for o in self.ops[e]:
                need = {}
                for d in o.deps:
                    s, v = dep_sem(d)
                    k = id(s)
                    if need.get(k, (None, 0))[1] < v:
                        need[k] = (s, v)
                for k, (s, v) in need.items():
                    if waited.get(k, 0) < v:
                        engobj.wait_ge(s, v)
                        waited[k] = v
                ins = o.fn(engobj)
                if o.dma_key is not None:
                    ins.then_inc(key_sems[o.dma_key], 16)
                elif o.signal:
                    w = (o.ms - 1) // WINDOW
                    ins.then_inc(eng_sems[e][w], 1)
            if e == "sp":
                for k in final_wait_keys:
                    engobj.wait_ge(key_sems[k], 16 * self.dma_keys[k])

        with nc.Block() as block:
            @block.tensor
            def _(pe):
                run("pe", pe)

            @block.scalar
            def _(a):
                run("act", a)

            @block.vector
            def _(v):
                run("dve", v)

            @block.gpsimd
            def _(g):
                run("pool", g)

            @block.sync
            def _(s):
                run("sp", s)
        es.close()

D = 1024
SEQ = 4096
NMEM = 256
DIN = 3080
DFF = 4096
TS = 512
NT = TS // 128
EPS = 1e-6
NRING = 3


class Arena:
    PAGE = 128

    def __init__(self, P, name, words):
        self.t = P.sb(name, [128, words], F32)
        self.words = words
        self.pages = [Buf(f"{name}_pg{i}") for i in range((words + self.PAGE - 1) // self.PAGE)]

    def view(self, off, nwords, dt=F32, pat=None, **kw):
        assert off + nwords <= self.words, (off, nwords, self.words)
        ap = self.t.h[:, off:off + nwords]
        if dt != F32:
            ap = ap.bitcast(dt)
        if pat is not None:
            ap = ap.rearrange(pat, **kw)
        p0 = off // self.PAGE
        p1 = (off + nwords - 1) // self.PAGE
        return T(ap, "av", bufs=tuple(self.pages[p0:p1 + 1]))


class CT:
    def __init__(self, ap, chunk_bufs):
        self.h = ap
        self.cb = chunk_bufs
        allb = []
        for t in chunk_bufs:
            for b_ in t:
                if b_ not in allb:
                    allb.append(b_)
        self.bufs = tuple(allb)

    def __getitem__(self, idx):
        ap = self.h[idx]
        if isinstance(idx, tuple) and len(idx) >= 2:
            i1 = idx[1]
            if isinstance(i1, int):
                return View(ap, self.cb[i1])
            if isinstance(i1, slice):
                rng = range(*i1.indices(len(self.cb)))
                allb = []
                for i in rng:
                    for b_ in self.cb[i]:
                        if b_ not in allb:
                            allb.append(b_)
                return View(ap, tuple(allb))
        return View(ap, self.bufs)


def chunked_sb(P, name, n, w, dt):
    h = P.nc.alloc_sbuf_tensor(name, [128, n, w], dt)
    return CT(h[:, :, :], [(Buf(f"{name}_{i}"),) for i in range(n)])


class Alloc:
    def __init__(self, arena, start=0):
        self.a = arena
        self.o = start

    def f32(self, shape):
        n = int(np.prod(shape))
        n_al = (n + 127) // 128 * 128
        pat, kw = _pat(shape)
        t = self.a.view(self.o, n, F32, pat, **kw)
        self.last_off = self.o
        self.o += n_al
        return t

    def bf16(self, shape):
        n = int(np.prod(shape))
        w = (n + 1) // 2
        w_al = (w + 127) // 128 * 128
        pat, kw = _pat(shape)
        t = self.a.view(self.o, w, BF16, pat, **kw)
        self.last_off = self.o
        self.o += w_al
        return t

    def chunked(self, n, w, dt):
        t = self.bf16([n, w]) if dt == BF16 else self.f32([n, w])
        off = self.last_off
        wpc = w / 2.0 if dt == BF16 else float(w)
        cb = []
        for i in range(n):
            p0 = int((off + i * wpc) // Arena.PAGE)
            p1 = int((off + (i + 1) * wpc - 1e-9) // Arena.PAGE)
            cb.append(tuple(self.a.pages[p0:p1 + 1]))
        return CT(t.h, cb)


def _pat(shape):
    if len(shape) == 1:
        return None, {}
    if len(shape) == 2:
        return "p (a b) -> p a b", {"a": shape[0]}
    if len(shape) == 3:
        return "p (a b c) -> p a b c", {"a": shape[0], "b": shape[1]}
    raise ValueError(shape)


def build_program(NSEG=SEQ // TS, NL=2, dbg=None):
    nc = bass.Bass("TRN2", target_bir_lowering=False)
    P = Prog(nc)
    L = 2

    class _Stop(Exception):
        pass

    dseg = 0
    if dbg is not None and "@" in dbg:
        dbg, dseg = dbg.split("@")
        dseg = int(dseg)
    st = {"dbg": dbg, "seg": 0}

    def ck(name, l=None):
        if st["seg"] != dseg and name not in ("setup", "kv"):
            return
        if dbg == name or (l is not None and dbg == f"{name}{l}"):
            if "xT" in st:
                xT_, outd_ = st["xT"], st["outd"]
                for c in range(8):
                    P.dma(View(outd_.h[c * 128:(c + 1) * 128, 0:TS], outd_.bufs), xT_[:, c, :], key="out", eng="sp")
            raise _Stop()
    try:
        _build_body(nc, P, L, NSEG, NL, ck, st)
    except _Stop:
        P.emit(final_wait_keys=[k for k in ("out",) if k in P.dma_keys])
    return nc, P


def _build_body(nc, P, L, NSEG, NL, ck, st):
    din = {}

    def di(name, shape):
        din[name] = P.dram(name, shape, F32, "ExternalInput")
        return din[name]
    xd = di("x", [SEQ, D])
    memd = di("mem", [NMEM, D])
    norm_mix = di("norm_mix", [L, D])
    w_in = di("w_in", [L, D, DIN])
    gm_ln_g = di("gm_ln_g", [L, 4, 128])
    gm_ln_b = di("gm_ln_b", [L, 4, 128])
    gm_ws = di("gm_ws", [L, 4, 128, 128])
    gm_bs = di("gm_bs", [L, 4, 128])
    dn_conv = di("dn_conv", [L, 4, 1536])
    dn_a_log = di("dn_a_log", [L, 4])
    dn_dt_bias = di("dn_dt_bias", [L, 4])
    dn_onorm = di("dn_onorm", [L, 128])
    w_out = di("w_out", [L, D, D])
    norm_xa = di("norm_xa", [L, D])
    norm_mem = di("norm_mem", [L, D])
    xa_wq = di("xa_wq", [L, D, D])
    xa_wk = di("xa_wk", [L, D, D])
    xa_wv = di("xa_wv", [L, D, D])
    xa_wo = di("xa_wo", [L, D, D])
    norm_ffn = di("norm_ffn", [L, D])
    ffn_w1 = di("ffn_w1", [L, D, DFF])
    ffn_w2 = di("ffn_w2", [L, DFF, D])
    norm_final = di("norm_final", [D])
    outd = P.dram("out", [SEQ, D], F32, "ExternalOutput")

    banks = [P.ps(f"bank{i}", [128, 512], F32) for i in range(8)]
    bstate = {"i": 0}

    def nb(exclude=()):
        while True:
            b = banks[bstate["i"] % 8]
            bstate["i"] += 1
            if b not in exclude:
                return b

    def b16(bank):
        return View(bank.h[:, :].bitcast(BF16), bank.bufs)

    ident_f = P.sb("ident_f", [128, 128], F32)
    ident_b = P.sb("ident_b", [128, 128], BF16)
    ones_b = P.sb("ones_b", [128, 128], BF16)
    ones_f = P.sb("ones_f", [128, 128], F32)
    U_f = P.sb("U_f", [128, 128], F32)
    bones_f = P.sb("bones_f", [128, 128], F32)
    csel_f = P.sb("csel_f", [128, 2, 128], F32)
    negm_b = P.sb("negm_b", [128, 4, 128], BF16)
    nsu_f = P.sb("nsu_f", [128, 128], F32)
    colsA = P.sb("colsA", [128, 128], F32)
    colsB = P.sb("colsB", [128, 96], F32)
    onorm_bc = P.sb("onorm_bc", [128, L, 128], F32)
    alog_bc = P.sb("alog_bc", [128, L, 4], F32)
    dtb_bc = P.sb("dtb_bc", [128, L, 4], F32)
    negA_bc = P.sb("negA_bc", [128, L, 4], F32)
    Bg = P.sb("Bg", [128, L, 512], F32)
    WsT = P.sb("WsT", [128, L, 512], BF16)
    wtail = P.sb("wtail", [128, L, 64], BF16)
    hist = [P.sb(f"hist{l}", [128, 12, 4], BF16) for l in range(L)]
    KT = [P.sb(f"KT{l}", [128, 8, 256], BF16) for l in range(L)]
    Vm = [P.sb(f"Vm{l}", [128, 2, 1024], BF16) for l in range(L)]
    diag = chunked_sb(P, "diag", 48, 128, BF16)
    Sst = [P.sb(f"S{l}", [128, 4, 128], F32) for l in range(L)]
    Rrs = [P.sb(f"Rr{i}", [128, 4, 128], F32R) for i in range(2)]
    residc = [P.sb(f"resid_r{c}", [128, 4, 128], F32R) for c in range(2)]
    xT = chunked_sb(P, "xT", 8, TS, F32)
    arena2 = Arena(P, "arena2", 3840)
    a2 = Alloc(arena2)
    hT = a2.chunked(8, TS, BF16)
    sq = [a2.bf16([TS]) for _ in range(2)]
    lnv = a2.f32([TS])
    rstd = a2.f32([TS])
    b2 = Alloc(arena2)
    PS1 = {k: b2.f32([4, 128]) for k in ("kdT_f", "qdT_f", "QKT_f", "kdec_f", "v_f")}
    st["xT"] = xT
    st["outd"] = outd
    ring = [P.sb(f"ring{i}", [128, 8, 512], BF16) for i in range(NRING)]
    YR = [P.sb(f"YR{i}", [128, 4, 2, 128], F32R) for i in range(2)]
    XX = [P.sb(f"XX{i}", [128, 4, 128], F32R) for i in range(2)]
    ARW = 21 * 1024
    arena = Arena(P, "arena", ARW)

    NSCR = 28 * NL
    wsc = nc.dram_tensor("wscratch", [NSCR, 128, 4096], BF16, kind="Internal").ap()
    wsc_bufs = [(Buf(f"wsc{i}"),) for i in range(NSCR)]

    class WStream:
        def __init__(self):
            self.blocks = []
            self.issued = 0
            self.taken = 0
            self.seen = set()

        def add(self, src, bid=None):
            first = bid not in self.seen
            if bid is not None:
                self.seen.add(bid)
            self.blocks.append((src, bid, first))

        def _issue(self, n):
            while self.issued < min(n, len(self.blocks)):
                i = self.issued
                s = i % NRING
                slot = ring[s]
                src, bid, first = self.blocks[i]
                flat = slot[:, :, :].rr("p a b -> p (a b)")
                if bid is None or first:
                    for q in range(4):
                        P.dma(slot[:, 2 * q:2 * q + 2, :], src[:, 2 * q:2 * q + 2, :], key=f"ring{s}", eng="pool")
                    if bid is not None and NSEG > 1:
                        P.dma(View(wsc[bid], wsc_bufs[bid]), flat, key=f"wb{s}", eng="sp")
                else:
                    P.dma(flat, View(wsc[bid], wsc_bufs[bid]), key=f"ring{s}", eng="sp")
                self.issued += 1

        def take(self):
            i = self.taken
            self._issue(i + NRING)
            self.taken += 1
            return ring[i % NRING]

    WS = WStream()

    def wblk(Wd, l, r0, c0):
        return View(Wd.h[l, r0:r0 + 1024, c0:c0 + 512].rearrange("(kc p) n -> p kc n", p=128), Wd.bufs)

    for l in range(NL):
        for c in range(2):
            WS.add(wblk(xa_wk, l, 0, 512 * c))
        for c in range(2):
            WS.add(wblk(xa_wv, l, 0, 512 * c))
    for seg in range(NSEG):
        for l in range(NL):
            bid = [28 * l]

            def nxt():
                bid[0] += 1
                return bid[0] - 1
            for c in range(6):
                WS.add(wblk(w_in, l, 0, 512 * c), nxt())
            for c in range(2):
                WS.add(wblk(w_out, l, 0, 512 * c), nxt())
            for c in range(2):
                WS.add(wblk(xa_wq, l, 0, 512 * c), nxt())
            for c in range(2):
                WS.add(wblk(xa_wo, l, 0, 512 * c), nxt())
            for c in range(8):
                WS.add(wblk(ffn_w1, l, 0, 512 * c), nxt())
            for cg in range(2):
                for ks in range(4):
                    WS.add(wblk(ffn_w2, l, 1024 * ks, 512 * cg), nxt())

    def iota_mask(t, fill_keep_cmp, fill, pattern_w=128, nrep=1):
        pass

    P.memset(ident_f[:, :], 0.0, eng="pool")
    P.op("pool", lambda e: e.affine_select(ident_f.h[:, :], ident_f.h[:, :], pattern=[[-1, 128]],
                                           compare_op=ALU.not_equal, fill=1.0, base=0, channel_multiplier=1),
         reads=[ident_f], writes=[ident_f])
    P.copy(ident_b[:, :], ident_f[:, :], eng="pool")

    P.memset(ones_b[:, :], 1.0, eng="pool")
    P.memset(ones_f[:, :], 1.0, eng="pool")
    P.memset(U_f[:, :], 1.0, eng="pool")
    P.op("pool", lambda e: e.affine_select(U_f.h[:, :], U_f.h[:, :], pattern=[[1, 128]],
                                           compare_op=ALU.is_ge, fill=0.0, base=0, channel_multiplier=-1),
         reads=[U_f], writes=[U_f])
    P.memset(negm_b[:, :, :], 0.0, eng="pool")
    P.op("pool", lambda e: e.affine_select(negm_b.h[:, :, :], negm_b.h[:, :, :], pattern=[[0, 4], [1, 128]],
                                           compare_op=ALU.is_ge, fill=-30000.0, base=0, channel_multiplier=-1),
         reads=[negm_b], writes=[negm_b])
    P.memset(nsu_f[:, :], -1.0, eng="pool")
    P.op("pool", lambda e: e.affine_select(nsu_f.h[:, :], nsu_f.h[:, :], pattern=[[1, 128]],
                                           compare_op=ALU.is_gt, fill=0.0, base=0, channel_multiplier=-1),
         reads=[nsu_f], writes=[nsu_f])

    ck("s1")
    P.memset(U_f[0:64, 64:128], 0.0, eng="pool")
    P.memset(negm_b[0:64, :, 64:128], -30000.0, eng="pool")
    P.memset(nsu_f[0:64, 64:128], 0.0, eng="pool")
    P.memset(bones_f[:, :], 1.0, eng="pool")
    P.memset(bones_f[0:64, 64:128], 0.0, eng="pool")
    P.memset(bones_f[64:128, 0:64], 0.0, eng="pool")
    P.memset(csel_f[:, :, :], 1.0, eng="pool")
    P.memset(csel_f[64:128, 0, :], 0.0, eng="pool")
    P.memset(csel_f[0:64, 1, :], 0.0, eng="pool")
    for c in range(2):
        for h in range(4):
            P.ts(residc[c][:, h, :], nsu_f[:, :], 0.0, None, ALU.mult)
    pa = Alloc(arena)
    stageA = pa.f32([128])
    stageB = pa.f32([128])
    ws_nat = pa.f32([L * 4, 128])
    gmem_bc = pa.f32([L, D])
    memst = pa.f32([2, D])
    mn_b = pa.bf16([2, D])
    mnT = pa.bf16([8, 256])
    junk = pa.f32([D])
    sm = pa.f32([64])
    bs_bc = pa.f32([L, 512])
    lngb_dummy = None

    P.memset(stageA[:, :], 0.0, eng="dve")
    P.memset(stageB[:, :], 0.0, eng="dve")

    ikey = {"n": 0}

    def ik():
        ikey["n"] += 1
        return f"init{ikey['n']}"

    def rows(dst, r0, n, src_ap):
        P.dma(dst[r0:r0 + n, :], View(src_ap, (Buf("dram"),)), key=ik(), eng="sp")
    for l in range(L):
        rows(stageA, 8 * l, 8, norm_mix.h[l].rearrange("(c p) -> c p", p=128))
        rows(stageA, 16 + 8 * l, 8, norm_xa.h[l].rearrange("(c p) -> c p", p=128))
        rows(stageA, 32 + 8 * l, 8, norm_ffn.h[l].rearrange("(c p) -> c p", p=128))
        rows(stageA, 56 + 4 * l, 4, gm_ln_g.h[l])
        rows(stageA, 64 + 4 * l, 4, gm_ln_b.h[l])
        for j in range(4):
            rows(stageB, 48 * l + 12 * j, 12, dn_conv.h[l, j].rearrange("(c p) -> c p", p=128))
    rows(stageA, 48, 8, norm_final.h.rearrange("(c p) -> c p", p=128))

    ck("s2")

    def bcast_load(dst_view, src_ap):
        P.dma(dst_view, View(src_ap.partition_broadcast(128), (Buf("dram"),)), key=ik(), eng="sp")
    for l in range(L):
        bcast_load(onorm_bc[:, l, :], dn_onorm.h[l])
        bcast_load(bs_bc[:, l, :], gm_bs.h[l].rearrange("g t -> (g t)"))
        bcast_load(alog_bc[:, l, :], dn_a_log.h[l])
        bcast_load(dtb_bc[:, l, :], dn_dt_bias.h[l])
        bcast_load(gmem_bc[:, l, :], norm_mem.h[l])
        P.dma(ws_nat[:, 4 * l:4 * l + 4, :], View(gm_ws.h[l].rearrange("g t s -> t g s"), (Buf("dram"),)), key=ik(), eng="sp")
        P.dma(wtail[:, l, :].rr("p (kc n) -> p kc n", kc=8),
              View(w_in.h[l, :, 3072:3080].rearrange("(kc p) n -> p kc n", p=128), (Buf("dram"),)), key=ik(), eng="pool")
    P.dma(memst[:, :, :], View(memd.h.rearrange("(mt p) d -> p mt d", p=128), (Buf("dram"),)), key=ik(), eng="sp")

    ck("s3")
    bk = nb()
    P.tr(bk[:, 0:128], stageA[:, :], ident_f[:, :])
    P.tr(bk[:, 128:224], stageB[0:96, :], ident_f[0:96, 0:96])
    P.copy(colsA[:, :], bk[:, 0:128], eng="dve")
    P.copy(colsB[:, :], bk[:, 128:224], eng="dve")

    def gcol(kind, l):
        base = {"mix": 0, "xa": 16, "ffn": 32}[kind] + 8 * l
        return colsA[:, base:base + 8]

    ck("s4")
    P.act(negA_bc[:, :, :], alog_bc[:, :, :], AF.Exp)
    P.ts(negA_bc[:, :, :], negA_bc[:, :, :], -1.0, None, ALU.mult)

    ck("s5")
    for l in range(L):
        for g in range(4):
            i = 4 * l + g
            P.op("pool", lambda e, i=i: e.affine_select(ws_nat.h[:, i, :], ws_nat.h[:, i, :], pattern=[[-1, 128]],
                                                        compare_op=ALU.is_ge, fill=0.0, base=0, channel_multiplier=1),
                 reads=[ws_nat], writes=[ws_nat])
        ck("s6")
        bk = nb()
        for g in range(4):
            P.tr(bk[:, g * 128:(g + 1) * 128], ws_nat[:, 4 * l + g, :], ident_f[:, :])
        ck("s6a")
        wsT_f = junk
        P.copy(wsT_f[:, 0:512], bk[:, :], eng="dve")
        ck("s6b")
        P.copy(WsT[:, l, :], wsT_f[:, 0:512], eng="act")
        ck("s7")
        bk2 = nb()
        P.mm(bk2[:, :], ones_f[:, :], wsT_f[:, 0:512], start=True, stop=True)
        for g in range(4):
            P.stt(Bg[:, l, g * 128:(g + 1) * 128], bk2[:, g * 128:(g + 1) * 128], colsA[:, 64 + 4 * l + g:65 + 4 * l + g],
                  bs_bc[:, l, g * 128:(g + 1) * 128], ALU.mult, ALU.add)

    ck("setup")
    for l in range(NL):
        for mt in range(2):
            P.act(junk[:, :], memst[:, mt, :], AF.Square, accum_out=sm[:, mt:mt + 1])
        P.act(sm[:, 2:4], sm[:, 0:2], AF.Ln, bias=EPS, scale=1.0 / D)
        P.act(sm[:, 4:6], sm[:, 2:4], AF.Exp, scale=-0.5)
        for mt in range(2):
            P.stt(mn_b[:, mt, :], memst[:, mt, :], sm[:, 4 + mt:5 + mt], gmem_bc[:, l, :], ALU.mult, ALU.mult)
        for mt in range(2):
            bk = nb()
            for kc in range(8):
                P.tr(b16(bk)[:, kc * 128:(kc + 1) * 128], mn_b[:, mt, kc * 128:(kc + 1) * 128], ident_b[:, :])
            P.copy(mnT[:, :, mt * 128:(mt + 1) * 128], b16(bk)[:, :].rr("p (a b) -> p a b", a=8), eng="dve")
        for c in range(2):
            W = WS.take()
            for cc in range(4):
                bk = nb()
                for kc in range(8):
                    P.mm(bk[:, 0:256], W[:, kc, cc * 128:(cc + 1) * 128], mnT[:, kc, :], start=(kc == 0), stop=(kc == 7))
                P.copy(KT[l][:, 4 * c + cc, :], bk[:, 0:256], eng="act")
        for c in range(2):
            W = WS.take()
            for mt in range(2):
                bk = nb()
                for kc in range(8):
                    P.mm(bk[:, :], mnT[:, kc, mt * 128:(mt + 1) * 128], W[:, kc, :], start=(kc == 0), stop=(kc == 7))
                P.copy(Vm[l][:, mt, 512 * c:512 * (c + 1)], bk[:, :], eng="dve")

    ck("kv")
    for l in range(L):
        P.memset(Sst[l][:, :, :], 0.0, eng="dve")
        P.memset(hist[l][:, :, :], 0.0, eng="dve")

    ma = Alloc(arena)
    uT = ma.chunked(4, TS, BF16)
    gv = [ma.f32([4, 128]) for _ in range(NT)]
    pcT = ma.chunked(12, TS + 4, BF16)
    sg = [ma.bf16([512]) for _ in range(NT)]
    bl = ma.f32([NT, 8])
    yT = ma.chunked(8, TS, BF16)
    tok = ma.f32([16, 16])
    vn_b = ma.bf16([512])
    tmpA = ma.f32([512])
    tmpB = ma.f32([512])
    stat = ma.f32([64])
    statA = [ma.f32([32]) for _ in range(2)]
    statC = ma.f32([16])
    tmpC = ma.f32([512])
    tmpD = ma.f32([512])
    qk_f = ma.f32([2, 512])
    v_f = ma.f32([4, 128])
    qn_b = ma.bf16([4, 128])
    qd_f = ma.f32([4, 128])
    kn_b = ma.bf16([4, 128])
    kd_f = ma.f32([4, 128])
    kdT_f = ma.f32([4, 128])
    kdec_f = ma.f32([4, 128])
    kT_b = ma.bf16([4, 128])
    qnT_b = ma.bf16([4, 128])
    qdT_f = ma.f32([4, 128])
    ET = ma.f32([4, 128])
    GE = ma.f32([4, 128])
    QKT_f = ma.f32([4, 128])
    vnewc = [ma.f32([4, 128]) for _ in range(2)]
    o_f = ma.f32([4, 128])
    gsc = ma.f32([4, 128])
    yb_b = ma.bf16([4, 128])
    assert ma.o <= ARW, ma.o
    PS = [{"kdT_f": kdT_f, "qdT_f": qdT_f, "QKT_f": QKT_f, "kdec_f": kdec_f, "v_f": v_f, "Rr": Rrs[0]},
          dict(PS1, Rr=Rrs[1])]
    xa = Alloc(arena)
    qxT = xa.chunked(8, TS, BF16)
    oxT = xa.chunked(8, TS, BF16)
    expT = [xa.bf16([2, TS]) for _ in range(2)]
    rinv = [xa.f32([TS]) for _ in range(2)]
    fa = Alloc(arena)
    hidT = fa.chunked(32, TS, BF16)
    rl = [fa.bf16([TS]) for _ in range(2)]
    xst = fa.f32([NT, D])
    ost = fa.f32([NT, D])
    outT = ost
    assert fa.o <= ARW, fa.o

    K_BETA, K_NBETA, K_G, K_D, K_NEGD, K_ED, K_KDS, K_CD0, K_TMP, K_TMP2, K_DL, K_CD1 = range(12)

    def rmsnorm_to_hT(gains, final_out=None):
        bk = nb()
        for c in range(8):
            s = sq[c % 2]
            P.act(s[:, :], xT[:, c, :], AF.Square)
            P.mm(bk[:, :], ones_b[:, :], s[:, :], start=(c == 0), stop=(c == 7))
        P.act(lnv[:, :], bk[:, :], AF.Ln, bias=EPS, scale=1.0 / D)
        P.act(rstd[:, :], lnv[:, :], AF.Exp, scale=-0.5)
        for c in range(8):
            dst = hT[:, c, :] if final_out is None else final_out[:, c, :]
            P.stt(dst, xT[:, c, :], gains[:, c:c + 1], rstd[:, :], ALU.mult, ALU.mult)

    def proj_fm(W, rhsT, nk, evac):
        for cc in range(4):
            bk = nb()
            for kc in range(nk):
                P.mm(bk[:, :], W[:, kc, cc * 128:(cc + 1) * 128], rhsT[:, kc, :], start=(kc == 0), stop=(kc == nk - 1))
            evac(cc, bk)

    def proj_tm(W, lhsT_all, evac):
        for tt in range(NT):
            bk = nb()
            for kc in range(8):
                P.mm(bk[:, :], lhsT_all[:, kc, tt * 128:(tt + 1) * 128], W[:, kc, :], start=(kc == 0), stop=(kc == 7))
            evac(tt, bk)

    def resid_add(c, bk):
        P.tt(xT[:, c, :], xT[:, c, :], bk[:, :], ALU.add)

    first_x = {"done": False}

    def load_x(seg):
        P.dma(xst[:, :, :], View(xd.h[seg * TS:(seg + 1) * TS, :].rearrange("(tt p) d -> p tt d", p=128), (Buf("dram"),)),
              key="xin", eng="sp")

    for seg in range(NSEG):
        st["seg"] = seg
        if seg == 0:
            load_x(0)
        for c in range(8):
            bk = nb()
            for tt in range(NT):
                P.tr(bk[:, tt * 128:(tt + 1) * 128], xst[:, tt, c * 128:(c + 1) * 128], ident_f[:, :])
            P.copy(xT[:, c, :], bk[:, :], eng=("act" if c % 2 else "dve"))

        ck("xT")
        for l in range(NL):
            P.memset(vnewc[0][64:128, :, :], 0.0, eng="pool")
            P.memset(vnewc[1][0:64, :, :], 0.0, eng="pool")
            P.copy(pcT[:, :, 0:4], hist[l][:, :, :], eng="dve")
            rmsnorm_to_hT(gcol("mix", l))
            for j in range(4):
                for c in range(12):
                    r = 48 * l + 12 * j + c
                    P.ts(diag[:, 12 * j + c, :], ident_f[:, :], colsB[:, r:r + 1], None, ALU.mult,
                         eng=("dve" if (c % 3) else "pool"))
            ck("norm1")
            W = WS.take()
            proj_fm(W, hT, 8, lambda cc, bk: P.act(uT[:, cc, :], bk[:, :], AF.Gelu_apprx_tanh))
            W = WS.take()
            proj_tm(W, hT, lambda tt, bk: P.act(gv[tt][:, :, :], bk[:, :].rr("p (a b) -> p a b", a=4), AF.Gelu_apprx_tanh))
            for cg in range(3):
                W = WS.take()
                proj_fm(W, hT, 8, lambda cc, bk, cg=cg: P.copy(pcT[:, 4 * cg + cc, 4:4 + TS], bk[:, :],
                                                              eng=("act" if cc % 2 else "dve")))
            P.copy(hist[l][:, :, :], pcT[:, :, TS:TS + 4], eng="dve")
            W = WS.take()
            proj_tm(W, hT, lambda tt, bk: P.act(sg[tt][:, :], bk[:, :], AF.Silu))
            bk = nb()
            for tt in range(NT):
                for kc in range(8):
                    P.mm(bk[:, tt * 8:(tt + 1) * 8], hT[:, kc, tt * 128:(tt + 1) * 128], wtail[:, l, kc * 8:(kc + 1) * 8],
                         start=(kc == 0), stop=(kc == 7))
            P.copy(bl[:, :, :], bk[:, 0:NT * 8].rr("p (a b) -> p a b", a=NT), eng="dve")

            ck("inproj")
            def tk(kind):
                return tok[:, kind, :]

            def tk3(kind):
                return tok[:, kind, :].rr("p (a b) -> p a b", a=NT)
            P.act(tk3(K_TMP), bl[:, :, 0:4], AF.Tanh, scale=0.5)
            P.ts(tk(K_BETA), tk(K_TMP), 0.5, 0.5, ALU.mult, ALU.add)
            P.ts(tk(K_NBETA), tk(K_BETA), -1.0, None, ALU.mult)
            P.tt(tk3(K_TMP), bl[:, :, 4:8], dtb_bc[:, l:l + 1, :].bc([128, NT, 4]), ALU.add)
            P.act(tk(K_TMP2), tk(K_TMP), AF.Exp)
            P.act(tk(K_TMP), tk(K_TMP2), AF.Ln, bias=1.0)
            P.tt(tk3(K_G), tk3(K_TMP), negA_bc[:, l:l + 1, :].bc([128, NT, 4]), ALU.mult)
            bk = nb()
            P.mm(bk[:, 0:16], U_f[:, :], tk(K_G), start=True, stop=True)
            P.mm(bk[:, 16:32], bones_f[:, :], tk(K_G), start=True, stop=True)
            P.mm(bk[:, 32:48], csel_f[:, 0, :], tk(K_G), start=True, stop=True)
            P.mm(bk[:, 48:64], csel_f[:, 1, :], tk(K_G), start=True, stop=True)
            P.copy(tk(K_D), bk[:, 0:16], eng="dve")
            P.copy(tk(K_DL), bk[:, 16:32], eng="dve")
            P.act(tk(K_CD0), bk[:, 32:48], AF.Exp)
            P.act(tk(K_CD1), bk[:, 48:64], AF.Exp)
            P.ts(tk(K_NEGD), tk(K_D), -1.0, None, ALU.mult)
            P.act(tk(K_ED), tk(K_D), AF.Exp)
            P.tt(tk(K_TMP), tk(K_DL), tk(K_D), ALU.subtract)
            P.act(tk(K_KDS), tk(K_TMP), AF.Exp)

            ck("tok")
            S = Sst[l]

            def bc4(v):
                return v.rr("p (a b) -> p a b", b=1).bc([128, 4, 128])

            def f4(t):
                return t[:, :, :].rr("p a b -> p (a b)")

            def gen_A(tt):
                tsl = slice(tt * 128, (tt + 1) * 128)
                g3 = gv[tt]
                stA = statA[tt % 2]
                P.reduce(stA[:, 0:4], g3[:, :, :], ALU.add)
                P.act(tmpA[:, :], g3[:, :, :].rr("p a b -> p (a b)"), AF.Square)
                yield
                P.reduce(stA[:, 4:8], tmpA[:, :].rr("p (a b) -> p a b", a=4), ALU.add)
                P.ts(stA[:, 8:12], stA[:, 0:4], 1.0 / 128, None, ALU.mult)
                P.tt(stA[:, 12:16], stA[:, 8:12], stA[:, 8:12], ALU.mult)
                P.stt(stA[:, 16:20], stA[:, 4:8], 1.0 / 128, stA[:, 12:16], ALU.mult, ALU.subtract)
                yield
                P.act(stA[:, 20:24], stA[:, 16:20], AF.Ln, bias=EPS)
                P.act(stA[:, 24:28], stA[:, 20:24], AF.Exp, scale=-0.5)
                yield
                P.tt(tmpA[:, :].rr("p (a b) -> p a b", a=4), g3[:, :, :],
                     stA[:, 8:12].rr("p (a b) -> p a b", b=1).bc([128, 4, 128]), ALU.subtract)
                P.tt(vn_b[:, :].rr("p (a b) -> p a b", a=4), tmpA[:, :].rr("p (a b) -> p a b", a=4),
                     stA[:, 24:28].rr("p (a b) -> p a b", b=1).bc([128, 4, 128]), ALU.mult)
                yield
                bkA = nb()
                for g in range(4):
                    P.mm(bkA[:, g * 128:(g + 1) * 128], vn_b[:, g * 128:(g + 1) * 128], WsT[:, l, g * 128:(g + 1) * 128],
                         start=True, stop=True)
                yield
                P.tt(tmpA[:, :].rr("p (a b) -> p a b", a=4), bkA[:, :].rr("p (a b) -> p a b", a=4),
                     colsA[:, 56 + 4 * l:60 + 4 * l].rr("p (a b) -> p a b", b=1).bc([128, 4, 128]), ALU.mult)
                P.tt(tmpA[:, :], tmpA[:, :], Bg[:, l, :], ALU.add, eng="pool")
                P.tt(yT[:, 0:4, tsl], tmpA[:, :].rr("p (a b) -> p a b", a=4), uT[:, :, tsl], ALU.mult, eng="pool")

            def gen_pre(tt, ps):
                hs = slice(tt * 4, tt * 4 + 4)
                kdT_f, qdT_f, QKT_f, kdec_f, v_f, Rr = ps["kdT_f"], ps["qdT_f"], ps["QKT_f"], ps["kdec_f"], ps["v_f"], ps["Rr"]
                bq, bkk, bv = nb(), nb(), nb()
                for cg, bkc in enumerate((bq, bkk, bv)):
                    for cc in range(4):
                        c = 4 * cg + cc
                        for j in range(4):
                            c0 = tt * 128 + j + 1
                            P.mm(bkc[:, cc * 128:(cc + 1) * 128], pcT[:, c, c0:c0 + 128], diag[:, 12 * j + c, :],
                                 start=(j == 0), stop=(j == 3))
                    yield
                P.act(qk_f[:, 0, :], bq[:, :], AF.Silu)
                P.act(qk_f[:, 1, :], bkk[:, :], AF.Silu)
                P.act(f4(v_f), bv[:, :], AF.Silu)
                yield
                P.act(tmpB[:, :], qk_f[:, 0, :], AF.Square)
                P.act(tmpC[:, :], qk_f[:, 1, :], AF.Square)
                yield
                P.reduce(stat[:, 32:36], tmpB[:, :].rr("p (a b) -> p a b", a=4), ALU.add)
                P.reduce(stat[:, 36:40], tmpC[:, :].rr("p (a b) -> p a b", a=4), ALU.add)
                yield
                P.act(stat[:, 40:48], stat[:, 32:40], AF.Ln, bias=EPS)
                P.act(stat[:, 48:56], stat[:, 40:48], AF.Exp, scale=-0.5)
                yield
                P.ts(stat[:, 48:52], stat[:, 48:52], 128.0 ** -0.5, None, ALU.mult)
                P.tt(stat[:, 56:60], stat[:, 48:52], tok[:, K_ED, hs], ALU.mult)
                P.tt(stat[:, 60:64], stat[:, 52:56], tok[:, K_ED, hs], ALU.mult)
                P.tt(stat[:, 28:32], stat[:, 52:56], tok[:, K_KDS, hs], ALU.mult)
                yield
                q3 = qk_f[:, 0, :].rr("p (a b) -> p a b", a=4)
                k3 = qk_f[:, 1, :].rr("p (a b) -> p a b", a=4)
                P.tt(kn_b[:, :, :], k3, bc4(stat[:, 52:56]), ALU.mult)
                P.tt(qn_b[:, :, :], q3, bc4(stat[:, 48:52]), ALU.mult)
                P.tt(qd_f[:, :, :], q3, bc4(stat[:, 56:60]), ALU.mult, eng="pool")
                P.tt(kd_f[:, :, :], k3, bc4(stat[:, 60:64]), ALU.mult, eng="pool")
                P.tt(kdec_f[:, :, :], k3, bc4(stat[:, 28:32]), ALU.mult, eng="pool")
                yield
                bkT = nb()
                for h in range(4):
                    P.tr(b16(bkT)[:, h * 128:(h + 1) * 128], kn_b[:, h, :], ident_b[:, :])
                    P.tr(b16(bkT)[:, 512 + h * 128:512 + (h + 1) * 128], qn_b[:, h, :], ident_b[:, :])
                bkD = nb()
                P.mm(bkD[:, :], ident_b[:, :], negm_b[:, :, :].rr("p a b -> p (a b)"), start=True, stop=False)
                for h in range(4):
                    gcolv = tok[:, K_G, tt * 4 + h:tt * 4 + h + 1]
                    P.mm(bkD[:, h * 128:(h + 1) * 128], gcolv.bc([128, 128]), U_f[:, :], start=False, stop=True)
                yield
                P.copy(f4(kT_b), b16(bkT)[:, 0:512], eng="dve")
                P.copy(f4(qnT_b), b16(bkT)[:, 512:1024], eng="dve")
                for h in range(4):
                    P.act(ET[:, h, :], bkD[:, h * 128:(h + 1) * 128], AF.Exp, bias=tok[:, K_NEGD, tt * 4 + h:tt * 4 + h + 1])
                yield
                bkT2 = nb()
                for h in range(4):
                    P.tr(bkT2[:, h * 128:(h + 1) * 128], qd_f[:, h, :], ident_f[:, :])
                bkW = nb()
                for h in range(4):
                    P.tr(bkW[:, h * 128:(h + 1) * 128], kd_f[:, h, :], ident_f[:, :])
                bkG = nb()
                bkKQ = nb()
                for h in range(4):
                    P.mm(bkG[:, h * 128:(h + 1) * 128], kT_b[:, h, :], kT_b[:, h, :], start=True, stop=True)
                for h in range(4):
                    P.mm(bkKQ[:, h * 128:(h + 1) * 128], kT_b[:, h, :], qnT_b[:, h, :], start=True, stop=True)
                yield
                P.copy(f4(qdT_f), bkT2[:, :], eng="act")
                P.copy(f4(kdT_f), bkW[:, :], eng="act")
                P.tt(f4(GE), bkG[:, :], f4(ET), ALU.mult)
                P.tt(f4(QKT_f), bkKQ[:, :], f4(ET), ALU.mult)
                yield
                P.tt(GE[:, :, :], GE[:, :, :], nsu_f[:, :].rr("p (a b) -> p a b", a=1).bc([128, 4, 128]), ALU.mult, eng="pool")
                cur = 0
                P.tt(YR[cur][:, :, 0, :], GE[:, :, :], bc4(tok[:, K_BETA, hs]), ALU.mult)
                P.copy(YR[cur][:, :, 1, :], ident_f[:, :].rr("p (a b) -> p a b", a=1).bc([128, 4, 128]), eng="dve")
                yield
                bkX = nb()
                for h in range(4):
                    P.tr(bkX[:, h * 128:(h + 1) * 128], YR[cur][:, h, 0, :].bitcast(F32), ident_f[:, :])
                yield
                P.copy(f4(XX[cur]), bkX[:, :], eng="act")
                yield
                NLEV = 6
                for k in range(NLEV):
                    nxt = 1 - cur
                    last = (k == NLEV - 1)
                    bA0, bA1 = nb(), nb()
                    bAs = (bA0, bA0, bA1, bA1)
                    if not last:
                        for h in range(4):
                            P.mm(bAs[h][:, (h % 2) * 256:(h % 2) * 256 + 256], XX[cur][:, h, :],
                                 YR[cur][:, h, :, :].rr("p a b -> p (a b)"), start=True, stop=True)
                        bB = nb()
                        for h in range(4):
                            P.mm(bB[:, h * 128:(h + 1) * 128], YR[cur][:, h, 0, :],
                                 XX[cur][:, h, :], start=True, stop=True)
                        yield
                        for hp in range(2):
                            src = (bA0, bA1)[hp][:, :].rr("p (a b c) -> p a b c", a=2, b=2)
                            P.copy(YR[nxt][:, 2 * hp:2 * hp + 2, 0, :], src[:, :, 0, :], eng="act")
                            P.tt(YR[nxt][:, 2 * hp:2 * hp + 2, 1, :], src[:, :, 1, :],
                                 YR[cur][:, 2 * hp:2 * hp + 2, 1, :].bitcast(F32), ALU.add)
                        P.copy(f4(XX[nxt]), bB[:, :], eng="act")
                        cur = nxt
                        yield
                    else:
                        for h in range(4):
                            P.mm(bA0[:, h * 128:(h + 1) * 128], XX[cur][:, h, :],
                                 YR[cur][:, h, 1, :], start=True, stop=True)
                        yield
                        P.tt(Rr[:, :, :], bA0[:, :].rr("p (a b) -> p a b", a=4), YR[cur][:, :, 1, :].bitcast(F32), ALU.add)

            def gen_chain(tt, ps):
                tsl = slice(tt * 128, (tt + 1) * 128)
                hs = slice(tt * 4, tt * 4 + 4)
                kdT_f, qdT_f, QKT_f, kdec_f, v_f, Rr = ps["kdT_f"], ps["qdT_f"], ps["QKT_f"], ps["kdec_f"], ps["v_f"], ps["Rr"]
                P.tt(gsc[:, :, :], sg[tt][:, :].rr("p (a b) -> p a b", a=4), onorm_bc[:, l:l + 1, :].bc([128, 4, 128]),
                     ALU.mult, eng="pool")
                for c in range(2):
                    rs = slice(64 * c, 64 * c + 64)
                    res_c, vn_c = residc[c], vnewc[c]
                    bkV1 = nb()
                    for h in range(4):
                        P.mm(bkV1[:, h * 128:(h + 1) * 128], kdT_f[:, h, :], S[:, h, :], start=True, stop=True)
                    yield
                    P.tt(res_c[rs, :, :].rr("p a b -> p (a b)"), v_f[rs, :, :].rr("p a b -> p (a b)"), bkV1[rs, :], ALU.subtract)
                    yield
                    bkV = nb()
                    for h in range(4):
                        P.mm(bkV[:, h * 128:(h + 1) * 128], Rr[:, h, :], res_c[:, h, :], start=True, stop=True)
                    yield
                    beta_bc = View(tok.h[rs, K_BETA, hs].rearrange("p (a b) -> p a b", b=1).to_broadcast([64, 4, 128]), tok.bufs)
                    P.tt(vn_c[rs, :, :], bkV[rs, :].rr("p (a b) -> p a b", a=4), beta_bc, ALU.mult)
                    yield
                    bkS = nb()
                    for h in range(4):
                        P.mm(bkS[:, h * 128:(h + 1) * 128], kdec_f[:, h, :], vn_c[:, h, :], start=True, stop=True)
                    bkO = nb()
                    for h in range(4):
                        P.mm(bkO[:, h * 128:(h + 1) * 128], qdT_f[:, h, :], S[:, h, :], start=True, stop=False)
                        P.mm(bkO[:, h * 128:(h + 1) * 128], QKT_f[:, h, :], vn_c[:, h, :], start=False, stop=True)
                    yield
                    P.copy(o_f[rs, :, :].rr("p a b -> p (a b)"), bkO[rs, :], eng="act")
                    kcd = K_CD0 if c == 0 else K_CD1
                    for h in range(4):
                        P.stt(S[:, h, :], S[:, h, :], tok[:, kcd, tt * 4 + h:tt * 4 + h + 1], bkS[:, h * 128:(h + 1) * 128],
                              ALU.mult, ALU.add)
                    yield
                P.act(tmpD[:, :], f4(o_f), AF.Square)
                yield
                P.reduce(statC[:, 0:4], tmpD[:, :].rr("p (a b) -> p a b", a=4), ALU.add)
                yield
                P.act(statC[:, 4:8], statC[:, 0:4], AF.Ln, bias=EPS, scale=1.0 / 128)
                P.act(statC[:, 8:12], statC[:, 4:8], AF.Exp, scale=-0.5)
                yield
                on3 = tmpD[:, :].rr("p (a b) -> p a b", a=4)
                P.tt(on3, o_f[:, :, :], bc4(statC[:, 8:12]), ALU.mult)
                P.tt(yb_b[:, :, :], on3, gsc[:, :, :], ALU.mult, eng="pool")
                yield
                bkY = nb()
                for h in range(4):
                    P.tr(b16(bkY)[:, h * 128:(h + 1) * 128], yb_b[:, h, :], ident_b[:, :])
                yield
                P.copy(yT[:, 4:8, tsl], b16(bkY)[:, 0:512].rr("p (a b) -> p a b", a=4), eng="act")

            def run_rr(gens):
                gens = list(gens)
                while gens:
                    for g in list(gens):
                        try:
                            next(g)
                        except StopIteration:
                            gens.remove(g)

            def seq(*gs):
                for g in gs:
                    yield from g
            run_rr([gen_pre(0, PS[0]), seq(gen_A(0), gen_A(1))])
            for tt in range(NT):
                gs = [gen_chain(tt, PS[tt % 2])]
                if tt + 1 < NT:
                    gs.append(gen_pre(tt + 1, PS[(tt + 1) % 2]))
                if tt + 2 < NT:
                    gs.append(gen_A(tt + 2))
                run_rr(gs)

            st["yT"] = yT
            if st["dbg"] == f"ygdn{l}":
                for c in range(8):
                    P.dma(View(outd.h[c * 128:(c + 1) * 128, 0:TS], outd.bufs), yT[:, c, :], key="out", eng="pool")
                del st["xT"]
                ck(f"ygdn{l}")
            ck("gdn", l)
            for c in range(2):
                W = WS.take()
                proj_fm(W, yT, 8, lambda cc, bk, c=c: resid_add(4 * c + cc, bk))

            ck("wout", l)
            rmsnorm_to_hT(gcol("xa", l))
            for c in range(2):
                W = WS.take()
                proj_fm(W, hT, 8, lambda cc, bk, c=c: P.copy(qxT[:, 4 * c + cc, :], bk[:, :], eng=("act" if cc % 2 else "dve")))
            for h in range(4):
                e = expT[h % 2]
                for mc in range(2):
                    bk = nb()
                    for dc in range(2):
                        P.mm(bk[:, :], KT[l][:, 2 * h + dc, mc * 128:(mc + 1) * 128], qxT[:, 2 * h + dc, :],
                             start=(dc == 0), stop=(dc == 1))
                    P.act(e[:, mc, :], bk[:, :], AF.Exp, scale=1.0 / 16)
                bk = nb()
                for mc in range(2):
                    P.mm(bk[:, :], ones_b[:, :], e[:, mc, :], start=(mc == 0), stop=(mc == 1))
                ri = rinv[h % 2]
                P.recip(ri[:, :], bk[:, :])
                for dc in range(2):
                    bk = nb()
                    for mc in range(2):
                        P.mm(bk[:, :], Vm[l][:, mc, (2 * h + dc) * 128:(2 * h + dc + 1) * 128], e[:, mc, :],
                             start=(mc == 0), stop=(mc == 1))
                    P.tt(oxT[:, 2 * h + dc, :], bk[:, :], ri[:, :], ALU.mult)
            for c in range(2):
                W = WS.take()
                proj_fm(W, oxT, 8, lambda cc, bk, c=c: resid_add(4 * c + cc, bk))

            ck("xattn", l)
            rmsnorm_to_hT(gcol("ffn", l))
            for c in range(8):
                W = WS.take()

                def ev(cc, bk, c=c):
                    r = rl[cc % 2]
                    P.act(r[:, :], bk[:, :], AF.Relu)
                    P.tt(hidT[:, 4 * c + cc, :], r[:, :], r[:, :], ALU.mult, eng=("dve" if cc % 2 else "pool"))
                proj_fm(W, hT, 8, ev)
            if l == NL - 1 and seg + 1 < NSEG:
                load_x(seg + 1)
            for cg in range(2):
                accs = [nb() for _ in range(4)]
                for ks in range(4):
                    W = WS.take()
                    for cc in range(4):
                        for kc in range(8):
                            P.mm(accs[cc][:, :], W[:, kc, cc * 128:(cc + 1) * 128], hidT[:, ks * 8 + kc, :],
                                 start=(ks == 0 and kc == 0), stop=(ks == 3 and kc == 7))
                for cc in range(4):
                    resid_add(4 * cg + cc, accs[cc])
            ck("ffn", l)

        fin = hidT_f32 = None
        fo = Alloc(arena)
        finT = fo.chunked(8, TS, F32)
        rmsnorm_to_hT(colsA[:, 48:56], final_out=finT)
        for tt in range(NT):
            for half in range(2):
                bk = nb()
                for j in range(4):
                    c = 4 * half + j
                    P.tr(bk[:, j * 128:(j + 1) * 128], finT[:, c, tt * 128:(tt + 1) * 128], ident_f[:, :])
                P.copy(ost[:, tt, half * 512:(half + 1) * 512], bk[:, :], eng=("act" if half else "dve"))
        P.dma(View(outd.h[seg * TS:(seg + 1) * TS, :].rearrange("(tt p) d -> p tt d", p=128), outd.bufs), ost[:, :, :],
              key="out", eng="sp")

    P.emit(final_wait_keys=["out"])


_CACHE = {}


def kernel(**inputs):
    names = ["x", "mem", "norm_mix", "w_in", "gm_ln_g", "gm_ln_b", "gm_ws", "gm_bs", "dn_conv", "dn_a_log",
             "dn_dt_bias", "dn_onorm", "w_out", "norm_xa", "norm_mem", "xa_wq", "xa_wk", "xa_wv", "xa_wo",
             "norm_ffn", "ffn_w1", "ffn_w2", "norm_final"]
    arrs = {k: np.ascontiguousarray(np.asarray(inputs[k], dtype=np.float32)) for k in names}
    nc, _ = build_program()
    in_maps = []
    for b in range(8):
        m = {k: arrs[k] for k in names if k not in ("x", "mem")}
        m["x"] = np.ascontiguousarray(arrs["x"][b])
        m["mem"] = np.ascontiguousarray(arrs["mem"][b])
        in_maps.append(m)
    res = run_bass_kernel_spmd(nc, in_maps, core_ids=list(range(8)))
    out = np.stack([np.asarray(r["out"], dtype=np.float32) for r in res.results], axis=0)
    return out
```
